# Optimizing a Trainium2 kernel written in Bass

```python
import jax, jax.numpy as jnp
from jax import lax
import numpy as np

D_MODEL = 1024
BATCH = 16
SEQ = 2048
DEPTH = 4

GRID_W = 64
CTX_LEN = 256
ROPE_BASE = 10000.0
EPS = 1e-6
NEG = -1e30
BLOCK = 128

FOURIER_WIDTH = D_MODEL // 4
FOURIER_GROUPS = 4
FOURIER_GROUP_DIM = FOURIER_WIDTH // FOURIER_GROUPS

MLA_HEADS = D_MODEL // 128
MLA_NOPE = 64
MLA_ROPE = 32
MLA_V = 64
MLA_QK_DIM = MLA_NOPE + MLA_ROPE
MLA_Q_RANK = D_MODEL // 4
MLA_KV_RANK = D_MODEL // 8

SWA_Q_HEADS = D_MODEL // 256
SWA_KV_HEADS = 2
SWA_HEAD_DIM = 64
SWA_WINDOW = 128

MIX_WIDTH = FOURIER_WIDTH + MLA_HEADS * MLA_V + SWA_Q_HEADS * SWA_HEAD_DIM
IN_SPLITS = (FOURIER_WIDTH, MLA_Q_RANK, MLA_KV_RANK, MLA_ROPE,
             SWA_Q_HEADS * SWA_HEAD_DIM, SWA_KV_HEADS * SWA_HEAD_DIM, SWA_KV_HEADS * SWA_HEAD_DIM)
IN_WIDTH = sum(IN_SPLITS)
D_FF = 2816
N_MOD = 9

kernel_name = "hybrid_fourier_mla_swa_macaron_dit"


def rms_norm(x, g):
    xf = x.astype(jnp.float32)
    y = xf * lax.rsqrt(jnp.mean(xf * xf, axis=-1, keepdims=True) + EPS)
    return (y * g.astype(jnp.float32)).astype(x.dtype)


def modulation(cvec, w, b):
    m = jax.nn.silu(cvec) @ w + b
    return m.reshape(cvec.shape[0], N_MOD, D_MODEL)


def modulate(x, shift, scale):
    return x * (1.0 + scale[:, None, :]) + shift[:, None, :]


def swiglu(x, w1, w3, w2):
    return (jax.nn.silu(x @ w1) * (x @ w3)) @ w2


def macaron_half(h, m, i, g, w1, w3, w2):
    y = swiglu(modulate(rms_norm(h, g), m[:, 3 * i], m[:, 3 * i + 1]), w1, w3, w2)
    return h + 0.5 * m[:, 3 * i + 2, None, :] * y


def split_cols(u, widths):
    offs = [int(o) for o in np.cumsum(widths)[:-1]]
    return jnp.split(u, offs, axis=-1)


def axial_rope_tables(rows, dim):
    row = jnp.repeat(jnp.arange(rows, dtype=jnp.float32), GRID_W)
    col = jnp.tile(jnp.arange(GRID_W, dtype=jnp.float32), rows)
    axis_dim = dim // 2
    inv = ROPE_BASE ** (-jnp.arange(0, axis_dim, 2, dtype=jnp.float32) / axis_dim)
    ang = jnp.concatenate([row[:, None] * inv, col[:, None] * inv], axis=-1)
    return jnp.cos(ang), jnp.sin(ang)


def apply_rope(x, cos, sin):
    xf = x.astype(jnp.float32).reshape(*x.shape[:-1], x.shape[-1] // 2, 2)
    x1, x2 = xf[..., 0], xf[..., 1]
    cb, sb = cos[None, :, None, :], sin[None, :, None, :]
    y = jnp.stack([x1 * cb - x2 * sb, x1 * sb + x2 * cb], axis=-1).reshape(x.shape)
    return y.astype(x.dtype)


def fourier_mix(u):
    B, n, _ = u.shape
    z = u.astype(jnp.float32).reshape(B, n, FOURIER_GROUPS, FOURIER_GROUP_DIM)
    y = jnp.fft.fft2(z, axes=(1, 3), norm="ortho").real
    return y.reshape(B, n, FOURIER_WIDTH).astype(u.dtype)


def mla_queries(u_cq, g_cq, w_uq, g_mq):
    B, n, _ = u_cq.shape
    q = (rms_norm(u_cq, g_cq) @ w_uq).reshape(B, n, MLA_HEADS, MLA_QK_DIM)
    return rms_norm(q, g_mq)


def mla_keys_values(u_ckv, u_kr, g_ckv, w_ukv, g_mk):
    B, n, _ = u_ckv.shape
    kv = (rms_norm(u_ckv, g_ckv) @ w_ukv).reshape(B, n, MLA_HEADS, MLA_NOPE + MLA_V)
    k_nope, v = kv[..., :MLA_NOPE], kv[..., MLA_NOPE:]
    k_rope = jnp.broadcast_to(u_kr[:, :, None, :], (B, n, MLA_HEADS, MLA_ROPE))
    k = rms_norm(jnp.concatenate([k_nope, k_rope], axis=-1), g_mk)
    return k, v


def rope_tail(t, cos, sin):
    return jnp.concatenate([t[..., :MLA_NOPE], apply_rope(t[..., MLA_NOPE:], cos, sin)], axis=-1)


def context_attention(q, k, v, sink):
    s = jnp.einsum('bqhgd,bkhd->bhgqk', q, k).astype(jnp.float32) * (q.shape[-1] ** -0.5)
    if sink is None:
        p = jax.nn.softmax(s, axis=-1)
    else:
        sl = jnp.broadcast_to(sink.astype(jnp.float32)[None, :, :, None, None], s.shape[:-1] + (1,))
        p = jax.nn.softmax(jnp.concatenate([sl, s], axis=-1), axis=-1)[..., 1:]
    return jnp.einsum('bhgqk,bkhd->bqhgd', p.astype(v.dtype), v)


def dense_joint_attention(q, k, v, k_ctx, v_ctx):
    B, S, H, Dk = q.shape
    nb = S // BLOCK
    k_all = jnp.concatenate([k_ctx, k], axis=1)
    v_all = jnp.concatenate([v_ctx, v], axis=1)
    qb = jnp.moveaxis(q.reshape(B, nb, BLOCK, H, Dk), 1, 0)

    def block(qblk):
        s = jnp.einsum('bqhd,bkhd->bhqk', qblk, k_all).astype(jnp.float32) * (Dk ** -0.5)
        p = jax.nn.softmax(s, axis=-1)
        return jnp.einsum('bhqk,bkhd->bqhd', p.astype(v_all.dtype), v_all)

    o = lax.map(block, qb)
    return jnp.moveaxis(o, 0, 1).reshape(B, S, H, v.shape[-1])


def banded_window_attention(q, k, v, k_ctx, v_ctx, sink):
    B, S, Hq, D = q.shape
    Hkv = k.shape[2]
    G = Hq // Hkv
    nb = S // BLOCK
    C = k_ctx.shape[1]
    qb = q.reshape(B, nb, BLOCK, Hkv, G, D)

    def neighbours(t):
        tb = jnp.pad(t, ((0, 0), (BLOCK, BLOCK), (0, 0), (0, 0))).reshape(B, nb + 2, BLOCK, Hkv, t.shape[-1])
        return jnp.concatenate([tb[:, :-2], tb[:, 1:-1], tb[:, 2:]], axis=2)

    kb, vb = neighbours(k), neighbours(v)
    q_pos = jnp.arange(S).reshape(nb, BLOCK)
    k_pos = (jnp.arange(nb)[:, None] - 1) * BLOCK + jnp.arange(3 * BLOCK)[None, :]
    kp = k_pos[:, None, :]
    valid = (jnp.abs(q_pos[:, :, None] - kp) <= SWA_WINDOW) & (kp >= 0) & (kp < S)
    scale = D ** -0.5
    s_loc = jnp.einsum('bnqhgd,bnkhd->bnhgqk', qb, kb).astype(jnp.float32) * scale
    s_loc = jnp.where(valid[None, :, None, None], s_loc, NEG)
    s_ctx = jnp.einsum('bnqhgd,bchd->bnhgqc', qb, k_ctx).astype(jnp.float32) * scale
    sl = jnp.broadcast_to(sink.reshape(Hkv, G).astype(jnp.float32)[None, None, :, :, None, None],
                          s_ctx.shape[:-1] + (1,))
    p = jax.nn.softmax(jnp.concatenate([sl, s_ctx, s_loc], axis=-1), axis=-1).astype(v.dtype)
    o = (jnp.einsum('bnhgqc,bchd->bnqhgd', p[..., 1:1 + C], v_ctx)
         + jnp.einsum('bnhgqk,bnkhd->bnqhgd', p[..., 1 + C:], vb))
    return o.reshape(B, S, Hq, D)


def token_mixing(n, nc, w_in, g_cq, w_uq, g_ckv, w_ukv, g_mq, g_mk, g_sq, g_sk, sink, w_out,
                 cos_m, sin_m, cos_s, sin_s, with_ctx_out):
    B, S, _ = n.shape
    C = nc.shape[1]
    G = SWA_Q_HEADS // SWA_KV_HEADS
    f_in, u_cq, u_ckv, u_kr, u_sq, u_sk, u_sv = split_cols(n @ w_in, IN_SPLITS)
    fc_in, uc_cq, uc_ckv, uc_kr, uc_sq, uc_sk, uc_sv = split_cols(nc @ w_in, IN_SPLITS)

    cmk, cmv = mla_keys_values(uc_ckv, uc_kr, g_ckv, w_ukv, g_mk)
    csk = rms_norm(uc_sk.reshape(B, C, SWA_KV_HEADS, SWA_HEAD_DIM), g_sk)
    csv = uc_sv.reshape(B, C, SWA_KV_HEADS, SWA_HEAD_DIM)

    mq = rope_tail(mla_queries(u_cq, g_cq, w_uq, g_mq), cos_m, sin_m)
    mk, mv = mla_keys_values(u_ckv, u_kr, g_ckv, w_ukv, g_mk)
    mk = rope_tail(mk, cos_m, sin_m)
    a = dense_joint_attention(mq, mk, mv, cmk, cmv)

    sq = apply_rope(rms_norm(u_sq.reshape(B, S, SWA_Q_HEADS, SWA_HEAD_DIM), g_sq), cos_s, sin_s)
    sk = apply_rope(rms_norm(u_sk.reshape(B, S, SWA_KV_HEADS, SWA_HEAD_DIM), g_sk), cos_s, sin_s)
    sv = u_sv.reshape(B, S, SWA_KV_HEADS, SWA_HEAD_DIM)
    w = banded_window_attention(sq, sk, sv, csk, csv, sink)

    out = jnp.concatenate([fourier_mix(f_in), a.reshape(B, S, -1), w.reshape(B, S, -1)], axis=-1) @ w_out
    if not with_ctx_out:
        return out, None

    cmq = mla_queries(uc_cq, g_cq, w_uq, g_mq)[:, :, :, None, :]
    ac = context_attention(cmq, cmk, cmv, None)
    csq = rms_norm(uc_sq.reshape(B, C, SWA_Q_HEADS, SWA_HEAD_DIM), g_sq).reshape(B, C, SWA_KV_HEADS, G, SWA_HEAD_DIM)
    wc = context_attention(csq, csk, csv, sink.reshape(SWA_KV_HEADS, G))
    out_c = jnp.concatenate([fourier_mix(fc_in), ac.reshape(B, C, -1), wc.reshape(B, C, -1)], axis=-1) @ w_out
    return out, out_c


def setup_inputs(seed: int = 0) -> dict:
    key = jax.random.key(seed)
    ks = jax.random.split(key, 26)
    nrm = jax.random.normal
    L, D = DEPTH, D_MODEL
    f32 = jnp.float32

    def gain(k, shape):
        return 1.0 + 0.05 * nrm(k, shape, f32)

    return {
        "x": nrm(ks[0], (BATCH, SEQ, D), f32),
        "c": nrm(ks[1], (BATCH, D), f32),
        "ctx": nrm(ks[2], (BATCH, CTX_LEN, D), f32),
        "c_ctx": nrm(ks[3], (D,), f32),
        "w_ada": nrm(ks[4], (L, D, N_MOD * D), f32) * (0.5 * D ** -0.5),
        "b_ada": 0.02 * nrm(ks[5], (L, N_MOD * D), f32),
        "g_ffn1": gain(ks[6], (L, D)),
        "w1_ffn1": nrm(ks[7], (L, D, D_FF), f32) * D ** -0.5,
        "w3_ffn1": nrm(ks[8], (L, D, D_FF), f32) * D ** -0.5,
        "w2_ffn1": nrm(ks[9], (L, D_FF, D), f32) * D_FF ** -0.5,
        "g_mix": gain(ks[10], (L, D)),
        "w_in": nrm(ks[11], (L, D, IN_WIDTH), f32) * D ** -0.5,
        "g_cq": gain(ks[12], (L, MLA_Q_RANK)),
        "w_uq": nrm(ks[13], (L, MLA_Q_RANK, MLA_HEADS * MLA_QK_DIM), f32) * MLA_Q_RANK ** -0.5,
        "g_ckv": gain(ks[14], (L, MLA_KV_RANK)),
        "w_ukv": nrm(ks[15], (L, MLA_KV_RANK, MLA_HEADS * (MLA_NOPE + MLA_V)), f32) * MLA_KV_RANK ** -0.5,
        "g_mla_q": gain(ks[16], (L, MLA_QK_DIM)),
        "g_mla_k": gain(ks[17], (L, MLA_QK_DIM)),
        "g_swa_q": gain(ks[18], (L, SWA_HEAD_DIM)),
        "g_swa_k": gain(ks[19], (L, SWA_HEAD_DIM)),
        "sink": 0.5 * nrm(ks[20], (L, SWA_Q_HEADS), f32),
        "w_out": nrm(ks[21], (L, MIX_WIDTH, D), f32) * MIX_WIDTH ** -0.5,
        "g_ffn2": gain(ks[22], (L, D)),
        "w1_ffn2": nrm(ks[23], (L, D, D_FF), f32) * D ** -0.5,
        "w3_ffn2": nrm(ks[24], (L, D, D_FF), f32) * D ** -0.5,
        "w2_ffn2": nrm(ks[25], (L, D_FF, D), f32) * D_FF ** -0.5,
    }


def reference(x, c, ctx, c_ctx, w_ada, b_ada, g_ffn1, w1_ffn1, w3_ffn1, w2_ffn1, g_mix, w_in,
              g_cq, w_uq, g_ckv, w_ukv, g_mla_q, g_mla_k, g_swa_q, g_swa_k, sink, w_out,
              g_ffn2, w1_ffn2, w3_ffn2, w2_ffn2):
    rows = x.shape[1] // GRID_W
    cos_m, sin_m = axial_rope_tables(rows, MLA_ROPE)
    cos_s, sin_s = axial_rope_tables(rows, SWA_HEAD_DIM)
    h, hc = x, ctx
    for l in range(DEPTH):
        last = l == DEPTH - 1
        m = modulation(c, w_ada[l], b_ada[l])
        mc = modulation(c_ctx[None, :], w_ada[l], b_ada[l])
        h = macaron_half(h, m, 0, g_ffn1[l], w1_ffn1[l], w3_ffn1[l], w2_ffn1[l])
        hc = macaron_half(hc, mc, 0, g_ffn1[l], w1_ffn1[l], w3_ffn1[l], w2_ffn1[l])
        n = modulate(rms_norm(h, g_mix[l]), m[:, 3], m[:, 4])
        nc = modulate(rms_norm(hc, g_mix[l]), mc[:, 3], mc[:, 4])
        out, out_c = token_mixing(n, nc, w_in[l], g_cq[l], w_uq[l], g_ckv[l], w_ukv[l],
                                  g_mla_q[l], g_mla_k[l], g_swa_q[l], g_swa_k[l], sink[l], w_out[l],
                                  cos_m, sin_m, cos_s, sin_s, not last)
        h = h + m[:, 5, None, :] * out
        h = macaron_half(h, m, 2, g_ffn2[l], w1_ffn2[l], w3_ffn2[l], w2_ffn2[l])
        if not last:
            hc = hc + mc[:, 5, None, :] * out_c
            hc = macaron_half(hc, mc, 2, g_ffn2[l], w1_ffn2[l], w3_ffn2[l], w2_ffn2[l])
    return h
```

```python
import contextlib
import os
_EXP = os.environ.get("KEXP", "")
import numpy as np
import ml_dtypes
import concourse.bass as bass
import concourse.mybir as mybir
from concourse.bass_utils import run_bass_kernel_spmd

F32 = mybir.dt.float32
BF16 = mybir.dt.bfloat16
ALU = mybir.AluOpType
AF = mybir.ActivationFunctionType

L = 4
D = 1024
NB = 2
CTX = 256
SEQ = 2048
T = CTX + SEQ
DFF = 2816
NFC = DFF // 128
CH = 256
NCH = T // CH
EPS = 1e-6
NEGV = -30000.0

WC_F = 0
WC_CQ = 256
WC_CKV = 512
WC_KRC = 640
WC_KRS = 736
WC_SQA = 832
WC_SQB = 960
WC_SQAS = 1088
WC_SQBS = 1216
WC_SK = 1344
WC_SKS = 1472
WC_SV = 1600
NWC = 1728

V_GF1 = 0
V_GMIX = 8
V_GF2 = 16
V_GCQ = 24
V_GCKV = 26
V_GMK = 27
V_GMQ = 29
V_GSQ = 31
V_GSK = 33
V_SINK = 35
V_BADA = 39
VL = 111


class Tok:
    __slots__ = ("w", "r", "excl")

    def __init__(self, excl=False):
        self.w = None
        self.r = {}
        self.excl = excl


class Prog:
    def __init__(self, nc, stack):
        self.nc = nc
        self.eng = {"pe": nc.tensor, "act": nc.scalar, "dve": nc.vector, "pool": nc.gpsimd, "sp": nc.sync}
        self.sem = {}
        self.total = {}
        for e in self.eng:
            self.sem[e] = stack.enter_context(nc.semaphore("s_" + e))
            self.total[e] = 0
        self.dq = {"sp": [], "pool": []}
        for q in self.dq:
            for i in range(14):
                n = "d_%s%d" % (q, i)
                self.sem[n] = stack.enter_context(nc.semaphore(n))
                self.total[n] = 0
                self.dq[q].append(n)
        self.dqi = {"sp": 0, "pool": 0}
        self.known = {e: {} for e in self.eng}
        self.ninstr = 0

    def wait(self, e, ev):
        s, v = ev
        if self.known[e].get(s, 0) >= v:
            return
        self.eng[e].wait_ge(self.sem[s], v)
        self.known[e][s] = v
        self.ninstr += 1

    def _deps(self, e, reads, writes):
        evs = {}
        for t in reads:
            if t.w is not None:
                s, v = t.w
                if v > evs.get(s, 0):
                    evs[s] = v
            if t.excl:
                for s, v in t.r.items():
                    if s != e and v > evs.get(s, 0):
                        evs[s] = v
        for t in writes:
            if t.w is not None:
                s, v = t.w
                if s != e and v > evs.get(s, 0):
                    evs[s] = v
            for s, v in t.r.items():
                if s != e and v > evs.get(s, 0):
                    evs[s] = v
        if e == "pe":
            evs.pop("pe", None)
        for s, v in evs.items():
            self.wait(e, (s, v))

    def _mark(self, ev, reads, writes):
        s, v = ev
        for t in reads:
            if v > t.r.get(s, 0):
                t.r[s] = v
        for t in writes:
            t.w = ev
            t.r = {}

    def op(self, e, ins_fn, reads=(), writes=()):
        self._deps(e, reads, writes)
        ins = ins_fn(self.eng[e])
        ins.then_inc(self.sem[e], 1)
        self.total[e] += 1
        self.ninstr += 1
        self._mark((e, self.total[e]), reads, writes)

    def dma(self, q, out, in_, reads=(), writes=()):
        sems = self.dq[q]
        s = sems[self.dqi[q] % len(sems)]
        self.dqi[q] += 1
        if self.total[s] > 0:
            self.wait(q, (s, self.total[s]))
        self._deps(q, reads, writes)
        self.eng[q].dma_start(out=out, in_=in_).then_inc(self.sem[s], 16)
        self.total[s] += 16
        self.ninstr += 1
        self._mark((s, self.total[s]), reads, writes)

    def barrier(self):
        for e in self.eng:
            for s, v in self.total.items():
                if v > 0 and s != e:
                    self.wait(e, (s, v))
            if self.total[e] > 0:
                self.wait(e, (e, self.total[e]))


def _rope_tables():
    pos = np.arange(SEQ)
    row = (pos // 64).astype(np.float32)
    col = (pos % 64).astype(np.float32)

    def tab(dim):
        axis_dim = dim // 2
        inv = (10000.0 ** (-np.arange(0, axis_dim, 2, dtype=np.float32) / axis_dim)).astype(np.float32)
        ang = np.concatenate([row[:, None] * inv, col[:, None] * inv], axis=-1).astype(np.float32)
        c = np.cos(ang).astype(np.float32)
        s = np.sin(ang).astype(np.float32)
        C = np.repeat(c, 2, axis=1).T
        S = np.repeat(s, 2, axis=1).T.copy()
        S[0::2] *= -1.0
        return C, S

    Cm, Sm = tab(32)
    Cs, Ss = tab(64)
    ropem = np.zeros((128, 2, SEQ), np.float32)
    ropem[64:96, 0] = Cm
    ropem[64:96, 1] = Sm
    ropes = np.zeros((128, 2, SEQ), np.float32)
    ropes[0:64, 0] = Cs
    ropes[64:128, 0] = Cs
    ropes[0:64, 1] = Ss
    ropes[64:128, 1] = Ss
    return ropem, ropes


def _dft_tables():
    bf = ml_dtypes.bfloat16
    c = np.arange(64)
    ang = 2.0 * np.pi * np.outer(c, c) / 64.0
    Cc = np.cos(ang) / 8.0
    Sc = np.sin(ang) / 8.0
    dftc = np.zeros((128, 256), np.float64)
    for g in range(2):
        dftc[g * 64:(g + 1) * 64, g * 64:(g + 1) * 64] = Cc
        dftc[g * 64:(g + 1) * 64, 128 + g * 64:128 + (g + 1) * 64] = -Sc
    p = np.arange(SEQ)
    kp = np.outer(p, p) % SEQ
    ang = 2.0 * np.pi * kp / SEQ
    sc = 1.0 / np.sqrt(SEQ)
    Cn = (np.cos(ang) * sc).astype(np.float32)
    Sn = (np.sin(ang) * sc).astype(np.float32)
    dftn = np.empty((4, 128, 2, 16, 512), np.float32)
    for i, X in enumerate((Cn, Sn)):
        dftn[:, :, i] = X.reshape(16, 128, 4, 512).transpose(2, 1, 0, 3)
    p = np.arange(CTX)
    ang = 2.0 * np.pi * (np.outer(p, p) % CTX) / CTX
    dftx = np.empty((128, 2, 2, CTX), np.float32)
    for i, X in enumerate((np.cos(ang) / 16.0, np.sin(ang) / 16.0)):
        dftx[:, i] = X.reshape(2, 128, CTX).transpose(1, 0, 2)
    maskb = np.full((128, 6, 4, 128), NEGV, np.float32)
    sl = np.arange(128)[:, None]
    tl = np.arange(128)[None, :]
    for ri, r in enumerate(range(-1, 5)):
        for qq in range(4):
            d = qq - r
            if d == 0:
                maskb[:, ri, qq] = 0.0
            elif d == -1:
                maskb[:, ri, qq] = np.where(tl >= sl, 0.0, NEGV)
            elif d == 1:
                maskb[:, ri, qq] = np.where(tl <= sl, 0.0, NEGV)
    ident = np.eye(128, dtype=np.float32)
    return (dftc.astype(np.float32).astype(bf), dftn.astype(bf), dftx.astype(bf),
            maskb.reshape(128, 6, 512).astype(bf), ident.astype(bf))


def _fm(v, nk):
    return np.ascontiguousarray(np.asarray(v, np.float32).reshape(nk, 128).T)


def _swap_pairs(a, axis=-1):
    a = np.moveaxis(a, axis, -1)
    sh = a.shape
    b = a.reshape(sh[:-1] + (sh[-1] // 2, 2))[..., ::-1].reshape(sh)
    return np.moveaxis(b, -1, axis)


def _prep_shared(inp):
    f32 = np.float32
    w_ada = np.asarray(inp["w_ada"], f32)
    sh = {}
    sh["wada"] = np.ascontiguousarray(w_ada.reshape(L, 8, 128, 9 * D).transpose(0, 2, 1, 3))
    w1 = np.stack([np.asarray(inp["w1_ffn1"], f32), np.asarray(inp["w1_ffn2"], f32)], 1)
    w3 = np.stack([np.asarray(inp["w3_ffn1"], f32), np.asarray(inp["w3_ffn2"], f32)], 1)
    w2 = np.stack([np.asarray(inp["w2_ffn1"], f32), np.asarray(inp["w2_ffn2"], f32)], 1)
    sh["w1"] = np.ascontiguousarray(w1.reshape(L, 2, 8, 128, DFF).transpose(0, 1, 3, 2, 4))
    sh["w3"] = np.ascontiguousarray(w3.reshape(L, 2, 8, 128, DFF).transpose(0, 1, 3, 2, 4))
    sh["w2"] = np.ascontiguousarray(w2.reshape(L, 2, NFC, 128, D).transpose(0, 1, 3, 2, 4))
    w_in = np.asarray(inp["w_in"], f32)
    fcols = w_in[:, :, 0:256]
    cq = w_in[:, :, 256:512]
    ckv = w_in[:, :, 512:640]
    kr = w_in[:, :, 640:672]
    sq = w_in[:, :, 672:928].reshape(L, D, 4, 64)
    sk = w_in[:, :, 928:1056]
    sv = w_in[:, :, 1056:1184]
    sqA = np.concatenate([sq[:, :, 0], sq[:, :, 2]], -1)
    sqB = np.concatenate([sq[:, :, 1], sq[:, :, 3]], -1)
    ext = np.concatenate([
        fcols, cq, ckv,
        ckv[:, :, 64:128], kr,
        ckv[:, :, 64:128], _swap_pairs(kr),
        sqA, sqB, _swap_pairs(sqA), _swap_pairs(sqB),
        sk, _swap_pairs(sk), sv], -1)
    assert ext.shape[-1] == NWC
    sh["win"] = np.ascontiguousarray(ext.reshape(L, 8, 128, NWC).transpose(0, 2, 1, 3))
    w_uq = np.asarray(inp["w_uq"], f32).reshape(L, 2, 128, 8, 96)
    w_uq_s = w_uq.copy()
    w_uq_s[..., 64:96] = _swap_pairs(w_uq[..., 64:96])
    sh["wuq"] = np.ascontiguousarray(np.stack([w_uq, w_uq_s], 4).transpose(0, 2, 1, 3, 4, 5))
    sh["wukv"] = np.ascontiguousarray(np.asarray(inp["w_ukv"], f32).reshape(L, 128, 8, 128))
    w_out = np.asarray(inp["w_out"], f32)
    rows = np.concatenate([np.arange(0, 768), np.arange(768, 832), np.arange(896, 960),
                           np.arange(832, 896), np.arange(960, 1024)])
    sh["wout"] = np.ascontiguousarray(w_out[:, rows].reshape(L, 8, 128, D).transpose(0, 2, 1, 3))
    vecs = np.zeros((128, L, VL), f32)
    for l in range(L):
        vecs[:, l, V_GF1:V_GF1 + 8] = _fm(inp["g_ffn1"][l], 8)
        vecs[:, l, V_GMIX:V_GMIX + 8] = _fm(inp["g_mix"][l], 8)
        vecs[:, l, V_GF2:V_GF2 + 8] = _fm(inp["g_ffn2"][l], 8)
        vecs[:, l, V_GCQ:V_GCQ + 2] = _fm(inp["g_cq"][l], 2)
        vecs[:, l, V_GCKV] = np.asarray(inp["g_ckv"][l], f32)
        for (col, name) in ((V_GMK, "g_mla_k"), (V_GMQ, "g_mla_q")):
            g = np.asarray(inp[name][l], f32)
            vecs[0:96, l, col] = g
            vecs[64:96, l, col + 1] = _swap_pairs(g[64:96])
        for (col, name) in ((V_GSQ, "g_swa_q"), (V_GSK, "g_swa_k")):
            g = np.asarray(inp[name][l], f32)
            vecs[:, l, col] = np.concatenate([g, g])
            vecs[:, l, col + 1] = np.concatenate([_swap_pairs(g), _swap_pairs(g)])
        vecs[:, l, V_SINK:V_SINK + 4] = np.asarray(inp["sink"][l], f32)[None, :]
        vecs[:, l, V_BADA:V_BADA + 72] = _fm(inp["b_ada"][l], 72)
    sh["vecs"] = vecs.reshape(128, L * VL)
    sh["ropem"], sh["ropes"] = _rope_tables()
    sh["dftc"], sh["dftn"], sh["dftx"], sh["maskb"], sh["ident"] = _dft_tables()
    return sh


def _prep_core(inp, core):
    f32 = np.float32
    b0 = core * NB
    x = np.asarray(inp["x"], f32)
    ctx = np.asarray(inp["ctx"], f32)
    hin = np.empty((NB, D, T), f32)
    for i in range(NB):
        hin[i, :, :CTX] = ctx[b0 + i].T
        hin[i, :, CTX:] = x[b0 + i].T
    c = np.asarray(inp["c"], f32)
    cv = np.stack([c[b0], c[b0 + 1], np.asarray(inp["c_ctx"], f32)], 0)
    cT = np.ascontiguousarray(cv.reshape(3, 8, 128).transpose(2, 1, 0))
    return {"hin": hin, "cT": cT}


def build_nc(n_layers=L, stop_after=None, debug=False):
    nc = bass.Bass("TRN2", target_bir_lowering=False)
    okind = "ExternalOutput" if debug else "Internal"

    def din(name, shape, dt=F32):
        return nc.dram_tensor(name, list(shape), dt, kind="ExternalInput").ap()

    def dscr(name, shape, dt):
        return nc.dram_tensor(name, list(shape), dt, kind=okind).ap()

    hin = din("hin", [NB, D, T])
    cT = din("cT", [128, 8, 3])
    wada = din("wada", [n_layers, 128, 8, 9 * D])
    w1d = din("w1", [n_layers, 2, 128, 8, DFF])
    w3d = din("w3", [n_layers, 2, 128, 8, DFF])
    w2d = din("w2", [n_layers, 2, 128, NFC, D])
    wind = din("win", [n_layers, 128, 8, NWC])
    wuqd = din("wuq", [n_layers, 128, 2, 8, 2, 96])
    wukvd = din("wukv", [n_layers, 128, 8, 128])
    woutd = din("wout", [n_layers, 128, 8, D])
    vecsd = din("vecs", [128, L * VL])
    ropemd = din("ropem", [128, 2, SEQ])
    ropesd = din("ropes", [128, 2, SEQ])
    dftcd = din("dftc", [128, 256], BF16)
    dftnd = din("dftn", [4, 128, 2, 16, 512], BF16)
    dftxd = din("dftx", [128, 2, 2, CTX], BF16)
    maskbd = din("maskb", [128, 6, 512], BF16)
    identd = din("ident", [128, 128], BF16)
    outd = nc.dram_tensor("out", [NB, D, SEQ], F32, kind="ExternalOutput").ap()

    hd = dscr("h", [NB, D, T], F32)
    find = dscr("fin", [NB, 256, T], BF16)
    cqnd = dscr("cqn", [NB, 256, T], BF16)
    ckvnd = dscr("ckvn", [NB, 128, T], BF16)
    krrd = dscr("krr", [NB, 32, T], F32)
    krsqd = dscr("krsq", [NB, 32, T], BF16)
    sqd = dscr("sq", [NB, 2, 128, T], BF16)
    skd = dscr("sk", [NB, 128, T], BF16)
    svd = dscr("sv", [NB, T, 128], BF16)
    mixd = dscr("mix", [NB, D, T], BF16)
    nscd = dscr("nsc", [NB, D, T], BF16)

    with contextlib.ExitStack() as top:
        P = Prog(nc, top)

        uid = [0]

        def sb(stack, name, shape, dt):
            uid[0] += 1
            return stack.enter_context(nc.sbuf_tensor("s%d_%s" % (uid[0], name), list(shape), dt))

        def ps(stack, name, shape):
            uid[0] += 1
            n = 1
            for d in shape[1:]:
                n *= d
            assert n <= 512
            if n < 512 and len(shape) == 2:
                shape = [shape[0], 512]
            assert n == 512 or len(shape) == 2, shape
            return stack.enter_context(nc.psum_tensor("p%d_%s" % (uid[0], name), list(shape), F32))

        def PTok():
            return Tok(excl=True)

        vecs = sb(top, "vecs", [128, L * VL], F32)
        mod = sb(top, "mod", [128, L, 3, 72], F32)
        gsv = sb(top, "gsv", [128, L, 3, 24], F32)
        hgv = sb(top, "hgv", [128, L, 3, 16], F32)
        esink = sb(top, "esink", [128, L * 4], F32)
        ones = sb(top, "ones", [128, 128], BF16)
        onesbd = sb(top, "onesbd", [128, 128], BF16)
        ident = sb(top, "ident", [128, 128], BF16)
        t_vecs, t_mod, t_gsv, t_hgv, t_esink, t_const = Tok(), Tok(), Tok(), Tok(), Tok(), Tok()

        P.dma("sp", vecs[:], vecsd, writes=[t_vecs])
        P.dma("sp", ident[:], identd, writes=[t_const])
        P.op("dve", lambda e: e.memset(ones[:], 1.0), writes=[t_const])
        P.op("dve", lambda e: e.memset(onesbd[:], 0.0), writes=[t_const])
        P.op("dve", lambda e: e.memset(onesbd[0:64, 0:64], 1.0), writes=[t_const])
        P.op("dve", lambda e: e.memset(onesbd[64:128, 64:128], 1.0), writes=[t_const])

        def vcol(l, c, n=1):
            return vecs[:, l * VL + c: l * VL + c + n]

        with contextlib.ExitStack() as ph:
            c32 = sb(ph, "c32", [128, 8, 3], F32)
            scb = sb(ph, "scb", [128, 8, 3], BF16)
            wa = [sb(ph, "wa%d" % i, [128, 8, D], BF16) for i in range(3)]
            pm = ps(ph, "pm", [128, 128, 4])
            t_c, t_sc, t_pm = Tok(), Tok(), PTok()
            t_wa = [Tok() for _ in range(3)]
            P.dma("sp", c32[:], cT, writes=[t_c])
            P.op("act", lambda e: e.activation(out=scb[:], in_=c32[:], func=AF.Silu), reads=[t_c], writes=[t_sc])
            gi = 0
            for l in range(n_layers):
                for j in range(9):
                    w = wa[gi % 3]
                    tw = t_wa[gi % 3]
                    gi += 1
                    P.dma("pool", w[:], wada[l, :, :, j * D:(j + 1) * D], writes=[tw])
                    for dk in range(8):
                        for k in range(8):
                            P.op("pe", lambda e, w=w, k=k, dk=dk, j=j: e.matmul(
                                pm[:, j * 8 + dk, 0:3], w[:, k, dk * 128:(dk + 1) * 128], scb[:, k, :],
                                start=(k == 0), stop=(k == 7)), reads=[tw, t_sc], writes=[t_pm])
                for v in range(3):
                    P.op("dve", lambda e, l=l, v=v: e.tensor_tensor(
                        out=mod[:, l, v, :], in0=pm[:, 0:72, v], in1=vcol(l, V_BADA, 72), op=ALU.add),
                        reads=[t_pm, t_vecs], writes=[t_mod])
                for v in range(3):
                    for si, (jj, gc) in enumerate(((1, V_GF1), (4, V_GMIX), (7, V_GF2))):
                        P.op("dve", lambda e, l=l, v=v, si=si, jj=jj, gc=gc: e.scalar_tensor_tensor(
                            out=gsv[:, l, v, si * 8:(si + 1) * 8], in0=mod[:, l, v, jj * 8:(jj + 1) * 8], scalar=1.0,
                            in1=vcol(l, gc, 8), op0=ALU.add, op1=ALU.mult), reads=[t_mod, t_vecs], writes=[t_gsv])
                    for si, jj in enumerate((2, 8)):
                        P.op("dve", lambda e, l=l, v=v, si=si, jj=jj: e.tensor_scalar(
                            out=hgv[:, l, v, si * 8:(si + 1) * 8], in0=mod[:, l, v, jj * 8:(jj + 1) * 8],
                            scalar1=0.5, scalar2=None, op0=ALU.mult), reads=[t_mod], writes=[t_hgv])
            P.op("act", lambda e: e.activation(out=esink[:].rearrange("p (l f) -> p l f", l=L),
                                               in_=vecs[:].rearrange("p (l f) -> p l f", l=L)[:, :, V_SINK:V_SINK + 4],
                                               func=AF.Exp), reads=[t_vecs], writes=[t_esink])
            P.barrier()

        def vec_of(b, c):
            return 2 if c == 0 else b

        class NormBufs:
            def __init__(self, stack, pssq, nhc=2):
                self.nhc = nhc
                self.hc = [sb(stack, "hc%d" % i, [128, 8, CH], F32) for i in range(nhc)]
                self.t_hc = [Tok() for _ in range(nhc)]
                self.sqb = sb(stack, "sqb", [128, 8, CH], BF16)
                self.t_sqb = Tok()
                self.tmp = sb(stack, "ntmp", [128, 8, CH], F32)
                self.t_tmp = Tok()
                self.nb = [sb(stack, "nb%d" % i, [128, 8, CH], BF16) for i in range(2)]
                self.t_nb = [Tok(), Tok()]
                self.rt = sb(stack, "nrt", [128, CH], F32)
                self.rstd = sb(stack, "nrstd", [128, CH], F32)
                self.t_rt, self.t_rstd = Tok(), Tok()
                self.pssq = pssq
                self.t_pssq = PTok()

        def norm_stage(NBf, i, src, l, v, gs_off, sh_j, preloaded=False):
            hc, t_hc = NBf.hc[i % NBf.nhc], NBf.t_hc[i % NBf.nhc]
            nb, t_nb = NBf.nb[i % 2], NBf.t_nb[i % 2]
            if not preloaded:
                P.dma("sp", hc[:], src.rearrange("(k p) t -> p k t", p=128), writes=[t_hc])
            P.op("act", lambda e: e.activation(out=NBf.sqb[:], in_=hc[:], func=AF.Square),
                 reads=[t_hc], writes=[NBf.t_sqb])
            for k in range(8):
                P.op("pe", lambda e, k=k: e.matmul(NBf.pssq[:, 0:CH], ones[:], NBf.sqb[:, k, :], start=(k == 0), stop=(k == 7)),
                     reads=[NBf.t_sqb, t_const], writes=[NBf.t_pssq])
            P.op("act", lambda e: e.activation(out=NBf.rt[:], in_=NBf.pssq[:, 0:CH], func=AF.Ln, bias=EPS, scale=1.0 / D),
                 reads=[NBf.t_pssq], writes=[NBf.t_rt])
            P.op("act", lambda e: e.activation(out=NBf.rstd[:], in_=NBf.rt[:], func=AF.Exp, scale=-0.5),
                 reads=[NBf.t_rt], writes=[NBf.t_rstd])
            gs_b = gsv[:, l, v, gs_off:gs_off + 8].unsqueeze(2).broadcast_to([128, 8, CH])
            sh_b = mod[:, l, v, sh_j * 8:(sh_j + 1) * 8].unsqueeze(2).broadcast_to([128, 8, CH])
            rs_b = NBf.rstd[:].unsqueeze(1).broadcast_to([128, 8, CH])
            P.op("dve", lambda e: e.tensor_tensor(out=NBf.tmp[:], in0=hc[:], in1=gs_b, op=ALU.mult),
                 reads=[t_hc, t_gsv], writes=[NBf.t_tmp])
            P.op("dve", lambda e: e.tensor_tensor(out=NBf.tmp[:], in0=NBf.tmp[:], in1=rs_b, op=ALU.mult),
                 reads=[NBf.t_tmp, NBf.t_rstd], writes=[NBf.t_tmp])
            P.op("dve" if "nbdve" in _EXP else "pool", lambda e: e.tensor_tensor(out=nb[:], in0=NBf.tmp[:], in1=sh_b, op=ALU.add),
                 reads=[NBf.t_tmp, t_mod], writes=[t_nb])
            return hc, t_hc, nb, t_nb

        HF = NFC // 2

        class WSet:
            def __init__(self, stack, tag):
                self.W1 = sb(stack, "W1" + tag, [128, 8, HF * 128], BF16)
                self.W3 = sb(stack, "W3" + tag, [128, 8, HF * 128], BF16)
                self.W2 = sb(stack, "W2" + tag, [128, HF, D], BF16)
                self.t13 = Tok()
                self.t2 = Tok()

        def load_w(ws, l, which, half):
            c0, c1 = half * HF * 128, (half + 1) * HF * 128
            P.dma("pool", ws.W1[:], w1d[l, which, :, :, c0:c1], writes=[ws.t13])
            P.dma("pool", ws.W3[:], w3d[l, which, :, :, c0:c1], writes=[ws.t13])
            P.dma("pool", ws.W2[:], w2d[l, which, :, half * HF:(half + 1) * HF, :], writes=[ws.t2])

        def ffn_pass(l, which, half, ws, first, last, prefetch=None):
            with contextlib.ExitStack() as ph:
                pssq = ps(ph, "pssq", [128, CH])
                ph13 = [ps(ph, "ph13_%d" % i, [128, 2, CH]) for i in range(3)]
                t_ph13 = [PTok() for _ in range(3)]
                py = [ps(ph, "py%d" % i, [128, 2, CH]) for i in range(4)]
                t_py = [PTok() for _ in range(4)]
                NBf = NormBufs(ph, pssq, nhc=3)
                g = sb(ph, "g", [128, HF, CH], BF16)
                t_g = Tok()
                sl = [sb(ph, "sl%d" % i, [128, CH], F32) for i in range(3)]
                t_sl = [Tok() for _ in range(3)]
                if prefetch is not None:
                    prefetch()

                chunks = [(b, c) for b in range(NB) for c in range(NCH) if not (last and which == 1 and c == 0)]
                gs_off = 0 if which == 0 else 16
                sh_j = 0 if which == 0 else 6
                src_t = hin if (first and half == 0) else hd
                state = {}

                def load_chunk(i):
                    b, c = chunks[i]
                    tsl = slice(c * CH, (c + 1) * CH)
                    hc, t_hc = NBf.hc[i % 3], NBf.t_hc[i % 3]
                    P.dma("sp", hc[:], src_t[b, :, tsl].rearrange("(k p) t -> p k t", p=128), writes=[t_hc])
                    if half == 1:
                        nb, t_nb = NBf.nb[i % 2], NBf.t_nb[i % 2]
                        P.dma("sp", nb[:], nscd[b, :, tsl].rearrange("(k p) t -> p k t", p=128), writes=[t_nb])

                def stage_a(i):
                    b, c = chunks[i]
                    tsl = slice(c * CH, (c + 1) * CH)
                    if half == 0:
                        st = norm_stage(NBf, i, None, l, vec_of(b, c), gs_off, sh_j, preloaded=True)
                        hc, t_hc, nb, t_nb = st
                        if "nost" not in _EXP:
                            P.dma("pool", nscd[b, :, tsl].rearrange("(k p) t -> p k t", p=128), nb[:], reads=[t_nb])
                    else:
                        st = (NBf.hc[i % 3], NBf.t_hc[i % 3], NBf.nb[i % 2], NBf.t_nb[i % 2])
                    state[i] = st

                def stage_b(i):
                    hc, t_hc, nb, t_nb = state[i]
                    for fc in range(HF):
                        pp, tp = ph13[fc % 3], t_ph13[fc % 3]
                        for wi, W in enumerate((ws.W1, ws.W3)):
                            for k in range(8):
                                P.op("pe", lambda e, W=W, wi=wi, k=k, fc=fc, pp=pp: e.matmul(
                                    pp[:, wi, :], W[:, k, fc * 128:(fc + 1) * 128], nb[:, k, :],
                                    start=(k == 0), stop=(k == 7)), reads=[ws.t13, t_nb], writes=[tp])
                        s_, ts_ = sl[fc % 3], t_sl[fc % 3]
                        P.op("act", lambda e, pp=pp, s_=s_: e.activation(out=s_[:], in_=pp[:, 0, :], func=AF.Silu),
                             reads=[tp], writes=[ts_])
                        P.op("dve", lambda e, pp=pp, s_=s_, fc=fc: e.tensor_tensor(
                            out=g[:, fc, :], in0=s_[:], in1=pp[:, 1, :], op=ALU.mult),
                            reads=[ts_, tp], writes=[t_g])

                def stage_c(i):
                    b, c = chunks[i]
                    hc, t_hc, nb, t_nb = state.pop(i)
                    v = vec_of(b, c)
                    for dk in range(8):
                        yy = py[dk // 2][:, dk % 2, :]
                        for fc in range(HF):
                            P.op("pe", lambda e, yy=yy, fc=fc, dk=dk: e.matmul(
                                yy, ws.W2[:, fc, dk * 128:(dk + 1) * 128], g[:, fc, :],
                                start=(fc == 0), stop=(fc == HF - 1)), reads=[ws.t2, t_g], writes=[t_py[dk // 2]])
                        P.op("dve", lambda e, yy=yy, dk=dk: e.scalar_tensor_tensor(
                            out=hc[:, dk, :], in0=yy, scalar=hgv[:, l, v, which * 8 + dk: which * 8 + dk + 1],
                            in1=hc[:, dk, :], op0=ALU.mult, op1=ALU.add), reads=[t_py[dk // 2], t_hc, t_hgv], writes=[t_hc])
                    if last and which == 1 and half == 1:
                        dst = outd[b, :, c * CH - CTX:(c + 1) * CH - CTX]
                    else:
                        dst = hd[b, :, c * CH:(c + 1) * CH]
                    if "nost" not in _EXP:
                        P.dma("pool", dst.rearrange("(k p) t -> p k t", p=128), hc[:], reads=[t_hc])

                n = len(chunks)
                load_chunk(0)
                if half == 0 and n > 1:
                    load_chunk(1)
                stage_a(0)
                for i in range(n):
                    if half == 0:
                        if i + 2 < n:
                            load_chunk(i + 2)
                    else:
                        if i + 1 < n:
                            load_chunk(i + 1)
                    if i + 1 < n:
                        stage_a(i + 1)
                    stage_b(i)
                    stage_c(i)
                P.barrier()

        def mixin_phase(l):
            with contextlib.ExitStack() as ph:
                Win = sb(ph, "Win", [128, 8, NWC], BF16)
                t_win = Tok()
                P.dma("pool", Win[:], wind[l], writes=[t_win])
                ropem = sb(ph, "ropem", [128, 2, SEQ], F32)
                ropes = sb(ph, "ropes", [128, 2, SEQ], F32)
                t_rope = Tok()
                P.dma("sp", ropem[:], ropemd, writes=[t_rope])
                P.dma("sp", ropes[:], ropesd, writes=[t_rope])
                pssq = ps(ph, "pssq", [128, CH])
                pj = [ps(ph, "pj%d" % i, [128, 2, CH]) for i in range(5)]
                t_pj = [PTok() for _ in range(5)]
                pq = [ps(ph, "pq%d" % i, [128, 512]) for i in range(2)]
                t_pq = [PTok() for _ in range(2)]
                NBf = NormBufs(ph, pssq)
                cnt = {"pj": 0, "pq": 0, "st": 0}
                stg = [sb(ph, "stg%d" % i, [128, 2, CH], BF16) for i in range(6)]
                t_stg = [Tok() for _ in range(6)]
                stf = [sb(ph, "stf%d" % i, [128, CH], F32) for i in range(2)]
                t_stf = [Tok() for _ in range(2)]
                sqs = [sb(ph, "sqs%d" % i, [128, 2, CH], BF16) for i in range(2)]
                t_sqs = [Tok() for _ in range(2)]
                rts = [sb(ph, "rts%d" % i, [128, CH], F32) for i in range(2)]
                t_rts = [Tok() for _ in range(2)]
                rsd = [sb(ph, "rsd%d" % i, [128, CH], F32) for i in range(2)]
                t_rsd = [Tok() for _ in range(2)]
                t1s = [sb(ph, "t1s%d" % i, [128, CH], F32) for i in range(2)]
                t2s = [sb(ph, "t2s%d" % i, [128, CH], F32) for i in range(2)]
                t_t1 = [Tok() for _ in range(2)]
                t_t2 = [Tok() for _ in range(2)]
                c2 = {"n": 0}

                def proj(nb, t_nb, col0, ncols, slot):
                    tile_, idx, tk = slot
                    for k in range(8):
                        P.op("pe", lambda e, k=k: e.matmul(tile_[0:ncols, idx, :], Win[:, k, col0:col0 + ncols], nb[:, k, :],
                                                           start=(k == 0), stop=(k == 7)), reads=[t_win, t_nb], writes=[tk])

                def next_pj():
                    i = cnt["pj"] % 5
                    cnt["pj"] += 1
                    return pj[i], t_pj[i]

                def next_pq():
                    i = cnt["pq"] % 2
                    cnt["pq"] += 1
                    return pq[i][:, 0:CH], t_pq[i]

                def next_stg():
                    i = cnt["st"] % 6
                    cnt["st"] += 1
                    return stg[i], t_stg[i]

                def small_rstd(src_ap, t_src, nparts, lhs_ones, nfeat, nslots):
                    j = c2["n"] % 2
                    c2["n"] += 1
                    sq_ = sqs[j]
                    P.op("act", lambda e: e.activation(out=sq_[0:nparts, 0:nslots, :], in_=src_ap, func=AF.Square),
                         reads=[t_src], writes=[t_sqs[j]])
                    pqa, tq = next_pq()
                    for s in range(nslots):
                        P.op("pe", lambda e, s=s: e.matmul(pqa[0:nparts, :], lhs_ones[0:nparts, 0:nparts], sq_[0:nparts, s, :],
                                                           start=(s == 0), stop=(s == nslots - 1)),
                             reads=[t_sqs[j], t_const], writes=[tq])
                    P.op("act", lambda e: e.activation(out=rts[j][0:nparts, :], in_=pqa[0:nparts, :], func=AF.Ln,
                                                       bias=EPS, scale=1.0 / nfeat), reads=[tq], writes=[t_rts[j]])
                    P.op("act", lambda e: e.activation(out=rsd[j][0:nparts, :], in_=rts[j][0:nparts, :], func=AF.Exp, scale=-0.5),
                         reads=[t_rts[j]], writes=[t_rsd[j]])
                    return rsd[j], t_rsd[j]

                tmpq = sb(ph, "tmpq", [128, 2, CH], F32)
                t_tmpq = Tok()
                bcs = [(b, c) for b in range(NB) for c in range(NCH)]
                nxt = norm_stage(NBf, 0, hd[bcs[0][0], :, bcs[0][1] * CH:(bcs[0][1] + 1) * CH], l, vec_of(*bcs[0]), 8, 3)
                for i, (b, c) in enumerate(bcs):
                    if True:
                        hc, t_hc, nb, t_nb = nxt
                        tsl = slice(c * CH, (c + 1) * CH)
                        lat = c > 0
                        psl = slice((c - 1) * CH, c * CH)
                        pt, tk = next_pj()
                        proj(nb, t_nb, WC_F, 128, (pt, 0, tk))
                        proj(nb, t_nb, WC_F + 128, 128, (pt, 1, tk))
                        st, ts = next_stg()
                        P.op("act", lambda e, st=st, pt=pt: e.activation(out=st[:], in_=pt[:], func=AF.Copy),
                             reads=[tk], writes=[ts])
                        P.dma("pool", find[b, :, tsl].rearrange("(j p) t -> p j t", p=128), st[:], reads=[ts])
                        pt, tk = next_pj()
                        proj(nb, t_nb, WC_CQ, 128, (pt, 0, tk))
                        proj(nb, t_nb, WC_CQ + 128, 128, (pt, 1, tk))
                        rs, trs = small_rstd(pt[:], tk, 128, ones, 256.0, 2)
                        st, ts = next_stg()
                        gq_b = vcol(l, V_GCQ, 2).unsqueeze(2).broadcast_to([128, 2, CH])
                        rs_b = rs[:].unsqueeze(1).broadcast_to([128, 2, CH])
                        P.op("dve", lambda e, pt=pt, gq_b=gq_b: e.tensor_tensor(out=tmpq[:], in0=pt[:], in1=gq_b, op=ALU.mult),
                             reads=[tk, t_vecs], writes=[t_tmpq])
                        P.op("dve", lambda e, st=st, rs_b=rs_b: e.tensor_tensor(out=st[:], in0=tmpq[:], in1=rs_b, op=ALU.mult),
                             reads=[t_tmpq, trs], writes=[ts])
                        P.dma("pool", cqnd[b, :, tsl].rearrange("(j p) t -> p j t", p=128), st[:], reads=[ts])
                        if i + 1 < len(bcs):
                            b2, c2_ = bcs[i + 1]
                            nxt = norm_stage(NBf, i + 1, hd[b2, :, c2_ * CH:(c2_ + 1) * CH], l, vec_of(b2, c2_), 8, 3)
                        pt, tk = next_pj()
                        proj(nb, t_nb, WC_CKV, 128, (pt, 0, tk))
                        rs, trs = small_rstd(pt[:, 0:1, :], tk, 128, ones, 128.0, 1)
                        st, ts = next_stg()
                        P.op("dve", lambda e, pt=pt, st=st, rs=rs: e.scalar_tensor_tensor(
                            out=st[:, 0, :], in0=pt[:, 0, :], scalar=vcol(l, V_GCKV), in1=rs[:], op0=ALU.mult, op1=ALU.mult),
                            reads=[tk, trs, t_vecs], writes=[ts])
                        P.dma("pool", ckvnd[b, :, tsl], st[:, 0, :], reads=[ts])
                        pt, tk = next_pj()
                        proj(nb, t_nb, WC_KRC, 96, (pt, 0, tk))
                        if lat:
                            proj(nb, t_nb, WC_KRS, 96, (pt, 1, tk))
                        st, ts = next_stg()
                        P.op("act", lambda e, pt=pt, st=st: e.activation(out=st[64:96, 0, :], in_=pt[64:96, 0, :], func=AF.Square),
                             reads=[tk], writes=[ts])
                        P.dma("pool", krsqd[b, :, tsl], st[64:96, 0, :], reads=[ts])
                        jf = cnt["st"] % 2
                        sf, tsf = stf[jf], t_stf[jf]
                        if lat:
                            j = c2["n"] % 2
                            c2["n"] += 1
                            P.op("dve", lambda e, pt=pt, j=j: e.scalar_tensor_tensor(
                                out=t1s[j][64:96, :], in0=pt[64:96, 0, :], scalar=vcol(l, V_GMK)[64:96, :],
                                in1=ropem[64:96, 0, psl], op0=ALU.mult, op1=ALU.mult),
                                reads=[tk, t_vecs, t_rope], writes=[t_t1[j]])
                            P.op("dve", lambda e, pt=pt, j=j: e.scalar_tensor_tensor(
                                out=t2s[j][64:96, :], in0=pt[64:96, 1, :], scalar=vcol(l, V_GMK + 1)[64:96, :],
                                in1=ropem[64:96, 1, psl], op0=ALU.mult, op1=ALU.mult),
                                reads=[tk, t_vecs, t_rope], writes=[t_t2[j]])
                            P.op("pool", lambda e, sf=sf, j=j: e.tensor_tensor(
                                out=sf[64:96, :], in0=t1s[j][64:96, :], in1=t2s[j][64:96, :], op=ALU.add),
                                reads=[t_t1[j], t_t2[j]], writes=[tsf])
                        else:
                            P.op("dve", lambda e, pt=pt, sf=sf: e.tensor_scalar(
                                out=sf[64:96, :], in0=pt[64:96, 0, :], scalar1=vcol(l, V_GMK)[64:96, :], scalar2=None,
                                op0=ALU.mult), reads=[tk, t_vecs], writes=[tsf])
                        P.dma("pool", krrd[b, :, tsl], sf[64:96, :], reads=[tsf])
                        for (col, cols, gcol, dst) in ((WC_SQA, WC_SQAS, V_GSQ, sqd[b, 0, :, tsl]),
                                                       (WC_SQB, WC_SQBS, V_GSQ, sqd[b, 1, :, tsl]),
                                                       (WC_SK, WC_SKS, V_GSK, skd[b, :, tsl])):
                            pt, tk = next_pj()
                            proj(nb, t_nb, col, 128, (pt, 0, tk))
                            if lat:
                                proj(nb, t_nb, cols, 128, (pt, 1, tk))
                            rs, trs = small_rstd(pt[:, 0:1, :], tk, 128, onesbd, 64.0, 1)
                            st, ts = next_stg()
                            if lat:
                                j = c2["n"] % 2
                                c2["n"] += 1
                                P.op("dve", lambda e, pt=pt, j=j, gcol=gcol: e.scalar_tensor_tensor(
                                    out=t1s[j][:], in0=pt[:, 0, :], scalar=vcol(l, gcol),
                                    in1=ropes[:, 0, psl], op0=ALU.mult, op1=ALU.mult),
                                    reads=[tk, t_vecs, t_rope], writes=[t_t1[j]])
                                P.op("dve", lambda e, pt=pt, j=j, gcol=gcol: e.scalar_tensor_tensor(
                                    out=t2s[j][:], in0=pt[:, 1, :], scalar=vcol(l, gcol + 1),
                                    in1=ropes[:, 1, psl], op0=ALU.mult, op1=ALU.mult),
                                    reads=[tk, t_vecs, t_rope], writes=[t_t2[j]])
                                P.op("pool", lambda e, j=j: e.tensor_tensor(
                                    out=t1s[j][:], in0=t1s[j][:], in1=t2s[j][:], op=ALU.add),
                                    reads=[t_t1[j], t_t2[j]], writes=[t_t1[j]])
                                P.op("dve", lambda e, st=st, j=j, rs=rs: e.tensor_tensor(
                                    out=st[:, 0, :], in0=t1s[j][:], in1=rs[:], op=ALU.mult),
                                    reads=[t_t1[j], trs], writes=[ts])
                            else:
                                P.op("dve", lambda e, pt=pt, st=st, rs=rs, gcol=gcol: e.scalar_tensor_tensor(
                                    out=st[:, 0, :], in0=pt[:, 0, :], scalar=vcol(l, gcol), in1=rs[:],
                                    op0=ALU.mult, op1=ALU.mult), reads=[tk, trs, t_vecs], writes=[ts])
                            P.dma("pool", dst, st[:, 0, :], reads=[ts])
                        pt, tk = next_pj()
                        for tb in range(2):
                            for k in range(8):
                                P.op("pe", lambda e, pt=pt, tb=tb, k=k: e.matmul(
                                    pt[:, 0, tb * 128:(tb + 1) * 128], nb[:, k, tb * 128:(tb + 1) * 128],
                                    Win[:, k, WC_SV:WC_SV + 128], start=(k == 0), stop=(k == 7)),
                                    reads=[t_win, t_nb], writes=[tk])
                        st, ts = next_stg()
                        P.op("act", lambda e, pt=pt, st=st: e.activation(out=st[:, 0, :], in_=pt[:, 0, :], func=AF.Copy),
                             reads=[tk], writes=[ts])
                        P.dma("pool", svd[b, tsl, :].rearrange("(tb p) f -> p tb f", p=128),
                              st[:, 0, :].rearrange("p (tb f) -> p tb f", tb=2), reads=[ts])
                P.barrier()

        def fourier_phase(l, last):
            with contextlib.ExitStack() as ph:
                dftc = sb(ph, "dftc", [128, 256], BF16)
                dftx = sb(ph, "dftx", [128, 2, 2, CTX], BF16)
                t_dc = Tok()
                P.dma("sp", dftc[:], dftcd, writes=[t_dc])
                P.dma("sp", dftx[:], dftxd, writes=[t_dc])
                fin = [sb(ph, "fin%d" % b, [128, 2, T], BF16) for b in range(NB)]
                t_fin = [Tok() for _ in range(NB)]
                A = [sb(ph, "A%d" % b, [128, 18, 2, 256], BF16) for b in range(NB)]
                t_A = [Tok() for _ in range(NB)]
                tab = [sb(ph, "tab%d" % i, [128, 2, 16, 512], BF16) for i in range(2)]
                t_tab = [Tok(), Tok()]
                pa = [ps(ph, "pa%d" % i, [128, 2, 256]) for i in range(3)]
                t_pa = [PTok() for _ in range(3)]
                po = [ps(ph, "po%d" % i, [128, 512]) for i in range(3)]
                t_po = [PTok() for _ in range(3)]
                so = [sb(ph, "so%d" % i, [128, 512], BF16) for i in range(3)]
                t_so = [Tok() for _ in range(3)]
                for b in range(NB):
                    P.dma("sp", fin[b][:], find[b].rearrange("(j p) t -> p j t", p=128), writes=[t_fin[b]])
                n = 0
                for b in range(NB):
                    for pb in range(18):
                        if last and pb < 2:
                            continue
                        pt, tk = pa[n % 3], t_pa[n % 3]
                        for j in range(2):
                            P.op("pe", lambda e, pt=pt, j=j, b=b, pb=pb: e.matmul(
                                pt[:, j, :], fin[b][:, j, pb * 128:(pb + 1) * 128], dftc[:], start=True, stop=True),
                                reads=[t_fin[b], t_dc], writes=[tk])
                        if n % 2 == 0:
                            P.op("act", lambda e, pt=pt, b=b, pb=pb: e.activation(out=A[b][:, pb, :, :], in_=pt[:], func=AF.Copy),
                                 reads=[tk], writes=[t_A[b]])
                        else:
                            P.op("dve", lambda e, pt=pt, b=b, pb=pb: e.tensor_copy(out=A[b][:, pb, :, :], in_=pt[:]),
                                 reads=[tk], writes=[t_A[b]])
                        n += 1
                n = 0
                for kc in range(4):
                    tb_, tt = tab[kc % 2], t_tab[kc % 2]
                    P.dma("sp", tb_[:], dftnd[kc], writes=[tt])
                    for b in range(NB):
                        for j in range(2):
                            pt, tk = po[n % 3], t_po[n % 3]
                            st, ts = so[n % 3], t_so[n % 3]
                            for pb in range(16):
                                for cs in range(2):
                                    P.op("pe", lambda e, pt=pt, b=b, j=j, pb=pb, cs=cs, tb_=tb_: e.matmul(
                                        pt[:], A[b][:, 2 + pb, j, cs * 128:(cs + 1) * 128], tb_[:, cs, pb, :],
                                        start=(pb == 0 and cs == 0), stop=(pb == 15 and cs == 1)),
                                        reads=[t_A[b], tt], writes=[tk])
                            if n % 2 == 0:
                                P.op("act", lambda e, pt=pt, st=st: e.activation(out=st[:], in_=pt[:], func=AF.Copy),
                                     reads=[tk], writes=[ts])
                            else:
                                P.op("dve", lambda e, pt=pt, st=st: e.tensor_copy(out=st[:], in_=pt[:]), reads=[tk], writes=[ts])
                            P.dma("pool", mixd[b, j * 128:(j + 1) * 128, CTX + kc * 512:CTX + (kc + 1) * 512], st[:], reads=[ts])
                            n += 1
                if not last:
                    for b in range(NB):
                        for j in range(2):
                            pt, tk = po[n % 3], t_po[n % 3]
                            st, ts = so[n % 3], t_so[n % 3]
                            for pb in range(2):
                                for cs in range(2):
                                    P.op("pe", lambda e, pt=pt, b=b, j=j, pb=pb, cs=cs: e.matmul(
                                        pt[:, 0:CTX], A[b][:, pb, j, cs * 128:(cs + 1) * 128], dftx[:, cs, pb, :],
                                        start=(pb == 0 and cs == 0), stop=(pb == 1 and cs == 1)),
                                        reads=[t_A[b], t_dc], writes=[tk])
                            P.op("dve", lambda e, pt=pt, st=st: e.tensor_copy(out=st[:, 0:CTX], in_=pt[:, 0:CTX]),
                                 reads=[tk], writes=[ts])
                            P.dma("pool", mixd[b, j * 128:(j + 1) * 128, 0:CTX], st[:, 0:CTX], reads=[ts])
                            n += 1
                P.barrier()

        def normalize_out(pO, t_pO, orow, srow, ncols, extra, dst, den, t_den, so_, t_so_, oc=None, t_oc=None):
            P.op("dve", lambda e: e.tensor_copy(out=den[orow, 0:ncols], in_=pO[srow, 0:ncols]), reads=[t_pO], writes=[t_den])
            if oc is not None:
                P.op("dve", lambda e: e.tensor_copy(out=oc[orow, 0:ncols], in_=pO[orow, 0:ncols]), reads=[t_pO], writes=[t_oc])
            if extra is not None:
                P.op("dve", lambda e: e.tensor_scalar(out=den[orow, 0:ncols], in0=den[orow, 0:ncols], scalar1=extra[orow, :],
                                                      scalar2=None, op0=ALU.add), reads=[t_den, t_esink], writes=[t_den])
            P.op("act", lambda e: e.activation(out=den[orow, 0:ncols], in_=den[orow, 0:ncols], func=AF.Ln), reads=[t_den], writes=[t_den])
            P.op("act", lambda e: e.activation(out=den[orow, 0:ncols], in_=den[orow, 0:ncols], func=AF.Exp, scale=-1.0),
                 reads=[t_den], writes=[t_den])
            if oc is not None:
                P.op("dve", lambda e: e.tensor_tensor(out=so_[orow, 0:ncols], in0=oc[orow, 0:ncols], in1=den[orow, 0:ncols], op=ALU.mult),
                     reads=[t_oc, t_den], writes=[t_so_])
            else:
                P.op("dve", lambda e: e.tensor_tensor(out=so_[orow, 0:ncols], in0=pO[orow, 0:ncols], in1=den[orow, 0:ncols], op=ALU.mult),
                     reads=[t_pO, t_den], writes=[t_so_])
            P.dma("pool", dst, so_[orow, 0:ncols], reads=[t_so_])

        LO = slice(0, 64)
        HI = slice(64, 128)

        def run_pipeline(items, LA=2):
            n = len(items)
            for j in range(n + LA):
                if j < n:
                    items[j][0]()
                if j >= LA:
                    qk, ex, pv, post, pre = items[j - LA]
                    if pre is not None:
                        pre()
                    ex()
                    pv()
                    if post is not None:
                        post()

        def swa_phase(l, last):
            with contextlib.ExitStack() as ph:
                maskb = sb(ph, "maskb", [128, 6, 512], BF16)
                t_mk = Tok()
                P.dma("sp", maskb[:], maskbd, writes=[t_mk])
                pS = [ps(ph, "pS%d" % i, [128, 512]) for i in range(5)]
                t_pS = [PTok() for _ in range(5)]
                pO = [ps(ph, "pO%d" % i, [128, 512]) for i in range(2)]
                t_pO = [PTok() for _ in range(2)]
                PT = [sb(ph, "PT%d" % i, [128, 512], BF16) for i in range(5)]
                t_PT = [Tok() for _ in range(5)]
                den = [sb(ph, "den%d" % i, [128, 512], F32) for i in range(2)]
                t_den = [Tok() for _ in range(2)]
                so = [sb(ph, "so%d" % i, [128, 512], BF16) for i in range(2)]
                t_so = [Tok() for _ in range(2)]
                sq = sb(ph, "sq", [128, 2, T], BF16)
                sk = sb(ph, "sk", [128, T], BF16)
                Vx = sb(ph, "Vx", [128, 18, 2, 128], BF16)
                t_q, t_k, t_v = Tok(), Tok(), Tok()
                P.op("dve", lambda e: e.memset(Vx[:, :, 0, 64:128], 1.0), writes=[t_v])
                P.op("dve", lambda e: e.memset(Vx[:, :, 1, 0:64], 1.0), writes=[t_v])
                ns = 0
                no = 0
                for b in range(NB):
                    for i in range(2):
                        P.dma("sp", sq[:, i, :], sqd[b, i], writes=[t_q])
                    P.dma("sp", sk[:], skd[b], writes=[t_k])
                    svv = svd[b].rearrange("(blk p) f -> p blk f", p=128)
                    P.dma("sp", Vx[:, :, 0, 0:64], svv[:, :, 0:64], writes=[t_v])
                    P.dma("sp", Vx[:, :, 1, 64:128], svv[:, :, 64:128], writes=[t_v])
                    items = []
                    for g in range(2):
                        half = LO if g == 0 else HI
                        oth = HI if g == 0 else LO
                        for i in range(2):
                            head = 2 * g + i
                            esk = esink[:, l * 4 + head: l * 4 + head + 1]
                            groups = [] if last else [("ctx", 0)]
                            groups += [("lat", qg) for qg in range(4)]
                            for kind, qg in groups:
                                if kind == "ctx":
                                    ncols = CTX
                                    qsl = slice(0, CTX)
                                    keys = [(0, None), (1, None)]
                                else:
                                    ncols = 512
                                    qsl = slice(CTX + qg * 512, CTX + (qg + 1) * 512)
                                    keys = [(0, None), (1, None)]
                                    for r in range(-1, 5):
                                        kb = qg * 4 + r
                                        if 0 <= kb < 16:
                                            keys.append((2 + kb, r + 1))
                                k2 = no % 2
                                no += 1
                                o_, to_ = pO[k2], t_pO[k2]
                                dst = mixd[b, 768 + i * 128 + g * 64: 768 + i * 128 + (g + 1) * 64, qsl]
                                nk = len(keys)
                                for ki, (sblk, mi) in enumerate(keys):
                                    s_, ts_ = pS[ns % 5], t_pS[ns % 5]
                                    p_, tp_ = PT[ns % 5], t_PT[ns % 5]
                                    ns += 1

                                    def qk(s_=s_, ts_=ts_, sblk=sblk, qsl=qsl, i=i, half=half, mi=mi, ncols=ncols):
                                        P.op("pe", lambda e: e.matmul(
                                            s_[:, 0:ncols], sk[half, sblk * 128:(sblk + 1) * 128], sq[half, i, qsl],
                                            start=True, stop=(mi is None)), reads=[t_k, t_q], writes=[ts_])
                                        if mi is not None:
                                            P.op("pe", lambda e: e.matmul(
                                                s_[:], ident[:], maskb[:, mi, :], start=False, stop=True),
                                                reads=[t_const, t_mk], writes=[ts_])

                                    def ex(s_=s_, ts_=ts_, p_=p_, tp_=tp_, ncols=ncols):
                                        P.op("act", lambda e: e.activation(
                                            out=p_[:, 0:ncols], in_=s_[:, 0:ncols], func=AF.Exp, scale=0.125),
                                            reads=[ts_], writes=[tp_])

                                    def pv(o_=o_, to_=to_, p_=p_, tp_=tp_, sblk=sblk, g=g, ki=ki, nk=nk, ncols=ncols):
                                        P.op("pe", lambda e: e.matmul(
                                            o_[:, 0:ncols], Vx[:, sblk, g, :], p_[:, 0:ncols],
                                            start=(ki == 0), stop=(ki == nk - 1)), reads=[t_v, tp_], writes=[to_])

                                    post = None
                                    if ki == nk - 1:
                                        def post(o_=o_, to_=to_, half=half, oth=oth, ncols=ncols, esk=esk, dst=dst, k2=k2):
                                            normalize_out(o_, to_, half, oth, ncols, esk, dst, den[k2], t_den[k2],
                                                          so[k2], t_so[k2])
                                    items.append([qk, ex, pv, post, None])
                    run_pipeline(items, LA=4)
                P.barrier()

        def mla_phase(l, last):
            with contextlib.ExitStack() as ph:
                Wuq = sb(ph, "Wuq", [128, 2, 8, 2, 96], BF16)
                Wukv = sb(ph, "Wukv", [128, 8, 128], BF16)
                t_w = Tok()
                P.dma("pool", Wuq[:], wuqd[l], writes=[t_w])
                P.dma("pool", Wukv[:], wukvd[l], writes=[t_w])
                ropem = sb(ph, "ropem", [128, 2, SEQ], F32)
                t_rope = Tok()
                P.dma("sp", ropem[:], ropemd, writes=[t_rope])
                cqn = sb(ph, "cqn", [128, 2, T], BF16)
                ckvn = sb(ph, "ckvn", [128, T], BF16)
                Kpre = sb(ph, "Kpre", [96, T], F32)
                SQK = sb(ph, "SQK", [96, T], BF16)
                t_cqn, t_ckvn, t_kpre_hi, t_sqk_hi = Tok(), Tok(), Tok(), Tok()
                t_kpre_lo, t_sqk_lo = Tok(), Tok()
                Kh = [sb(ph, "Kh%d" % i, [96, T], BF16) for i in range(2)]
                Qh = [sb(ph, "Qh%d" % i, [96, T], BF16) for i in range(2)]
                Vx = [sb(ph, "Vx%d" % i, [128, 18, 128], BF16) for i in range(2)]
                t_Kh = [Tok(), Tok()]
                t_Qh = [Tok(), Tok()]
                t_Vx = [Tok(), Tok()]
                P.op("dve", lambda e: e.memset(Vx[0][:, :, 64:128], 1.0), writes=[t_Vx[0]])
                P.op("dve", lambda e: e.memset(Vx[1][:, :, 0:64], 1.0), writes=[t_Vx[1]])
                pS = [ps(ph, "pS%d" % i, [128, 512]) for i in range(5)]
                t_pS = [PTok() for _ in range(5)]
                pO = [ps(ph, "pO%d" % i, [128, 512]) for i in range(1)]
                t_pO = [PTok() for _ in range(1)]
                pM = [ps(ph, "pM%d" % i, [128, 512]) for i in range(2)]
                t_pM = [PTok() for _ in range(2)]
                PT = [sb(ph, "PT%d" % i, [128, 512], BF16) for i in range(5)]
                t_PT = [Tok() for _ in range(5)]
                den = [sb(ph, "den%d" % i, [128, 512], F32) for i in range(2)]
                t_den = [Tok() for _ in range(2)]
                so = [sb(ph, "so%d" % i, [128, 512], BF16) for i in range(2)]
                t_so = [Tok() for _ in range(2)]
                oc = [sb(ph, "oc%d" % i, [128, 512], F32) for i in range(2)]
                t_oc = [Tok() for _ in range(2)]
                sqt = [sb(ph, "sqt%d" % i, [96, 512], BF16) for i in range(2)]
                t_sqt = [Tok() for _ in range(2)]
                rtt = [sb(ph, "rtt%d" % i, [96, 512], F32) for i in range(2)]
                t_rtt = [Tok() for _ in range(2)]
                rsd = [sb(ph, "rsd%d" % i, [96, 512], F32) for i in range(2)]
                t_rsd = [Tok() for _ in range(2)]
                t1s = [sb(ph, "t1s%d" % i, [96, 512], F32) for i in range(2)]
                t2s = [sb(ph, "t2s%d" % i, [96, 512], F32) for i in range(2)]
                t_t1 = [Tok() for _ in range(2)]
                t_t2 = [Tok() for _ in range(2)]
                cn = {"m": 0, "r": 0, "s": 0, "o": 0, "k": 0}
                tchunks = [(0, CTX)] + [(CTX + i * 512, 512) for i in range(4)]
                scale = 96.0 ** -0.5

                def next_pm():
                    i = cn["m"] % 2
                    cn["m"] += 1
                    return pM[i], t_pM[i]

                kraw = [sb(ph, "kraw%d" % i, [96, 512], F32) for i in range(2)]
                t_kraw = [Tok() for _ in range(2)]
                lnv = [sb(ph, "lnv%d" % i, [96, 512], F32) for i in range(2)]
                t_lnv = [Tok() for _ in range(2)]

                def prep_steps(b, h):
                    hb = h % 2
                    K_, tK = Kh[hb], t_Kh[hb]
                    Q_, tQ = Qh[hb], t_Qh[hb]
                    V_, tV = Vx[hb], t_Vx[hb]
                    gk = vcol(l, V_GMK)
                    gq = vcol(l, V_GMQ)
                    gqs = vcol(l, V_GMQ + 1)
                    steps = []

                    def rstd_steps(rhs_fn, t_rhs, n, box):
                        def s3():
                            pm_, tm_ = next_pm()
                            box["pm"] = (pm_, tm_)
                            P.op("pe", lambda e: e.matmul(pm_[0:96, 0:n], ones[0:96, 0:96], rhs_fn(), start=True, stop=True),
                                 reads=list(t_rhs) + [t_const], writes=[tm_])

                        def s4():
                            pm_, tm_ = box["pm"]
                            j = cn["r"] % 2
                            cn["r"] += 1
                            box["j"] = j
                            P.op("act", lambda e: e.activation(out=lnv[j][:, 0:n], in_=pm_[0:96, 0:n], func=AF.Ln,
                                                               bias=EPS, scale=1.0 / 96.0), reads=[tm_], writes=[t_lnv[j]])

                        def s5():
                            j = box["j"]
                            P.op("act", lambda e: e.activation(out=rsd[j][:, 0:n], in_=lnv[j][:, 0:n], func=AF.Exp, scale=-0.5),
                                 reads=[t_lnv[j]], writes=[t_rsd[j]])
                            box["rs"] = (rsd[j], t_rsd[j])
                        return [s3, s4, s5]

                    for (t0, n) in tchunks:
                        cs = slice(t0, t0 + n)
                        box = {}

                        def k1(box=box, cs=cs, n=n):
                            pm_, tm_ = next_pm()
                            box["a"] = (pm_, tm_)
                            P.op("pe", lambda e: e.matmul(pm_[0:64, 0:n], Wukv[:, h, 0:64], ckvn[:, cs], start=True, stop=True),
                                 reads=[t_w, t_ckvn], writes=[tm_])

                        def k2(box=box, cs=cs, n=n):
                            pm_, tm_ = box["a"]
                            jr = cn["k"] % 2
                            cn["k"] += 1
                            kr_, tkr_ = kraw[jr], t_kraw[jr]
                            P.op("dve", lambda e: e.tensor_copy(out=kr_[0:64, 0:n], in_=pm_[0:64, 0:n]), reads=[tm_], writes=[tkr_])
                            P.op("pool", lambda e: e.tensor_tensor(out=SQK[0:64, cs], in0=kr_[0:64, 0:n], in1=kr_[0:64, 0:n], op=ALU.mult),
                                 reads=[tkr_], writes=[t_sqk_lo])
                            P.op("dve", lambda e: e.tensor_scalar(out=Kpre[0:64, cs], in0=kr_[0:64, 0:n], scalar1=gk[0:64, :],
                                                                  scalar2=None, op0=ALU.mult), reads=[tkr_, t_vecs], writes=[t_kpre_lo])

                        def k6(box=box, cs=cs, n=n):
                            rs, trs = box["rs"]
                            P.op("dve", lambda e: e.tensor_tensor(out=K_[:, cs], in0=Kpre[:, cs], in1=rs[:, 0:n], op=ALU.mult),
                                 reads=[t_kpre_lo, t_kpre_hi, trs], writes=[tK])
                        steps += [k1, k2] + rstd_steps(lambda cs=cs: SQK[0:96, cs], [t_sqk_lo, t_sqk_hi], n, box) + [k6]
                    vcols = slice(0, 64) if hb == 0 else slice(64, 128)
                    for (s0, s1) in ((0, 8), (8, 16), (16, 18)):
                        box = {}

                        def v1(box=box, s0=s0, s1=s1):
                            pm_, tm_ = next_pm()
                            box["a"] = (pm_, tm_)
                            for sblk in range(s0, s1):
                                P.op("pe", lambda e, sblk=sblk: e.matmul(
                                    pm_[:, (sblk - s0) * 64:(sblk - s0 + 1) * 64], ckvn[:, sblk * 128:(sblk + 1) * 128],
                                    Wukv[:, h, 64:128], start=True, stop=True), reads=[t_w, t_ckvn], writes=[tm_])

                        def v2(box=box, s0=s0, s1=s1):
                            pm_, tm_ = box["a"]
                            ns_ = s1 - s0
                            P.op("dve", lambda e: e.tensor_copy(
                                out=V_[:, s0:s1, vcols], in_=pm_[:, 0:ns_ * 64].rearrange("p (s f) -> p s f", f=64)),
                                reads=[tm_], writes=[tV])
                        steps += [v1, v2]
                    for (t0, n) in tchunks:
                        if last and t0 == 0:
                            continue
                        cs = slice(t0, t0 + n)
                        lat = t0 >= CTX
                        psl = slice(t0 - CTX, t0 - CTX + n)
                        box = {}

                        def q1(box=box, cs=cs, n=n, lat=lat):
                            pq_, tq_ = next_pm()
                            box["a"] = (pq_, tq_)
                            for j in range(2):
                                P.op("pe", lambda e, j=j: e.matmul(
                                    pq_[0:96, 0:n], Wuq[:, j, h, 0, :], cqn[:, j, cs], start=(j == 0), stop=(j == 1)),
                                    reads=[t_w, t_cqn], writes=[tq_])
                            if lat:
                                px_, tx_ = next_pm()
                                box["c"] = (px_, tx_)
                                for j in range(2):
                                    P.op("pe", lambda e, j=j: e.matmul(
                                        px_[0:96, 0:n], Wuq[:, j, h, 1, :], cqn[:, j, cs], start=(j == 0), stop=(j == 1)),
                                        reads=[t_w, t_cqn], writes=[tx_])

                        def q2(box=box, n=n, lat=lat, psl=psl):
                            pq_, tq_ = box["a"]
                            jr = cn["k"] % 2
                            cn["k"] += 1
                            box["jr"] = jr
                            qr_, tqr_ = kraw[jr], t_kraw[jr]
                            P.op("dve", lambda e: e.tensor_copy(out=qr_[0:96, 0:n], in_=pq_[0:96, 0:n]), reads=[tq_], writes=[tqr_])
                            P.op("pool", lambda e: e.tensor_tensor(out=sqt[jr][:, 0:n], in0=qr_[0:96, 0:n], in1=qr_[0:96, 0:n], op=ALU.mult),
                                 reads=[tqr_], writes=[t_sqt[jr]])
                            if lat:
                                px_, tx_ = box["c"]
                                P.op("dve", lambda e: e.scalar_tensor_tensor(
                                    out=t2s[jr][64:96, 0:n], in0=px_[64:96, 0:n], scalar=gqs[64:96, :], in1=ropem[64:96, 1, psl],
                                    op0=ALU.mult, op1=ALU.mult), reads=[tx_, t_vecs, t_rope], writes=[t_t2[jr]])

                        def q6(box=box, cs=cs, n=n, lat=lat, psl=psl):
                            rs, trs = box["rs"]
                            jr = box["jr"]
                            qr_, tqr_ = kraw[jr], t_kraw[jr]
                            if lat:
                                P.op("dve", lambda e: e.scalar_tensor_tensor(
                                    out=Q_[0:64, cs], in0=qr_[0:64, 0:n], scalar=gq[0:64, :], in1=rs[0:64, 0:n],
                                    op0=ALU.mult, op1=ALU.mult), reads=[tqr_, trs, t_vecs], writes=[tQ])
                                P.op("dve", lambda e: e.scalar_tensor_tensor(
                                    out=t1s[jr][64:96, 0:n], in0=qr_[64:96, 0:n], scalar=gq[64:96, :], in1=ropem[64:96, 0, psl],
                                    op0=ALU.mult, op1=ALU.mult), reads=[tqr_, t_vecs, t_rope], writes=[t_t1[jr]])
                                P.op("pool", lambda e: e.tensor_tensor(
                                    out=t1s[jr][64:96, 0:n], in0=t1s[jr][64:96, 0:n], in1=t2s[jr][64:96, 0:n], op=ALU.add),
                                    reads=[t_t1[jr], t_t2[jr]], writes=[t_t1[jr]])
                                P.op("dve", lambda e: e.tensor_tensor(
                                    out=Q_[64:96, cs], in0=t1s[jr][64:96, 0:n], in1=rs[64:96, 0:n], op=ALU.mult),
                                    reads=[t_t1[jr], trs], writes=[tQ])
                            else:
                                P.op("dve", lambda e: e.scalar_tensor_tensor(
                                    out=Q_[:, cs], in0=qr_[0:96, 0:n], scalar=gq[0:96, :], in1=rs[:, 0:n],
                                    op0=ALU.mult, op1=ALU.mult), reads=[tqr_, trs, t_vecs], writes=[tQ])
                        steps += [q1, q2] + rstd_steps(lambda box=box, n=n: sqt[box["jr"]][:, 0:n], [t_sqt[0], t_sqt[1]], n, box) + [q6]
                    return steps

                def head_items(b, h):
                    hb = h % 2
                    K_, tK = Kh[hb], t_Kh[hb]
                    Q_, tQ = Qh[hb], t_Qh[hb]
                    V_, tV = Vx[hb], t_Vx[hb]
                    orow = LO if hb == 0 else HI
                    srow = HI if hb == 0 else LO
                    groups = [] if last else [(0, CTX, 2)]
                    groups += [(CTX + i * 512, 512, 18) for i in range(4)]
                    items = []
                    for (t0, n, nk) in groups:
                        k2 = cn["o"] % 2
                        cn["o"] += 1
                        o_, to_ = pO[0], t_pO[0]
                        dst = mixd[b, 256 + h * 64: 256 + (h + 1) * 64, t0:t0 + n]
                        for sblk in range(nk):
                            i3 = cn["s3"] % 5
                            cn["s3"] += 1
                            s_, ts_ = pS[i3], t_pS[i3]
                            p_, tp_ = PT[i3], t_PT[i3]

                            def qk(s_=s_, ts_=ts_, sblk=sblk, t0=t0, n=n):
                                P.op("pe", lambda e: e.matmul(
                                    s_[:, 0:n], K_[:, sblk * 128:(sblk + 1) * 128], Q_[:, t0:t0 + n], start=True, stop=True),
                                    reads=[tK, tQ], writes=[ts_])

                            def ex(s_=s_, ts_=ts_, p_=p_, tp_=tp_, n=n):
                                P.op("act", lambda e: e.activation(out=p_[:, 0:n], in_=s_[:, 0:n], func=AF.Exp, scale=scale),
                                     reads=[ts_], writes=[tp_])

                            def pv(o_=o_, to_=to_, p_=p_, tp_=tp_, sblk=sblk, n=n, nk=nk):
                                P.op("pe", lambda e: e.matmul(
                                    o_[:, 0:n], V_[:, sblk, :], p_[:, 0:n], start=(sblk == 0), stop=(sblk == nk - 1)),
                                    reads=[tV, tp_], writes=[to_])

                            post = None
                            if sblk == nk - 1:
                                def post(o_=o_, to_=to_, n=n, dst=dst, k2=k2):
                                    normalize_out(o_, to_, orow, srow, n, None, dst, den[k2], t_den[k2], so[k2], t_so[k2], oc[k2], t_oc[k2])
                            items.append([qk, ex, pv, post, None])
                    return items

                cn["s3"] = 0
                for b in range(NB):
                    P.dma("sp", cqn[:], cqnd[b].rearrange("(j p) t -> p j t", p=128), writes=[t_cqn])
                    P.dma("sp", ckvn[:], ckvnd[b], writes=[t_ckvn])
                    P.dma("sp", Kpre[64:96, :], krrd[b], writes=[t_kpre_hi])
                    P.dma("sp", SQK[64:96, :], krsqd[b], writes=[t_sqk_hi])
                    for st_ in prep_steps(b, 0):
                        st_()
                    items = []
                    for h in range(8):
                        its = head_items(b, h)
                        if h + 1 < 8:
                            steps = prep_steps(b, h + 1)
                            per = [[] for _ in its]
                            for si, st_ in enumerate(steps):
                                per[min(si, len(its) - 1)].append(st_)
                            for it, sts in zip(its, per):
                                if sts:
                                    it[4] = (lambda sts=sts: [f() for f in sts])
                        items += its
                    run_pipeline(items, LA=4)
                P.barrier()

        def outproj_phase(l, last):
            with contextlib.ExitStack() as ph:
                Wout = sb(ph, "Wout", [128, 8, D], BF16)
                t_w = Tok()
                P.dma("pool", Wout[:], woutd[l], writes=[t_w])
                hc = [sb(ph, "hc%d" % i, [128, 8, CH], F32) for i in range(3)]
                t_hc = [Tok() for _ in range(3)]
                mx = [sb(ph, "mx%d" % i, [128, 8, CH], BF16) for i in range(3)]
                t_mx = [Tok() for _ in range(3)]
                py = [ps(ph, "py%d" % i, [128, 2, CH]) for i in range(8)]
                t_py = [PTok() for _ in range(8)]
                i = 0
                n = 0
                for b in range(NB):
                    for c in range(NCH):
                        if last and c == 0:
                            continue
                        v = vec_of(b, c)
                        tsl = slice(c * CH, (c + 1) * CH)
                        h_, th_ = hc[i % 3], t_hc[i % 3]
                        m_, tm_ = mx[i % 3], t_mx[i % 3]
                        i += 1
                        P.dma("sp", h_[:], hd[b, :, tsl].rearrange("(k p) t -> p k t", p=128), writes=[th_])
                        P.dma("sp", m_[:], mixd[b, :, tsl].rearrange("(k p) t -> p k t", p=128), writes=[tm_])
                        for dk in range(8):
                            yy = py[n % 8][:, 0, :]
                            ty = t_py[n % 8]
                            n += 1
                            for k in range(8):
                                P.op("pe", lambda e, yy=yy, k=k, dk=dk, m_=m_: e.matmul(
                                    yy, Wout[:, k, dk * 128:(dk + 1) * 128], m_[:, k, :], start=(k == 0), stop=(k == 7)),
                                    reads=[t_w, tm_], writes=[ty])
                            P.op("dve", lambda e, yy=yy, dk=dk, h_=h_, v=v: e.scalar_tensor_tensor(
                                out=h_[:, dk, :], in0=yy, scalar=mod[:, l, v, 40 + dk: 41 + dk], in1=h_[:, dk, :],
                                op0=ALU.mult, op1=ALU.add), reads=[ty, th_, t_mod], writes=[th_])
                        P.dma("pool", hd[b, :, tsl].rearrange("(k p) t -> p k t", p=128), h_[:], reads=[th_])
                P.barrier()

        def want(l, k):
            return stop_after is None or (l * 7 + k) <= stop_after

        wstack = contextlib.ExitStack()
        X = Y = None
        if want(0, 0):
            wstack = contextlib.ExitStack()
            X, Y = WSet(wstack, "x0"), WSet(wstack, "y0")
            load_w(X, 0, 0, 0)
            ffn_pass(0, 0, 0, X, True, False, prefetch=lambda: load_w(Y, 0, 0, 1))
            ffn_pass(0, 0, 1, Y, True, False)
            wstack.close()
        for l in range(n_layers):
            last = (l == L - 1)
            if want(l, 1):
                mixin_phase(l)
            if want(l, 2):
                fourier_phase(l, last)
            if want(l, 3):
                swa_phase(l, last)
            if want(l, 4):
                mla_phase(l, last)
            if want(l, 5):
                wstack = contextlib.ExitStack()
                X, Y = WSet(wstack, "x%d" % (l + 1)), WSet(wstack, "y%d" % (l + 1))
                if want(l, 6):
                    load_w(X, l, 1, 0)
                outproj_phase(l, last)
                if want(l, 6):
                    nxt_ffn1 = (l + 1 < n_layers) and want(l + 1, 0)
                    ffn_pass(l, 1, 0, X, False, last, prefetch=lambda: load_w(Y, l, 1, 1))
                    ffn_pass(l, 1, 1, Y, False, last,
                             prefetch=(lambda: load_w(X, l + 1, 0, 0)) if nxt_ffn1 else None)
                    if nxt_ffn1:
                        ffn_pass(l + 1, 0, 0, X, False, False, prefetch=lambda: load_w(Y, l + 1, 0, 1))
                        ffn_pass(l + 1, 0, 1, Y, False, False)
                wstack.close()
        P.barrier()
        nc._n_emitted = P.ninstr
    return nc


_CACHE = {}


def kernel(**inputs):
    n_cores = 8
    shared = _prep_shared(inputs)
    in_maps = []
    for core in range(n_cores):
        m = dict(shared)
        m.update(_prep_core(inputs, core))
        in_maps.append(m)
    if "nc" not in _CACHE:
        _CACHE["nc"] = build_nc()
    nc = _CACHE["nc"]
    res = run_bass_kernel_spmd(nc, in_maps, core_ids=list(range(n_cores)))
    out = np.empty((n_cores * NB, SEQ, D), np.float32)
    for core in range(n_cores):
        o = res.results[core]["out"]
        for i in range(NB):
            out[core * NB + i] = o[i].T
    return out
```

```python
import contextlib
import os
_EXP = os.environ.get("KEXP", "")
import numpy as np
import ml_dtypes
import concourse.bass as bass
import concourse.mybir as mybir
from concourse.bass_utils import run_bass_kernel_spmd

F32 = mybir.dt.float32
BF16 = mybir.dt.bfloat16
ALU = mybir.AluOpType
AF = mybir.ActivationFunctionType

L = 4
D = 1024
NB = 2
CTX = 256
SEQ = 2048
T = CTX + SEQ
DFF = 2816
NFC = DFF // 128
CH = 256
NCH = T // CH
EPS = 1e-6
NEGV = -30000.0

WC_F = 0
WC_CQ = 256
WC_CKV = 512
WC_KRC = 640
WC_KRS = 736
WC_SQA = 832
WC_SQB = 960
WC_SQAS = 1088
WC_SQBS = 1216
WC_SK = 1344
WC_SKS = 1472
WC_SV = 1600
NWC = 1728

V_GF1 = 0
V_GMIX = 8
V_GF2 = 16
V_GCQ = 24
V_GCKV = 26
V_GMK = 27
V_GMQ = 29
V_GSQ = 31
V_GSK = 33
V_SINK = 35
V_BADA = 39
VL = 111


class Tok:
    __slots__ = ("w", "r", "excl")

    def __init__(self, excl=False):
        self.w = None
        self.r = {}
        self.excl = excl


class Prog:
    def __init__(self, nc, stack):
        self.nc = nc
        self.eng = {"pe": nc.tensor, "act": nc.scalar, "dve": nc.vector, "pool": nc.gpsimd, "sp": nc.sync}
        self.sem = {}
        self.total = {}
        for e in self.eng:
            self.sem[e] = stack.enter_context(nc.semaphore("s_" + e))
            self.total[e] = 0
        self.dq = {"sp": [], "pool": []}
        for q in self.dq:
            for i in range(14):
                n = "d_%s%d" % (q, i)
                self.sem[n] = stack.enter_context(nc.semaphore(n))
                self.total[n] = 0
                self.dq[q].append(n)
        self.dqi = {"sp": 0, "pool": 0}
        self.known = {e: {} for e in self.eng}
        self.ninstr = 0

    def wait(self, e, ev):
        s, v = ev
        if self.known[e].get(s, 0) >= v:
            return
        self.eng[e].wait_ge(self.sem[s], v)
        self.known[e][s] = v
        self.ninstr += 1

    def _deps(self, e, reads, writes):
        evs = {}
        for t in reads:
            if t.w is not None:
                s, v = t.w
                if v > evs.get(s, 0):
                    evs[s] = v
            if t.excl:
                for s, v in t.r.items():
                    if s != e and v > evs.get(s, 0):
                        evs[s] = v
        for t in writes:
            if t.w is not None:
                s, v = t.w
                if s != e and v > evs.get(s, 0):
                    evs[s] = v
            for s, v in t.r.items():
                if s != e and v > evs.get(s, 0):
                    evs[s] = v
        if e == "pe":
            evs.pop("pe", None)
        for s, v in evs.items():
            self.wait(e, (s, v))

    def _mark(self, ev, reads, writes):
        s, v = ev
        for t in reads:
            if v > t.r.get(s, 0):
                t.r[s] = v
        for t in writes:
            t.w = ev
            t.r = {}

    def op(self, e, ins_fn, reads=(), writes=()):
        self._deps(e, reads, writes)
        ins = ins_fn(self.eng[e])
        ins.then_inc(self.sem[e], 1)
        self.total[e] += 1
        self.ninstr += 1
        self._mark((e, self.total[e]), reads, writes)

    def dma(self, q, out, in_, reads=(), writes=()):
        sems = self.dq[q]
        s = sems[self.dqi[q] % len(sems)]
        self.dqi[q] += 1
        if self.total[s] > 0:
            self.wait(q, (s, self.total[s]))
        self._deps(q, reads, writes)
        self.eng[q].dma_start(out=out, in_=in_).then_inc(self.sem[s], 16)
        self.total[s] += 16
        self.ninstr += 1
        self._mark((s, self.total[s]), reads, writes)

    def barrier(self):
        for e in self.eng:
            for s, v in self.total.items():
                if v > 0 and s != e:
                    self.wait(e, (s, v))
            if self.total[e] > 0:
                self.wait(e, (e, self.total[e]))


def _rope_tables():
    pos = np.arange(SEQ)
    row = (pos // 64).astype(np.float32)
    col = (pos % 64).astype(np.float32)

    def tab(dim):
        axis_dim = dim // 2
        inv = (10000.0 ** (-np.arange(0, axis_dim, 2, dtype=np.float32) / axis_dim)).astype(np.float32)
        ang = np.concatenate([row[:, None] * inv, col[:, None] * inv], axis=-1).astype(np.float32)
        c = np.cos(ang).astype(np.float32)
        s = np.sin(ang).astype(np.float32)
        C = np.repeat(c, 2, axis=1).T
        S = np.repeat(s, 2, axis=1).T.copy()
        S[0::2] *= -1.0
        return C, S

    Cm, Sm = tab(32)
    Cs, Ss = tab(64)
    ropem = np.zeros((128, 2, SEQ), np.float32)
    ropem[64:96, 0] = Cm
    ropem[64:96, 1] = Sm
    ropes = np.zeros((128, 2, SEQ), np.float32)
    ropes[0:64, 0] = Cs
    ropes[64:128, 0] = Cs
    ropes[0:64, 1] = Ss
    ropes[64:128, 1] = Ss
    return ropem, ropes


def _dft_tables():
    bf = ml_dtypes.bfloat16
    c = np.arange(64)
    ang = 2.0 * np.pi * np.outer(c, c) / 64.0
    Cc = np.cos(ang) / 8.0
    Sc = np.sin(ang) / 8.0
    dftc = np.zeros((128, 256), np.float64)
    for g in range(2):
        dftc[g * 64:(g + 1) * 64, g * 64:(g + 1) * 64] = Cc
        dftc[g * 64:(g + 1) * 64, 128 + g * 64:128 + (g + 1) * 64] = -Sc
    p = np.arange(SEQ)
    kp = np.outer(p, p) % SEQ
    ang = 2.0 * np.pi * kp / SEQ
    sc = 1.0 / np.sqrt(SEQ)
    Cn = (np.cos(ang) * sc).astype(np.float32)
    Sn = (np.sin(ang) * sc).astype(np.float32)
    dftn = np.empty((4, 128, 2, 16, 512), np.float32)
    for i, X in enumerate((Cn, Sn)):
        dftn[:, :, i] = X.reshape(16, 128, 4, 512).transpose(2, 1, 0, 3)
    p = np.arange(CTX)
    ang = 2.0 * np.pi * (np.outer(p, p) % CTX) / CTX
    dftx = np.empty((128, 2, 2, CTX), np.float32)
    for i, X in enumerate((np.cos(ang) / 16.0, np.sin(ang) / 16.0)):
        dftx[:, i] = X.reshape(2, 128, CTX).transpose(1, 0, 2)
    maskb = np.full((128, 6, 4, 128), NEGV, np.float32)
    sl = np.arange(128)[:, None]
    tl = np.arange(128)[None, :]
    for ri, r in enumerate(range(-1, 5)):
        for qq in range(4):
            d = qq - r
            if d == 0:
                maskb[:, ri, qq] = 0.0
            elif d == -1:
                maskb[:, ri, qq] = np.where(tl >= sl, 0.0, NEGV)
            elif d == 1:
                maskb[:, ri, qq] = np.where(tl <= sl, 0.0, NEGV)
    ident = np.eye(128, dtype=np.float32)
    return (dftc.astype(np.float32).astype(bf), dftn.astype(bf), dftx.astype(bf),
            maskb.reshape(128, 6, 512).astype(bf), ident.astype(bf))


def _fm(v, nk):
    return np.ascontiguousarray(np.asarray(v, np.float32).reshape(nk, 128).T)


def _swap_pairs(a, axis=-1):
    a = np.moveaxis(a, axis, -1)
    sh = a.shape
    b = a.reshape(sh[:-1] + (sh[-1] // 2, 2))[..., ::-1].reshape(sh)
    return np.moveaxis(b, -1, axis)


def _prep_shared(inp):
    f32 = np.float32
    w_ada = np.asarray(inp["w_ada"], f32)
    sh = {}
    sh["wada"] = np.ascontiguousarray(w_ada.reshape(L, 8, 128, 9 * D).transpose(0, 2, 1, 3))
    w1 = np.stack([np.asarray(inp["w1_ffn1"], f32), np.asarray(inp["w1_ffn2"], f32)], 1)
    w3 = np.stack([np.asarray(inp["w3_ffn1"], f32), np.asarray(inp["w3_ffn2"], f32)], 1)
    w2 = np.stack([np.asarray(inp["w2_ffn1"], f32), np.asarray(inp["w2_ffn2"], f32)], 1)
    sh["w1"] = np.ascontiguousarray(w1.reshape(L, 2, 8, 128, DFF).transpose(0, 1, 3, 2, 4))
    sh["w3"] = np.ascontiguousarray(w3.reshape(L, 2, 8, 128, DFF).transpose(0, 1, 3, 2, 4))
    sh["w2"] = np.ascontiguousarray(w2.reshape(L, 2, NFC, 128, D).transpose(0, 1, 3, 2, 4))
    w_in = np.asarray(inp["w_in"], f32)
    fcols = w_in[:, :, 0:256]
    cq = w_in[:, :, 256:512]
    ckv = w_in[:, :, 512:640]
    kr = w_in[:, :, 640:672]
    sq = w_in[:, :, 672:928].reshape(L, D, 4, 64)
    sk = w_in[:, :, 928:1056]
    sv = w_in[:, :, 1056:1184]
    sqA = np.concatenate([sq[:, :, 0], sq[:, :, 2]], -1)
    sqB = np.concatenate([sq[:, :, 1], sq[:, :, 3]], -1)
    ext = np.concatenate([
        fcols, cq, ckv,
        ckv[:, :, 64:128], kr,
        ckv[:, :, 64:128], _swap_pairs(kr),
        sqA, sqB, _swap_pairs(sqA), _swap_pairs(sqB),
        sk, _swap_pairs(sk), sv], -1)
    assert ext.shape[-1] == NWC
    sh["win"] = np.ascontiguousarray(ext.reshape(L, 8, 128, NWC).transpose(0, 2, 1, 3))
    w_uq = np.asarray(inp["w_uq"], f32).reshape(L, 2, 128, 8, 96)
    w_uq_s = w_uq.copy()
    w_uq_s[..., 64:96] = _swap_pairs(w_uq[..., 64:96])
    sh["wuq"] = np.ascontiguousarray(np.stack([w_uq, w_uq_s], 4).transpose(0, 2, 1, 3, 4, 5))
    sh["wukv"] = np.ascontiguousarray(np.asarray(inp["w_ukv"], f32).reshape(L, 128, 8, 128))
    w_out = np.asarray(inp["w_out"], f32)
    rows = np.concatenate([np.arange(0, 768), np.arange(768, 832), np.arange(896, 960),
                           np.arange(832, 896), np.arange(960, 1024)])
    sh["wout"] = np.ascontiguousarray(w_out[:, rows].reshape(L, 8, 128, D).transpose(0, 2, 1, 3))
    vecs = np.zeros((128, L, VL), f32)
    for l in range(L):
        vecs[:, l, V_GF1:V_GF1 + 8] = _fm(inp["g_ffn1"][l], 8)
        vecs[:, l, V_GMIX:V_GMIX + 8] = _fm(inp["g_mix"][l], 8)
        vecs[:, l, V_GF2:V_GF2 + 8] = _fm(inp["g_ffn2"][l], 8)
        vecs[:, l, V_GCQ:V_GCQ + 2] = _fm(inp["g_cq"][l], 2)
        vecs[:, l, V_GCKV] = np.asarray(inp["g_ckv"][l], f32)
        for (col, name) in ((V_GMK, "g_mla_k"), (V_GMQ, "g_mla_q")):
            g = np.asarray(inp[name][l], f32)
            vecs[0:96, l, col] = g
            vecs[64:96, l, col + 1] = _swap_pairs(g[64:96])
        for (col, name) in ((V_GSQ, "g_swa_q"), (V_GSK, "g_swa_k")):
            g = np.asarray(inp[name][l], f32)
            vecs[:, l, col] = np.concatenate([g, g])
            vecs[:, l, col + 1] = np.concatenate([_swap_pairs(g), _swap_pairs(g)])
        vecs[:, l, V_SINK:V_SINK + 4] = np.asarray(inp["sink"][l], f32)[None, :]
        vecs[:, l, V_BADA:V_BADA + 72] = _fm(inp["b_ada"][l], 72)
    sh["vecs"] = vecs.reshape(128, L * VL)
    sh["ropem"], sh["ropes"] = _rope_tables()
    sh["dftc"], sh["dftn"], sh["dftx"], sh["maskb"], sh["ident"] = _dft_tables()
    return sh


def _prep_core(inp, core):
    f32 = np.float32
    b0 = core * NB
    x = np.asarray(inp["x"], f32)
    ctx = np.asarray(inp["ctx"], f32)
    hin = np.empty((NB, D, T), f32)
    for i in range(NB):
        hin[i, :, :CTX] = ctx[b0 + i].T
        hin[i, :, CTX:] = x[b0 + i].T
    c = np.asarray(inp["c"], f32)
    cv = np.stack([c[b0], c[b0 + 1], np.asarray(inp["c_ctx"], f32)], 0)
    cT = np.ascontiguousarray(cv.reshape(3, 8, 128).transpose(2, 1, 0))
    return {"hin": hin, "cT": cT}


def build_nc(n_layers=L, stop_after=None, debug=False):
    nc = bass.Bass("TRN2", target_bir_lowering=False)
    okind = "ExternalOutput" if debug else "Internal"

    def din(name, shape, dt=F32):
        return nc.dram_tensor(name, list(shape), dt, kind="ExternalInput").ap()

    def dscr(name, shape, dt):
        return nc.dram_tensor(name, list(shape), dt, kind=okind).ap()

    hin = din("hin", [NB, D, T])
    cT = din("cT", [128, 8, 3])
    wada = din("wada", [n_layers, 128, 8, 9 * D])
    w1d = din("w1", [n_layers, 2, 128, 8, DFF])
    w3d = din("w3", [n_layers, 2, 128, 8, DFF])
    w2d = din("w2", [n_layers, 2, 128, NFC, D])
    wind = din("win", [n_layers, 128, 8, NWC])
    wuqd = din("wuq", [n_layers, 128, 2, 8, 2, 96])
    wukvd = din("wukv", [n_layers, 128, 8, 128])
    woutd = din("wout", [n_layers, 128, 8, D])
    vecsd = din("vecs", [128, L * VL])
    ropemd = din("ropem", [128, 2, SEQ])
    ropesd = din("ropes", [128, 2, SEQ])
    dftcd = din("dftc", [128, 256], BF16)
    dftnd = din("dftn", [4, 128, 2, 16, 512], BF16)
    dftxd = din("dftx", [128, 2, 2, CTX], BF16)
    maskbd = din("maskb", [128, 6, 512], BF16)
    identd = din("ident", [128, 128], BF16)
    outd = nc.dram_tensor("out", [NB, D, SEQ], F32, kind="ExternalOutput").ap()

    hd = dscr("h", [NB, D, T], F32)
    find = dscr("fin", [NB, 256, T], BF16)
    cqnd = dscr("cqn", [NB, 256, T], BF16)
    ckvnd = dscr("ckvn", [NB, 128, T], BF16)
    krrd = dscr("krr", [NB, 32, T], F32)
    krsqd = dscr("krsq", [NB, 32, T], BF16)
    sqd = dscr("sq", [NB, 2, 128, T], BF16)
    skd = dscr("sk", [NB, 128, T], BF16)
    svd = dscr("sv", [NB, T, 128], BF16)
    mixd = dscr("mix", [NB, D, T], BF16)
    nscd = dscr("nsc", [NB, D, T], BF16)

    with contextlib.ExitStack() as top:
        P = Prog(nc, top)

        uid = [0]

        def sb(stack, name, shape, dt):
            uid[0] += 1
            return stack.enter_context(nc.sbuf_tensor("s%d_%s" % (uid[0], name), list(shape), dt))

        def ps(stack, name, shape):
            uid[0] += 1
            n = 1
            for d in shape[1:]:
                n *= d
            assert n <= 512
            if n < 512 and len(shape) == 2:
                shape = [shape[0], 512]
            assert n == 512 or len(shape) == 2, shape
            return stack.enter_context(nc.psum_tensor("p%d_%s" % (uid[0], name), list(shape), F32))

        def PTok():
            return Tok(excl=True)

        vecs = sb(top, "vecs", [128, L * VL], F32)
        mod = sb(top, "mod", [128, L, 3, 72], F32)
        gsv = sb(top, "gsv", [128, L, 3, 24], F32)
        hgv = sb(top, "hgv", [128, L, 3, 16], F32)
        esink = sb(top, "esink", [128, L * 4], F32)
        ones = sb(top, "ones", [128, 128], BF16)
        onesbd = sb(top, "onesbd", [128, 128], BF16)
        ident = sb(top, "ident", [128, 128], BF16)
        t_vecs, t_mod, t_gsv, t_hgv, t_esink, t_const = Tok(), Tok(), Tok(), Tok(), Tok(), Tok()

        P.dma("sp", vecs[:], vecsd, writes=[t_vecs])
        P.dma("sp", ident[:], identd, writes=[t_const])
        P.op("dve", lambda e: e.memset(ones[:], 1.0), writes=[t_const])
        P.op("dve", lambda e: e.memset(onesbd[:], 0.0), writes=[t_const])
        P.op("dve", lambda e: e.memset(onesbd[0:64, 0:64], 1.0), writes=[t_const])
        P.op("dve", lambda e: e.memset(onesbd[64:128, 64:128], 1.0), writes=[t_const])

        def vcol(l, c, n=1):
            return vecs[:, l * VL + c: l * VL + c + n]

        with contextlib.ExitStack() as ph:
            c32 = sb(ph, "c32", [128, 8, 3], F32)
            scb = sb(ph, "scb", [128, 8, 3], BF16)
            wa = [sb(ph, "wa%d" % i, [128, 8, D], BF16) for i in range(3)]
            pm = ps(ph, "pm", [128, 128, 4])
            t_c, t_sc, t_pm = Tok(), Tok(), PTok()
            t_wa = [Tok() for _ in range(3)]
            P.dma("sp", c32[:], cT, writes=[t_c])
            P.op("act", lambda e: e.activation(out=scb[:], in_=c32[:], func=AF.Silu), reads=[t_c], writes=[t_sc])
            gi = 0
            for l in range(n_layers):
                for j in range(9):
                    w = wa[gi % 3]
                    tw = t_wa[gi % 3]
                    gi += 1
                    P.dma("pool", w[:], wada[l, :, :, j * D:(j + 1) * D], writes=[tw])
                    for dk in range(8):
                        for k in range(8):
                            P.op("pe", lambda e, w=w, k=k, dk=dk, j=j: e.matmul(
                                pm[:, j * 8 + dk, 0:3], w[:, k, dk * 128:(dk + 1) * 128], scb[:, k, :],
                                start=(k == 0), stop=(k == 7)), reads=[tw, t_sc], writes=[t_pm])
                for v in range(3):
                    P.op("dve", lambda e, l=l, v=v: e.tensor_tensor(
                        out=mod[:, l, v, :], in0=pm[:, 0:72, v], in1=vcol(l, V_BADA, 72), op=ALU.add),
                        reads=[t_pm, t_vecs], writes=[t_mod])
                for v in range(3):
                    for si, (jj, gc) in enumerate(((1, V_GF1), (4, V_GMIX), (7, V_GF2))):
                        P.op("dve", lambda e, l=l, v=v, si=si, jj=jj, gc=gc: e.scalar_tensor_tensor(
                            out=gsv[:, l, v, si * 8:(si + 1) * 8], in0=mod[:, l, v, jj * 8:(jj + 1) * 8], scalar=1.0,
                            in1=vcol(l, gc, 8), op0=ALU.add, op1=ALU.mult), reads=[t_mod, t_vecs], writes=[t_gsv])
                    for si, jj in enumerate((2, 8)):
                        P.op("dve", lambda e, l=l, v=v, si=si, jj=jj: e.tensor_scalar(
                            out=hgv[:, l, v, si * 8:(si + 1) * 8], in0=mod[:, l, v, jj * 8:(jj + 1) * 8],
                            scalar1=0.5, scalar2=None, op0=ALU.mult), reads=[t_mod], writes=[t_hgv])
            P.op("act", lambda e: e.activation(out=esink[:].rearrange("p (l f) -> p l f", l=L),
                                               in_=vecs[:].rearrange("p (l f) -> p l f", l=L)[:, :, V_SINK:V_SINK + 4],
                                               func=AF.Exp), reads=[t_vecs], writes=[t_esink])
            P.barrier()

        def vec_of(b, c):
            return 2 if c == 0 else b

        class NormBufs:
            def __init__(self, stack, pssq, nhc=2):
                self.nhc = nhc
                self.hc = [sb(stack, "hc%d" % i, [128, 8, CH], F32) for i in range(nhc)]
                self.t_hc = [Tok() for _ in range(nhc)]
                self.sqb = sb(stack, "sqb", [128, 8, CH], BF16)
                self.t_sqb = Tok()
                self.tmp = sb(stack, "ntmp", [128, 8, CH], F32)
                self.t_tmp = Tok()
                self.nb = [sb(stack, "nb%d" % i, [128, 8, CH], BF16) for i in range(2)]
                self.t_nb = [Tok(), Tok()]
                self.rt = sb(stack, "nrt", [128, CH], F32)
                self.rstd = sb(stack, "nrstd", [128, CH], F32)
                self.t_rt, self.t_rstd = Tok(), Tok()
                self.pssq = pssq
                self.t_pssq = PTok()

        def norm_stage(NBf, i, src, l, v, gs_off, sh_j, preloaded=False):
            hc, t_hc = NBf.hc[i % NBf.nhc], NBf.t_hc[i % NBf.nhc]
            nb, t_nb = NBf.nb[i % 2], NBf.t_nb[i % 2]
            if not preloaded:
                P.dma("sp", hc[:], src.rearrange("(k p) t -> p k t", p=128), writes=[t_hc])
            P.op("act", lambda e: e.activation(out=NBf.sqb[:], in_=hc[:], func=AF.Square),
                 reads=[t_hc], writes=[NBf.t_sqb])
            for k in range(8):
                P.op("pe", lambda e, k=k: e.matmul(NBf.pssq[:, 0:CH], ones[:], NBf.sqb[:, k, :], start=(k == 0), stop=(k == 7)),
                     reads=[NBf.t_sqb, t_const], writes=[NBf.t_pssq])
            P.op("act", lambda e: e.activation(out=NBf.rt[:], in_=NBf.pssq[:, 0:CH], func=AF.Ln, bias=EPS, scale=1.0 / D),
                 reads=[NBf.t_pssq], writes=[NBf.t_rt])
            P.op("act", lambda e: e.activation(out=NBf.rstd[:], in_=NBf.rt[:], func=AF.Exp, scale=-0.5),
                 reads=[NBf.t_rt], writes=[NBf.t_rstd])
            gs_b = gsv[:, l, v, gs_off:gs_off + 8].unsqueeze(2).broadcast_to([128, 8, CH])
            sh_b = mod[:, l, v, sh_j * 8:(sh_j + 1) * 8].unsqueeze(2).broadcast_to([128, 8, CH])
            rs_b = NBf.rstd[:].unsqueeze(1).broadcast_to([128, 8, CH])
            P.op("dve", lambda e: e.tensor_tensor(out=NBf.tmp[:], in0=hc[:], in1=gs_b, op=ALU.mult),
                 reads=[t_hc, t_gsv], writes=[NBf.t_tmp])
            P.op("dve", lambda e: e.tensor_tensor(out=NBf.tmp[:], in0=NBf.tmp[:], in1=rs_b, op=ALU.mult),
                 reads=[NBf.t_tmp, NBf.t_rstd], writes=[NBf.t_tmp])
            P.op("dve" if "nbdve" in _EXP else "pool", lambda e: e.tensor_tensor(out=nb[:], in0=NBf.tmp[:], in1=sh_b, op=ALU.add),
                 reads=[NBf.t_tmp, t_mod], writes=[t_nb])
            return hc, t_hc, nb, t_nb

        HF = NFC // 2

        class WSet:
            def __init__(self, stack, tag):
                self.W1 = sb(stack, "W1" + tag, [128, 8, HF * 128], BF16)
                self.W3 = sb(stack, "W3" + tag, [128, 8, HF * 128], BF16)
                self.W2 = sb(stack, "W2" + tag, [128, HF, D], BF16)
                self.t13 = Tok()
                self.t2 = Tok()

        def load_w(ws, l, which, half):
            c0, c1 = half * HF * 128, (half + 1) * HF * 128
            P.dma("pool", ws.W1[:], w1d[l, which, :, :, c0:c1], writes=[ws.t13])
            P.dma("pool", ws.W3[:], w3d[l, which, :, :, c0:c1], writes=[ws.t13])
            P.dma("pool", ws.W2[:], w2d[l, which, :, half * HF:(half + 1) * HF, :], writes=[ws.t2])

        def ffn_pass(l, which, half, ws, first, last, prefetch=None):
            with contextlib.ExitStack() as ph:
                pssq = ps(ph, "pssq", [128, CH])
                ph13 = [ps(ph, "ph13_%d" % i, [128, 2, CH]) for i in range(3)]
                t_ph13 = [PTok() for _ in range(3)]
                py = [ps(ph, "py%d" % i, [128, 2, CH]) for i in range(4)]
                t_py = [PTok() for _ in range(4)]
                NBf = NormBufs(ph, pssq, nhc=3)
                g = sb(ph, "g", [128, HF, CH], BF16)
                t_g = Tok()
                sl = [sb(ph, "sl%d" % i, [128, CH], F32) for i in range(3)]
                t_sl = [Tok() for _ in range(3)]
                if prefetch is not None:
                    prefetch()

                chunks = [(b, c) for b in range(NB) for c in range(NCH) if not (last and which == 1 and c == 0)]
                gs_off = 0 if which == 0 else 16
                sh_j = 0 if which == 0 else 6
                src_t = hin if (first and half == 0) else hd
                state = {}

                def load_chunk(i):
                    b, c = chunks[i]
                    tsl = slice(c * CH, (c + 1) * CH)
                    hc, t_hc = NBf.hc[i % 3], NBf.t_hc[i % 3]
                    P.dma("sp", hc[:], src_t[b, :, tsl].rearrange("(k p) t -> p k t", p=128), writes=[t_hc])
                    if half == 1:
                        nb, t_nb = NBf.nb[i % 2], NBf.t_nb[i % 2]
                        P.dma("sp", nb[:], nscd[b, :, tsl].rearrange("(k p) t -> p k t", p=128), writes=[t_nb])

                def stage_a(i):
                    b, c = chunks[i]
                    tsl = slice(c * CH, (c + 1) * CH)
                    if half == 0:
                        st = norm_stage(NBf, i, None, l, vec_of(b, c), gs_off, sh_j, preloaded=True)
                        hc, t_hc, nb, t_nb = st
                        if "nost" not in _EXP:
                            P.dma("pool", nscd[b, :, tsl].rearrange("(k p) t -> p k t", p=128), nb[:], reads=[t_nb])
                    else:
                        st = (NBf.hc[i % 3], NBf.t_hc[i % 3], NBf.nb[i % 2], NBf.t_nb[i % 2])
                    state[i] = st

                def stage_b(i):
                    hc, t_hc, nb, t_nb = state[i]
                    for fc in range(HF):
                        pp, tp = ph13[fc % 3], t_ph13[fc % 3]
                        for wi, W in enumerate((ws.W1, ws.W3)):
                            for k in range(8):
                                P.op("pe", lambda e, W=W, wi=wi, k=k, fc=fc, pp=pp: e.matmul(
                                    pp[:, wi, :], W[:, k, fc * 128:(fc + 1) * 128], nb[:, k, :],
                                    start=(k == 0), stop=(k == 7)), reads=[ws.t13, t_nb], writes=[tp])
                        s_, ts_ = sl[fc % 3], t_sl[fc % 3]
                        P.op("act", lambda e, pp=pp, s_=s_: e.activation(out=s_[:], in_=pp[:, 0, :], func=AF.Silu),
                             reads=[tp], writes=[ts_])
                        P.op("dve", lambda e, pp=pp, s_=s_, fc=fc: e.tensor_tensor(
                            out=g[:, fc, :], in0=s_[:], in1=pp[:, 1, :], op=ALU.mult),
                            reads=[ts_, tp], writes=[t_g])

                def stage_c(i):
                    b, c = chunks[i]
                    hc, t_hc, nb, t_nb = state.pop(i)
                    v = vec_of(b, c)
                    for dk in range(8):
                        yy = py[dk // 2][:, dk % 2, :]
                        for fc in range(HF):
                            P.op("pe", lambda e, yy=yy, fc=fc, dk=dk: e.matmul(
                                yy, ws.W2[:, fc, dk * 128:(dk + 1) * 128], g[:, fc, :],
                                start=(fc == 0), stop=(fc == HF - 1)), reads=[ws.t2, t_g], writes=[t_py[dk // 2]])
                        P.op("dve", lambda e, yy=yy, dk=dk: e.scalar_tensor_tensor(
                            out=hc[:, dk, :], in0=yy, scalar=hgv[:, l, v, which * 8 + dk: which * 8 + dk + 1],
                            in1=hc[:, dk, :], op0=ALU.mult, op1=ALU.add), reads=[t_py[dk // 2], t_hc, t_hgv], writes=[t_hc])
                    if last and which == 1 and half == 1:
                        dst = outd[b, :, c * CH - CTX:(c + 1) * CH - CTX]
                    else:
                        dst = hd[b, :, c * CH:(c + 1) * CH]
                    if "nost" not in _EXP:
                        P.dma("pool", dst.rearrange("(k p) t -> p k t", p=128), hc[:], reads=[t_hc])

                n = len(chunks)
                load_chunk(0)
                if half == 0 and n > 1:
                    load_chunk(1)
                stage_a(0)
                for i in range(n):
                    if half == 0:
                        if i + 2 < n:
                            load_chunk(i + 2)
                    else:
                        if i + 1 < n:
                            load_chunk(i + 1)
                    if i + 1 < n:
                        stage_a(i + 1)
                    stage_b(i)
                    stage_c(i)
                P.barrier()

        def mixin_phase(l):
            with contextlib.ExitStack() as ph:
                Win = sb(ph, "Win", [128, 8, NWC], BF16)
                t_win = Tok()
                P.dma("pool", Win[:], wind[l], writes=[t_win])
                ropem = sb(ph, "ropem", [128, 2, SEQ], F32)
                ropes = sb(ph, "ropes", [128, 2, SEQ], F32)
                t_rope = Tok()
                P.dma("sp", ropem[:], ropemd, writes=[t_rope])
                P.dma("sp", ropes[:], ropesd, writes=[t_rope])
                pssq = ps(ph, "pssq", [128, CH])
                pj = [ps(ph, "pj%d" % i, [128, 2, CH]) for i in range(5)]
                t_pj = [PTok() for _ in range(5)]
                pq = [ps(ph, "pq%d" % i, [128, 512]) for i in range(2)]
                t_pq = [PTok() for _ in range(2)]
                NBf = NormBufs(ph, pssq)
                cnt = {"pj": 0, "pq": 0, "st": 0}
                stg = [sb(ph, "stg%d" % i, [128, 2, CH], BF16) for i in range(6)]
                t_stg = [Tok() for _ in range(6)]
                stf = [sb(ph, "stf%d" % i, [128, CH], F32) for i in range(2)]
                t_stf = [Tok() for _ in range(2)]
                sqs = [sb(ph, "sqs%d" % i, [128, 2, CH], BF16) for i in range(2)]
                t_sqs = [Tok() for _ in range(2)]
                rts = [sb(ph, "rts%d" % i, [128, CH], F32) for i in range(2)]
                t_rts = [Tok() for _ in range(2)]
                rsd = [sb(ph, "rsd%d" % i, [128, CH], F32) for i in range(2)]
                t_rsd = [Tok() for _ in range(2)]
                t1s = [sb(ph, "t1s%d" % i, [128, CH], F32) for i in range(2)]
                t2s = [sb(ph, "t2s%d" % i, [128, CH], F32) for i in range(2)]
                t_t1 = [Tok() for _ in range(2)]
                t_t2 = [Tok() for _ in range(2)]
                c2 = {"n": 0}

                def proj(nb, t_nb, col0, ncols, slot):
                    tile_, idx, tk = slot
                    for k in range(8):
                        P.op("pe", lambda e, k=k: e.matmul(tile_[0:ncols, idx, :], Win[:, k, col0:col0 + ncols], nb[:, k, :],
                                                           start=(k == 0), stop=(k == 7)), reads=[t_win, t_nb], writes=[tk])

                def next_pj():
                    i = cnt["pj"] % 5
                    cnt["pj"] += 1
                    return pj[i], t_pj[i]

                def next_pq():
                    i = cnt["pq"] % 2
                    cnt["pq"] += 1
                    return pq[i][:, 0:CH], t_pq[i]

                def next_stg():
                    i = cnt["st"] % 6
                    cnt["st"] += 1
                    return stg[i], t_stg[i]

                def small_rstd(src_ap, t_src, nparts, lhs_ones, nfeat, nslots):
                    j = c2["n"] % 2
                    c2["n"] += 1
                    sq_ = sqs[j]
                    P.op("act", lambda e: e.activation(out=sq_[0:nparts, 0:nslots, :], in_=src_ap, func=AF.Square),
                         reads=[t_src], writes=[t_sqs[j]])
                    pqa, tq = next_pq()
                    for s in range(nslots):
                        P.op("pe", lambda e, s=s: e.matmul(pqa[0:nparts, :], lhs_ones[0:nparts, 0:nparts], sq_[0:nparts, s, :],
                                                           start=(s == 0), stop=(s == nslots - 1)),
                             reads=[t_sqs[j], t_const], writes=[tq])
                    P.op("act", lambda e: e.activation(out=rts[j][0:nparts, :], in_=pqa[0:nparts, :], func=AF.Ln,
                                                       bias=EPS, scale=1.0 / nfeat), reads=[tq], writes=[t_rts[j]])
                    P.op("act", lambda e: e.activation(out=rsd[j][0:nparts, :], in_=rts[j][0:nparts, :], func=AF.Exp, scale=-0.5),
                         reads=[t_rts[j]], writes=[t_rsd[j]])
                    return rsd[j], t_rsd[j]

                tmpq = sb(ph, "tmpq", [128, 2, CH], F32)
                t_tmpq = Tok()
                bcs = [(b, c) for b in range(NB) for c in range(NCH)]
                nxt = norm_stage(NBf, 0, hd[bcs[0][0], :, bcs[0][1] * CH:(bcs[0][1] + 1) * CH], l, vec_of(*bcs[0]), 8, 3)
                for i, (b, c) in enumerate(bcs):
                    if True:
                        hc, t_hc, nb, t_nb = nxt
                        tsl = slice(c * CH, (c + 1) * CH)
                        lat = c > 0
                        psl = slice((c - 1) * CH, c * CH)
                        pt, tk = next_pj()
                        proj(nb, t_nb, WC_F, 128, (pt, 0, tk))
                        proj(nb, t_nb, WC_F + 128, 128, (pt, 1, tk))
                        st, ts = next_stg()
                        P.op("act", lambda e, st=st, pt=pt: e.activation(out=st[:], in_=pt[:], func=AF.Copy),
                             reads=[tk], writes=[ts])
                        P.dma("pool", find[b, :, tsl].rearrange("(j p) t -> p j t", p=128), st[:], reads=[ts])
                        pt, tk = next_pj()
                        proj(nb, t_nb, WC_CQ, 128, (pt, 0, tk))
                        proj(nb, t_nb, WC_CQ + 128, 128, (pt, 1, tk))
                        rs, trs = small_rstd(pt[:], tk, 128, ones, 256.0, 2)
                        st, ts = next_stg()
                        gq_b = vcol(l, V_GCQ, 2).unsqueeze(2).broadcast_to([128, 2, CH])
                        rs_b = rs[:].unsqueeze(1).broadcast_to([128, 2, CH])
                        P.op("dve", lambda e, pt=pt, gq_b=gq_b: e.tensor_tensor(out=tmpq[:], in0=pt[:], in1=gq_b, op=ALU.mult),
                             reads=[tk, t_vecs], writes=[t_tmpq])
                        P.op("dve", lambda e, st=st, rs_b=rs_b: e.tensor_tensor(out=st[:], in0=tmpq[:], in1=rs_b, op=ALU.mult),
                             reads=[t_tmpq, trs], writes=[ts])
                        P.dma("pool", cqnd[b, :, tsl].rearrange("(j p) t -> p j t", p=128), st[:], reads=[ts])
                        if i + 1 < len(bcs):
                            b2, c2_ = bcs[i + 1]
                            nxt = norm_stage(NBf, i + 1, hd[b2, :, c2_ * CH:(c2_ + 1) * CH], l, vec_of(b2, c2_), 8, 3)
                        pt, tk = next_pj()
                        proj(nb, t_nb, WC_CKV, 128, (pt, 0, tk))
                        rs, trs = small_rstd(pt[:, 0:1, :], tk, 128, ones, 128.0, 1)
                        st, ts = next_stg()
                        P.op("dve", lambda e, pt=pt, st=st, rs=rs: e.scalar_tensor_tensor(
                            out=st[:, 0, :], in0=pt[:, 0, :], scalar=vcol(l, V_GCKV), in1=rs[:], op0=ALU.mult, op1=ALU.mult),
                            reads=[tk, trs, t_vecs], writes=[ts])
                        P.dma("pool", ckvnd[b, :, tsl], st[:, 0, :], reads=[ts])
                        pt, tk = next_pj()
                        proj(nb, t_nb, WC_KRC, 96, (pt, 0, tk))
                        if lat:
                            proj(nb, t_nb, WC_KRS, 96, (pt, 1, tk))
                        st, ts = next_stg()
                        P.op("act", lambda e, pt=pt, st=st: e.activation(out=st[64:96, 0, :], in_=pt[64:96, 0, :], func=AF.Square),
                             reads=[tk], writes=[ts])
                        P.dma("pool", krsqd[b, :, tsl], st[64:96, 0, :], reads=[ts])
                        jf = cnt["st"] % 2
                        sf, tsf = stf[jf], t_stf[jf]
                        if lat:
                            j = c2["n"] % 2
                            c2["n"] += 1
                            P.op("dve", lambda e, pt=pt, j=j: e.scalar_tensor_tensor(
                                out=t1s[j][64:96, :], in0=pt[64:96, 0, :], scalar=vcol(l, V_GMK)[64:96, :],
                                in1=ropem[64:96, 0, psl], op0=ALU.mult, op1=ALU.mult),
                                reads=[tk, t_vecs, t_rope], writes=[t_t1[j]])
                            P.op("dve", lambda e, pt=pt, j=j: e.scalar_tensor_tensor(
                                out=t2s[j][64:96, :], in0=pt[64:96, 1, :], scalar=vcol(l, V_GMK + 1)[64:96, :],
                                in1=ropem[64:96, 1, psl], op0=ALU.mult, op1=ALU.mult),
                                reads=[tk, t_vecs, t_rope], writes=[t_t2[j]])
                            P.op("pool", lambda e, sf=sf, j=j: e.tensor_tensor(
                                out=sf[64:96, :], in0=t1s[j][64:96, :], in1=t2s[j][64:96, :], op=ALU.add),
                                reads=[t_t1[j], t_t2[j]], writes=[tsf])
                        else:
                            P.op("dve", lambda e, pt=pt, sf=sf: e.tensor_scalar(
                                out=sf[64:96, :], in0=pt[64:96, 0, :], scalar1=vcol(l, V_GMK)[64:96, :], scalar2=None,
                                op0=ALU.mult), reads=[tk, t_vecs], writes=[tsf])
                        P.dma("pool", krrd[b, :, tsl], sf[64:96, :], reads=[tsf])
                        for (col, cols, gcol, dst) in ((WC_SQA, WC_SQAS, V_GSQ, sqd[b, 0, :, tsl]),
                                                       (WC_SQB, WC_SQBS, V_GSQ, sqd[b, 1, :, tsl]),
                                                       (WC_SK, WC_SKS, V_GSK, skd[b, :, tsl])):
                            pt, tk = next_pj()
                            proj(nb, t_nb, col, 128, (pt, 0, tk))
                            if lat:
                                proj(nb, t_nb, cols, 128, (pt, 1, tk))
                            rs, trs = small_rstd(pt[:, 0:1, :], tk, 128, onesbd, 64.0, 1)
                            st, ts = next_stg()
                            if lat:
                                j = c2["n"] % 2
                                c2["n"] += 1
                                P.op("dve", lambda e, pt=pt, j=j, gcol=gcol: e.scalar_tensor_tensor(
                                    out=t1s[j][:], in0=pt[:, 0, :], scalar=vcol(l, gcol),
                                    in1=ropes[:, 0, psl], op0=ALU.mult, op1=ALU.mult),
                                    reads=[tk, t_vecs, t_rope], writes=[t_t1[j]])
                                P.op("dve", lambda e, pt=pt, j=j, gcol=gcol: e.scalar_tensor_tensor(
                                    out=t2s[j][:], in0=pt[:, 1, :], scalar=vcol(l, gcol + 1),
                                    in1=ropes[:, 1, psl], op0=ALU.mult, op1=ALU.mult),
                                    reads=[tk, t_vecs, t_rope], writes=[t_t2[j]])
                                P.op("pool", lambda e, j=j: e.tensor_tensor(
                                    out=t1s[j][:], in0=t1s[j][:], in1=t2s[j][:], op=ALU.add),
                                    reads=[t_t1[j], t_t2[j]], writes=[t_t1[j]])
                                P.op("dve", lambda e, st=st, j=j, rs=rs: e.tensor_tensor(
                                    out=st[:, 0, :], in0=t1s[j][:], in1=rs[:], op=ALU.mult),
                                    reads=[t_t1[j], trs], writes=[ts])
                            else:
                                P.op("dve", lambda e, pt=pt, st=st, rs=rs, gcol=gcol: e.scalar_tensor_tensor(
                                    out=st[:, 0, :], in0=pt[:, 0, :], scalar=vcol(l, gcol), in1=rs[:],
                                    op0=ALU.mult, op1=ALU.mult), reads=[tk, trs, t_vecs], writes=[ts])
                            P.dma("pool", dst, st[:, 0, :], reads=[ts])
                        pt, tk = next_pj()
                        for tb in range(2):
                            for k in range(8):
                                P.op("pe", lambda e, pt=pt, tb=tb, k=k: e.matmul(
                                    pt[:, 0, tb * 128:(tb + 1) * 128], nb[:, k, tb * 128:(tb + 1) * 128],
                                    Win[:, k, WC_SV:WC_SV + 128], start=(k == 0), stop=(k == 7)),
                                    reads=[t_win, t_nb], writes=[tk])
                        st, ts = next_stg()
                        P.op("act", lambda e, pt=pt, st=st: e.activation(out=st[:, 0, :], in_=pt[:, 0, :], func=AF.Copy),
                             reads=[tk], writes=[ts])
                        P.dma("pool", svd[b, tsl, :].rearrange("(tb p) f -> p tb f", p=128),
                              st[:, 0, :].rearrange("p (tb f) -> p tb f", tb=2), reads=[ts])
                P.barrier()

        def fourier_phase(l, last):
            with contextlib.ExitStack() as ph:
                dftc = sb(ph, "dftc", [128, 256], BF16)
                dftx = sb(ph, "dftx", [128, 2, 2, CTX], BF16)
                t_dc = Tok()
                P.dma("sp", dftc[:], dftcd, writes=[t_dc])
                P.dma("sp", dftx[:], dftxd, writes=[t_dc])
                fin = [sb(ph, "fin%d" % b, [128, 2, T], BF16) for b in range(NB)]
                t_fin = [Tok() for _ in range(NB)]
                A = [sb(ph, "A%d" % b, [128, 18, 2, 256], BF16) for b in range(NB)]
                t_A = [Tok() for _ in range(NB)]
                tab = [sb(ph, "tab%d" % i, [128, 2, 16, 512], BF16) for i in range(2)]
                t_tab = [Tok(), Tok()]
                pa = [ps(ph, "pa%d" % i, [128, 2, 256]) for i in range(3)]
                t_pa = [PTok() for _ in range(3)]
                po = [ps(ph, "po%d" % i, [128, 512]) for i in range(3)]
                t_po = [PTok() for _ in range(3)]
                so = [sb(ph, "so%d" % i, [128, 512], BF16) for i in range(3)]
                t_so = [Tok() for _ in range(3)]
                for b in range(NB):
                    P.dma("sp", fin[b][:], find[b].rearrange("(j p) t -> p j t", p=128), writes=[t_fin[b]])
                n = 0
                for b in range(NB):
                    for pb in range(18):
                        if last and pb < 2:
                            continue
                        pt, tk = pa[n % 3], t_pa[n % 3]
                        for j in range(2):
                            P.op("pe", lambda e, pt=pt, j=j, b=b, pb=pb: e.matmul(
                                pt[:, j, :], fin[b][:, j, pb * 128:(pb + 1) * 128], dftc[:], start=True, stop=True),
                                reads=[t_fin[b], t_dc], writes=[tk])
                        if n % 2 == 0:
                            P.op("act", lambda e, pt=pt, b=b, pb=pb: e.activation(out=A[b][:, pb, :, :], in_=pt[:], func=AF.Copy),
                                 reads=[tk], writes=[t_A[b]])
                        else:
                            P.op("dve", lambda e, pt=pt, b=b, pb=pb: e.tensor_copy(out=A[b][:, pb, :, :], in_=pt[:]),
                                 reads=[tk], writes=[t_A[b]])
                        n += 1
                n = 0
                for kc in range(4):
                    tb_, tt = tab[kc % 2], t_tab[kc % 2]
                    P.dma("sp", tb_[:], dftnd[kc], writes=[tt])
                    for b in range(NB):
                        for j in range(2):
                            pt, tk = po[n % 3], t_po[n % 3]
                            st, ts = so[n % 3], t_so[n % 3]
                            for pb in range(16):
                                for cs in range(2):
                                    P.op("pe", lambda e, pt=pt, b=b, j=j, pb=pb, cs=cs, tb_=tb_: e.matmul(
                                        pt[:], A[b][:, 2 + pb, j, cs * 128:(cs + 1) * 128], tb_[:, cs, pb, :],
                                        start=(pb == 0 and cs == 0), stop=(pb == 15 and cs == 1)),
                                        reads=[t_A[b], tt], writes=[tk])
                            if n % 2 == 0:
                                P.op("act", lambda e, pt=pt, st=st: e.activation(out=st[:], in_=pt[:], func=AF.Copy),
                                     reads=[tk], writes=[ts])
                            else:
                                P.op("dve", lambda e, pt=pt, st=st: e.tensor_copy(out=st[:], in_=pt[:]), reads=[tk], writes=[ts])
                            P.dma("pool", mixd[b, j * 128:(j + 1) * 128, CTX + kc * 512:CTX + (kc + 1) * 512], st[:], reads=[ts])
                            n += 1
                if not last:
                    for b in range(NB):
                        for j in range(2):
                            pt, tk = po[n % 3], t_po[n % 3]
                            st, ts = so[n % 3], t_so[n % 3]
                            for pb in range(2):
                                for cs in range(2):
                                    P.op("pe", lambda e, pt=pt, b=b, j=j, pb=pb, cs=cs: e.matmul(
                                        pt[:, 0:CTX], A[b][:, pb, j, cs * 128:(cs + 1) * 128], dftx[:, cs, pb, :],
                                        start=(pb == 0 and cs == 0), stop=(pb == 1 and cs == 1)),
                                        reads=[t_A[b], t_dc], writes=[tk])
                            P.op("dve", lambda e, pt=pt, st=st: e.tensor_copy(out=st[:, 0:CTX], in_=pt[:, 0:CTX]),
                                 reads=[tk], writes=[ts])
                            P.dma("pool", mixd[b, j * 128:(j + 1) * 128, 0:CTX], st[:, 0:CTX], reads=[ts])
                            n += 1
                P.barrier()

        def normalize_out(pO, t_pO, orow, srow, ncols, extra, dst, den, t_den, so_, t_so_, oc=None, t_oc=None):
            P.op("dve", lambda e: e.tensor_copy(out=den[orow, 0:ncols], in_=pO[srow, 0:ncols]), reads=[t_pO], writes=[t_den])
            if oc is not None:
                P.op("dve", lambda e: e.tensor_copy(out=oc[orow, 0:ncols], in_=pO[orow, 0:ncols]), reads=[t_pO], writes=[t_oc])
            if extra is not None:
                P.op("dve", lambda e: e.tensor_scalar(out=den[orow, 0:ncols], in0=den[orow, 0:ncols], scalar1=extra[orow, :],
                                                      scalar2=None, op0=ALU.add), reads=[t_den, t_esink], writes=[t_den])
            P.op("act", lambda e: e.activation(out=den[orow, 0:ncols], in_=den[orow, 0:ncols], func=AF.Ln), reads=[t_den], writes=[t_den])
            P.op("act", lambda e: e.activation(out=den[orow, 0:ncols], in_=den[orow, 0:ncols], func=AF.Exp, scale=-1.0),
                 reads=[t_den], writes=[t_den])
            if oc is not None:
                P.op("dve", lambda e: e.tensor_tensor(out=so_[orow, 0:ncols], in0=oc[orow, 0:ncols], in1=den[orow, 0:ncols], op=ALU.mult),
                     reads=[t_oc, t_den], writes=[t_so_])
            else:
                P.op("dve", lambda e: e.tensor_tensor(out=so_[orow, 0:ncols], in0=pO[orow, 0:ncols], in1=den[orow, 0:ncols], op=ALU.mult),
                     reads=[t_pO, t_den], writes=[t_so_])
            P.dma("pool", dst, so_[orow, 0:ncols], reads=[t_so_])

        LO = slice(0, 64)
        HI = slice(64, 128)

        def run_pipeline(items, LA=2):
            n = len(items)
            for j in range(n + LA):
                if j < n:
                    items[j][0]()
                if j >= LA:
                    qk, ex, pv, post, pre = items[j - LA]
                    if pre is not None:
                        pre()
                    ex()
                    pv()
                    if post is not None:
                        post()

        def swa_phase(l, last):
            with contextlib.ExitStack() as ph:
                maskb = sb(ph, "maskb", [128, 6, 512], BF16)
                t_mk = Tok()
                P.dma("sp", maskb[:], maskbd, writes=[t_mk])
                pS = [ps(ph, "pS%d" % i, [128, 512]) for i in range(5)]
                t_pS = [PTok() for _ in range(5)]
                pO = [ps(ph, "pO%d" % i, [128, 512]) for i in range(2)]
                t_pO = [PTok() for _ in range(2)]
                PT = [sb(ph, "PT%d" % i, [128, 512], BF16) for i in range(5)]
                t_PT = [Tok() for _ in range(5)]
                den = [sb(ph, "den%d" % i, [128, 512], F32) for i in range(2)]
                t_den = [Tok() for _ in range(2)]
                so = [sb(ph, "so%d" % i, [128, 512], BF16) for i in range(2)]
                t_so = [Tok() for _ in range(2)]
                sq = sb(ph, "sq", [128, 2, T], BF16)
                sk = sb(ph, "sk", [128, T], BF16)
                Vx = sb(ph, "Vx", [128, 18, 2, 128], BF16)
                t_q, t_k, t_v = Tok(), Tok(), Tok()
                P.op("dve", lambda e: e.memset(Vx[:, :, 0, 64:128], 1.0), writes=[t_v])
                P.op("dve", lambda e: e.memset(Vx[:, :, 1, 0:64], 1.0), writes=[t_v])
                ns = 0
                no = 0
                for b in range(NB):
                    for i in range(2):
                        P.dma("sp", sq[:, i, :], sqd[b, i], writes=[t_q])
                    P.dma("sp", sk[:], skd[b], writes=[t_k])
                    svv = svd[b].rearrange("(blk p) f -> p blk f", p=128)
                    P.dma("sp", Vx[:, :, 0, 0:64], svv[:, :, 0:64], writes=[t_v])
                    P.dma("sp", Vx[:, :, 1, 64:128], svv[:, :, 64:128], writes=[t_v])
                    items = []
                    for g in range(2):
                        half = LO if g == 0 else HI
                        oth = HI if g == 0 else LO
                        for i in range(2):
                            head = 2 * g + i
                            esk = esink[:, l * 4 + head: l * 4 + head + 1]
                            groups = [] if last else [("ctx", 0)]
                            groups += [("lat", qg) for qg in range(4)]
                            for kind, qg in groups:
                                if kind == "ctx":
                                    ncols = CTX
                                    qsl = slice(0, CTX)
                                    keys = [(0, None), (1, None)]
                                else:
                                    ncols = 512
                                    qsl = slice(CTX + qg * 512, CTX + (qg + 1) * 512)
                                    keys = [(0, None), (1, None)]
                                    for r in range(-1, 5):
                                        kb = qg * 4 + r
                                        if 0 <= kb < 16:
                                            keys.append((2 + kb, r + 1))
                                k2 = no % 2
                                no += 1
                                o_, to_ = pO[k2], t_pO[k2]
                                dst = mixd[b, 768 + i * 128 + g * 64: 768 + i * 128 + (g + 1) * 64, qsl]
                                nk = len(keys)
                                for ki, (sblk, mi) in enumerate(keys):
                                    s_, ts_ = pS[ns % 5], t_pS[ns % 5]
                                    p_, tp_ = PT[ns % 5], t_PT[ns % 5]
                                    ns += 1

                                    def qk(s_=s_, ts_=ts_, sblk=sblk, qsl=qsl, i=i, half=half, mi=mi, ncols=ncols):
                                        P.op("pe", lambda e: e.matmul(
                                            s_[:, 0:ncols], sk[half, sblk * 128:(sblk + 1) * 128], sq[half, i, qsl],
                                            start=True, stop=(mi is None)), reads=[t_k, t_q], writes=[ts_])
                                        if mi is not None:
                                            P.op("pe", lambda e: e.matmul(
                                                s_[:], ident[:], maskb[:, mi, :], start=False, stop=True),
                                                reads=[t_const, t_mk], writes=[ts_])

                                    def ex(s_=s_, ts_=ts_, p_=p_, tp_=tp_, ncols=ncols):
                                        P.op("act", lambda e: e.activation(
                                            out=p_[:, 0:ncols], in_=s_[:, 0:ncols], func=AF.Exp, scale=0.125),
                                            reads=[ts_], writes=[tp_])

                                    def pv(o_=o_, to_=to_, p_=p_, tp_=tp_, sblk=sblk, g=g, ki=ki, nk=nk, ncols=ncols):
                                        P.op("pe", lambda e: e.matmul(
                                            o_[:, 0:ncols], Vx[:, sblk, g, :], p_[:, 0:ncols],
                                            start=(ki == 0), stop=(ki == nk - 1)), reads=[t_v, tp_], writes=[to_])

                                    post = None
                                    if ki == nk - 1:
                                        def post(o_=o_, to_=to_, half=half, oth=oth, ncols=ncols, esk=esk, dst=dst, k2=k2):
                                            normalize_out(o_, to_, half, oth, ncols, esk, dst, den[k2], t_den[k2],
                                                          so[k2], t_so[k2])
                                    items.append([qk, ex, pv, post, None])
                    run_pipeline(items, LA=4)
                P.barrier()

        def mla_phase(l, last):
            with contextlib.ExitStack() as ph:
                Wuq = sb(ph, "Wuq", [128, 2, 8, 2, 96], BF16)
                Wukv = sb(ph, "Wukv", [128, 8, 128], BF16)
                t_w = Tok()
                P.dma("pool", Wuq[:], wuqd[l], writes=[t_w])
                P.dma("pool", Wukv[:], wukvd[l], writes=[t_w])
                ropem = sb(ph, "ropem", [128, 2, SEQ], F32)
                t_rope = Tok()
                P.dma("sp", ropem[:], ropemd, writes=[t_rope])
                cqn = sb(ph, "cqn", [128, 2, T], BF16)
                ckvn = sb(ph, "ckvn", [128, T], BF16)
                Kpre = sb(ph, "Kpre", [96, T], F32)
                SQK = sb(ph, "SQK", [96, T], BF16)
                t_cqn, t_ckvn, t_kpre_hi, t_sqk_hi = Tok(), Tok(), Tok(), Tok()
                t_kpre_lo, t_sqk_lo = Tok(), Tok()
                Kh = [sb(ph, "Kh%d" % i, [96, T], BF16) for i in range(2)]
                Qh = [sb(ph, "Qh%d" % i, [96, T], BF16) for i in range(2)]
                Vx = [sb(ph, "Vx%d" % i, [128, 18, 128], BF16) for i in range(2)]
                t_Kh = [Tok(), Tok()]
                t_Qh = [Tok(), Tok()]
                t_Vx = [Tok(), Tok()]
                P.op("dve", lambda e: e.memset(Vx[0][:, :, 64:128], 1.0), writes=[t_Vx[0]])
                P.op("dve", lambda e: e.memset(Vx[1][:, :, 0:64], 1.0), writes=[t_Vx[1]])
                pS = [ps(ph, "pS%d" % i, [128, 512]) for i in range(4)]
                t_pS = [PTok() for _ in range(4)]
                pO = [ps(ph, "pO%d" % i, [128, 512]) for i in range(1)]
                t_pO = [PTok() for _ in range(1)]
                pM = [ps(ph, "pM%d" % i, [128, 512]) for i in range(3)]
                t_pM = [PTok() for _ in range(3)]
                PT = [sb(ph, "PT%d" % i, [128, 512], BF16) for i in range(5)]
                t_PT = [Tok() for _ in range(5)]
                den = [sb(ph, "den%d" % i, [128, 512], F32) for i in range(2)]
                t_den = [Tok() for _ in range(2)]
                so = [sb(ph, "so%d" % i, [128, 512], BF16) for i in range(2)]
                t_so = [Tok() for _ in range(2)]
                oc = [sb(ph, "oc%d" % i, [128, 512], F32) for i in range(2)]
                t_oc = [Tok() for _ in range(2)]
                sqt = [sb(ph, "sqt%d" % i, [96, 512], BF16) for i in range(2)]
                t_sqt = [Tok() for _ in range(2)]
                rtt = [sb(ph, "rtt%d" % i, [96, 512], F32) for i in range(2)]
                t_rtt = [Tok() for _ in range(2)]
                rsd = [sb(ph, "rsd%d" % i, [96, 512], F32) for i in range(2)]
                t_rsd = [Tok() for _ in range(2)]
                t1s = [sb(ph, "t1s%d" % i, [96, 512], F32) for i in range(2)]
                t2s = [sb(ph, "t2s%d" % i, [96, 512], F32) for i in range(2)]
                t_t1 = [Tok() for _ in range(2)]
                t_t2 = [Tok() for _ in range(2)]
                cn = {"m": 0, "r": 0, "s": 0, "o": 0, "k": 0}
                tchunks = [(0, CTX)] + [(CTX + i * 512, 512) for i in range(4)]
                scale = 96.0 ** -0.5

                def next_pm():
                    i = cn["m"] % 3
                    cn["m"] += 1
                    return pM[i], t_pM[i]

                kraw = [sb(ph, "kraw%d" % i, [96, 512], F32) for i in range(2)]
                t_kraw = [Tok() for _ in range(2)]
                lnv = [sb(ph, "lnv%d" % i, [96, 512], F32) for i in range(2)]
                t_lnv = [Tok() for _ in range(2)]

                def prep_steps(b, h):
                    hb = h % 2
                    K_, tK = Kh[hb], t_Kh[hb]
                    Q_, tQ = Qh[hb], t_Qh[hb]
                    V_, tV = Vx[hb], t_Vx[hb]
                    gk = vcol(l, V_GMK)
                    gq = vcol(l, V_GMQ)
                    gqs = vcol(l, V_GMQ + 1)
                    steps = []
                    kchains, qchains, vsteps = [], [], []

                    def rstd_steps(rhs_fn, t_rhs, n, box):
                        def s3():
                            pm_, tm_ = next_pm()
                            box["pm"] = (pm_, tm_)
                            P.op("pe", lambda e: e.matmul(pm_[0:96, 0:n], ones[0:96, 0:96], rhs_fn(), start=True, stop=True),
                                 reads=list(t_rhs) + [t_const], writes=[tm_])

                        def s4():
                            pm_, tm_ = box["pm"]
                            j = cn["r"] % 2
                            cn["r"] += 1
                            box["j"] = j
                            P.op("act", lambda e: e.activation(out=lnv[j][:, 0:n], in_=pm_[0:96, 0:n], func=AF.Ln,
                                                               bias=EPS, scale=1.0 / 96.0), reads=[tm_], writes=[t_lnv[j]])

                        def s5():
                            j = box["j"]
                            P.op("act", lambda e: e.activation(out=rsd[j][:, 0:n], in_=lnv[j][:, 0:n], func=AF.Exp, scale=-0.5),
                                 reads=[t_lnv[j]], writes=[t_rsd[j]])
                            box["rs"] = (rsd[j], t_rsd[j])
                        return [s3, s4, s5]

                    for (t0, n) in tchunks:
                        cs = slice(t0, t0 + n)
                        box = {}

                        def k1(box=box, cs=cs, n=n):
                            pm_, tm_ = next_pm()
                            box["a"] = (pm_, tm_)
                            P.op("pe", lambda e: e.matmul(pm_[0:64, 0:n], Wukv[:, h, 0:64], ckvn[:, cs], start=True, stop=True),
                                 reads=[t_w, t_ckvn], writes=[tm_])

                        def k2(box=box, cs=cs, n=n):
                            pm_, tm_ = box["a"]
                            jr = cn["k"] % 2
                            cn["k"] += 1
                            kr_, tkr_ = kraw[jr], t_kraw[jr]
                            P.op("dve", lambda e: e.tensor_copy(out=kr_[0:64, 0:n], in_=pm_[0:64, 0:n]), reads=[tm_], writes=[tkr_])
                            P.op("pool", lambda e: e.tensor_tensor(out=SQK[0:64, cs], in0=kr_[0:64, 0:n], in1=kr_[0:64, 0:n], op=ALU.mult),
                                 reads=[tkr_], writes=[t_sqk_lo])
                            P.op("dve", lambda e: e.tensor_scalar(out=Kpre[0:64, cs], in0=kr_[0:64, 0:n], scalar1=gk[0:64, :],
                                                                  scalar2=None, op0=ALU.mult), reads=[tkr_, t_vecs], writes=[t_kpre_lo])

                        def k6(box=box, cs=cs, n=n):
                            rs, trs = box["rs"]
                            P.op("dve", lambda e: e.tensor_tensor(out=K_[:, cs], in0=Kpre[:, cs], in1=rs[:, 0:n], op=ALU.mult),
                                 reads=[t_kpre_lo, t_kpre_hi, trs], writes=[tK])
                        kchains.append([k1, k2] + rstd_steps(lambda cs=cs: SQK[0:96, cs], [t_sqk_lo, t_sqk_hi], n, box) + [k6])
                    vcols = slice(0, 64) if hb == 0 else slice(64, 128)
                    for (s0, s1) in ((0, 8), (8, 16), (16, 18)):
                        box = {}

                        def v1(box=box, s0=s0, s1=s1):
                            pm_, tm_ = next_pm()
                            box["a"] = (pm_, tm_)
                            for sblk in range(s0, s1):
                                P.op("pe", lambda e, sblk=sblk: e.matmul(
                                    pm_[:, (sblk - s0) * 64:(sblk - s0 + 1) * 64], ckvn[:, sblk * 128:(sblk + 1) * 128],
                                    Wukv[:, h, 64:128], start=True, stop=True), reads=[t_w, t_ckvn], writes=[tm_])

                        def v2(box=box, s0=s0, s1=s1):
                            pm_, tm_ = box["a"]
                            ns_ = s1 - s0
                            P.op("dve", lambda e: e.tensor_copy(
                                out=V_[:, s0:s1, vcols], in_=pm_[:, 0:ns_ * 64].rearrange("p (s f) -> p s f", f=64)),
                                reads=[tm_], writes=[tV])
                        vsteps += [v1, v2]
                    for (t0, n) in tchunks:
                        if last and t0 == 0:
                            continue
                        cs = slice(t0, t0 + n)
                        lat = t0 >= CTX
                        psl = slice(t0 - CTX, t0 - CTX + n)
                        box = {}

                        def q1(box=box, cs=cs, n=n, lat=lat):
                            pq_, tq_ = next_pm()
                            box["a"] = (pq_, tq_)
                            for j in range(2):
                                P.op("pe", lambda e, j=j: e.matmul(
                                    pq_[0:96, 0:n], Wuq[:, j, h, 0, :], cqn[:, j, cs], start=(j == 0), stop=(j == 1)),
                                    reads=[t_w, t_cqn], writes=[tq_])
                            if lat:
                                px_, tx_ = next_pm()
                                box["c"] = (px_, tx_)
                                for j in range(2):
                                    P.op("pe", lambda e, j=j: e.matmul(
                                        px_[0:96, 0:n], Wuq[:, j, h, 1, :], cqn[:, j, cs], start=(j == 0), stop=(j == 1)),
                                        reads=[t_w, t_cqn], writes=[tx_])

                        def q2(box=box, n=n, lat=lat, psl=psl):
                            pq_, tq_ = box["a"]
                            jr = cn["k"] % 2
                            cn["k"] += 1
                            box["jr"] = jr
                            qr_, tqr_ = kraw[jr], t_kraw[jr]
                            P.op("dve", lambda e: e.tensor_copy(out=qr_[0:96, 0:n], in_=pq_[0:96, 0:n]), reads=[tq_], writes=[tqr_])
                            P.op("pool", lambda e: e.tensor_tensor(out=sqt[jr][:, 0:n], in0=qr_[0:96, 0:n], in1=qr_[0:96, 0:n], op=ALU.mult),
                                 reads=[tqr_], writes=[t_sqt[jr]])
                            if lat:
                                px_, tx_ = box["c"]
                                P.op("dve", lambda e: e.scalar_tensor_tensor(
                                    out=t2s[jr][64:96, 0:n], in0=px_[64:96, 0:n], scalar=gqs[64:96, :], in1=ropem[64:96, 1, psl],
                                    op0=ALU.mult, op1=ALU.mult), reads=[tx_, t_vecs, t_rope], writes=[t_t2[jr]])

                        def q6(box=box, cs=cs, n=n, lat=lat, psl=psl):
                            rs, trs = box["rs"]
                            jr = box["jr"]
                            qr_, tqr_ = kraw[jr], t_kraw[jr]
                            if lat:
                                P.op("dve", lambda e: e.scalar_tensor_tensor(
                                    out=Q_[0:64, cs], in0=qr_[0:64, 0:n], scalar=gq[0:64, :], in1=rs[0:64, 0:n],
                                    op0=ALU.mult, op1=ALU.mult), reads=[tqr_, trs, t_vecs], writes=[tQ])
                                P.op("dve", lambda e: e.scalar_tensor_tensor(
                                    out=t1s[jr][64:96, 0:n], in0=qr_[64:96, 0:n], scalar=gq[64:96, :], in1=ropem[64:96, 0, psl],
                                    op0=ALU.mult, op1=ALU.mult), reads=[tqr_, t_vecs, t_rope], writes=[t_t1[jr]])
                                P.op("pool", lambda e: e.tensor_tensor(
                                    out=t1s[jr][64:96, 0:n], in0=t1s[jr][64:96, 0:n], in1=t2s[jr][64:96, 0:n], op=ALU.add),
                                    reads=[t_t1[jr], t_t2[jr]], writes=[t_t1[jr]])
                                P.op("dve", lambda e: e.tensor_tensor(
                                    out=Q_[64:96, cs], in0=t1s[jr][64:96, 0:n], in1=rs[64:96, 0:n], op=ALU.mult),
                                    reads=[t_t1[jr], trs], writes=[tQ])
                            else:
                                P.op("dve", lambda e: e.scalar_tensor_tensor(
                                    out=Q_[:, cs], in0=qr_[0:96, 0:n], scalar=gq[0:96, :], in1=rs[:, 0:n],
                                    op0=ALU.mult, op1=ALU.mult), reads=[tqr_, trs, t_vecs], writes=[tQ])
                        qchains.append([q1, q2] + rstd_steps(lambda box=box, n=n: sqt[box["jr"]][:, 0:n], [t_sqt[0], t_sqt[1]], n, box) + [q6])
                    while kchains or qchains:
                        kc = kchains.pop(0) if kchains else []
                        qc = qchains.pop(0) if qchains else []
                        for i_ in range(max(len(kc), len(qc))):
                            if i_ < len(kc):
                                steps.append(kc[i_])
                            if i_ < len(qc):
                                steps.append(qc[i_])
                    steps += vsteps
                    return steps

                def head_items(b, h):
                    hb = h % 2
                    K_, tK = Kh[hb], t_Kh[hb]
                    Q_, tQ = Qh[hb], t_Qh[hb]
                    V_, tV = Vx[hb], t_Vx[hb]
                    orow = LO if hb == 0 else HI
                    srow = HI if hb == 0 else LO
                    groups = [] if last else [(0, CTX, 2)]
                    groups += [(CTX + i * 512, 512, 18) for i in range(4)]
                    items = []
                    for (t0, n, nk) in groups:
                        k2 = cn["o"] % 2
                        cn["o"] += 1
                        o_, to_ = pO[0], t_pO[0]
                        dst = mixd[b, 256 + h * 64: 256 + (h + 1) * 64, t0:t0 + n]
                        for sblk in range(nk):
                            i3 = cn["s3"] % 4
                            cn["s3"] += 1
                            s_, ts_ = pS[i3], t_pS[i3]
                            p_, tp_ = PT[i3], t_PT[i3]

                            def qk(s_=s_, ts_=ts_, sblk=sblk, t0=t0, n=n):
                                P.op("pe", lambda e: e.matmul(
                                    s_[:, 0:n], K_[:, sblk * 128:(sblk + 1) * 128], Q_[:, t0:t0 + n], start=True, stop=True),
                                    reads=[tK, tQ], writes=[ts_])

                            def ex(s_=s_, ts_=ts_, p_=p_, tp_=tp_, n=n):
                                P.op("act", lambda e: e.activation(out=p_[:, 0:n], in_=s_[:, 0:n], func=AF.Exp, scale=scale),
                                     reads=[ts_], writes=[tp_])

                            def pv(o_=o_, to_=to_, p_=p_, tp_=tp_, sblk=sblk, n=n, nk=nk):
                                P.op("pe", lambda e: e.matmul(
                                    o_[:, 0:n], V_[:, sblk, :], p_[:, 0:n], start=(sblk == 0), stop=(sblk == nk - 1)),
                                    reads=[tV, tp_], writes=[to_])

                            post = None
                            if sblk == nk - 1:
                                def post(o_=o_, to_=to_, n=n, dst=dst, k2=k2):
                                    normalize_out(o_, to_, orow, srow, n, None, dst, den[k2], t_den[k2], so[k2], t_so[k2], oc[k2], t_oc[k2])
                            items.append([qk, ex, pv, post, None])
                    return items

                cn["s3"] = 0
                for b in range(NB):
                    P.dma("sp", cqn[:], cqnd[b].rearrange("(j p) t -> p j t", p=128), writes=[t_cqn])
                    P.dma("sp", ckvn[:], ckvnd[b], writes=[t_ckvn])
                    P.dma("sp", Kpre[64:96, :], krrd[b], writes=[t_kpre_hi])
                    P.dma("sp", SQK[64:96, :], krsqd[b], writes=[t_sqk_hi])
                    for st_ in prep_steps(b, 0):
                        st_()
                    items = []
                    for h in range(8):
                        its = head_items(b, h)
                        if h + 1 < 8:
                            steps = prep_steps(b, h + 1)
                            per = [[] for _ in its]
                            for si, st_ in enumerate(steps):
                                per[min(si, len(its) - 1)].append(st_)
                            for it, sts in zip(its, per):
                                if sts:
                                    it[4] = (lambda sts=sts: [f() for f in sts])
                        items += its
                    run_pipeline(items, LA=3)
                P.barrier()

        def outproj_phase(l, last):
            with contextlib.ExitStack() as ph:
                Wout = sb(ph, "Wout", [128, 8, D], BF16)
                t_w = Tok()
                P.dma("pool", Wout[:], woutd[l], writes=[t_w])
                hc = [sb(ph, "hc%d" % i, [128, 8, CH], F32) for i in range(3)]
                t_hc = [Tok() for _ in range(3)]
                mx = [sb(ph, "mx%d" % i, [128, 8, CH], BF16) for i in range(3)]
                t_mx = [Tok() for _ in range(3)]
                py = [ps(ph, "py%d" % i, [128, 2, CH]) for i in range(8)]
                t_py = [PTok() for _ in range(8)]
                i = 0
                n = 0
                for b in range(NB):
                    for c in range(NCH):
                        if last and c == 0:
                            continue
                        v = vec_of(b, c)
                        tsl = slice(c * CH, (c + 1) * CH)
                        h_, th_ = hc[i % 3], t_hc[i % 3]
                        m_, tm_ = mx[i % 3], t_mx[i % 3]
                        i += 1
                        P.dma("sp", h_[:], hd[b, :, tsl].rearrange("(k p) t -> p k t", p=128), writes=[th_])
                        P.dma("sp", m_[:], mixd[b, :, tsl].rearrange("(k p) t -> p k t", p=128), writes=[tm_])
                        for dk in range(8):
                            yy = py[n % 8][:, 0, :]
                            ty = t_py[n % 8]
                            n += 1
                            for k in range(8):
                                P.op("pe", lambda e, yy=yy, k=k, dk=dk, m_=m_: e.matmul(
                                    yy, Wout[:, k, dk * 128:(dk + 1) * 128], m_[:, k, :], start=(k == 0), stop=(k == 7)),
                                    reads=[t_w, tm_], writes=[ty])
                            P.op("dve", lambda e, yy=yy, dk=dk, h_=h_, v=v: e.scalar_tensor_tensor(
                                out=h_[:, dk, :], in0=yy, scalar=mod[:, l, v, 40 + dk: 41 + dk], in1=h_[:, dk, :],
                                op0=ALU.mult, op1=ALU.add), reads=[ty, th_, t_mod], writes=[th_])
                        P.dma("pool", hd[b, :, tsl].rearrange("(k p) t -> p k t", p=128), h_[:], reads=[th_])
                P.barrier()

        def want(l, k):
            return stop_after is None or (l * 7 + k) <= stop_after

        wstack = contextlib.ExitStack()
        X = Y = None
        if want(0, 0):
            wstack = contextlib.ExitStack()
            X, Y = WSet(wstack, "x0"), WSet(wstack, "y0")
            load_w(X, 0, 0, 0)
            ffn_pass(0, 0, 0, X, True, False, prefetch=lambda: load_w(Y, 0, 0, 1))
            ffn_pass(0, 0, 1, Y, True, False)
            wstack.close()
        for l in range(n_layers):
            last = (l == L - 1)
            if want(l, 1):
                mixin_phase(l)
            if want(l, 2):
                fourier_phase(l, last)
            if want(l, 3):
                swa_phase(l, last)
            if want(l, 4):
                mla_phase(l, last)
            if want(l, 5):
                wstack = contextlib.ExitStack()
                X, Y = WSet(wstack, "x%d" % (l + 1)), WSet(wstack, "y%d" % (l + 1))
                if want(l, 6):
                    load_w(X, l, 1, 0)
                outproj_phase(l, last)
                if want(l, 6):
                    nxt_ffn1 = (l + 1 < n_layers) and want(l + 1, 0)
                    ffn_pass(l, 1, 0, X, False, last, prefetch=lambda: load_w(Y, l, 1, 1))
                    ffn_pass(l, 1, 1, Y, False, last,
                             prefetch=(lambda: load_w(X, l + 1, 0, 0)) if nxt_ffn1 else None)
                    if nxt_ffn1:
                        ffn_pass(l + 1, 0, 0, X, False, False, prefetch=lambda: load_w(Y, l + 1, 0, 1))
                        ffn_pass(l + 1, 0, 1, Y, False, False)
                wstack.close()
        P.barrier()
        nc._n_emitted = P.ninstr
    return nc


_CACHE = {}


def kernel(**inputs):
    n_cores = 8
    shared = _prep_shared(inputs)
    in_maps = []
    for core in range(n_cores):
        m = dict(shared)
        m.update(_prep_core(inputs, core))
        in_maps.append(m)
    if "nc" not in _CACHE:
        _CACHE["nc"] = build_nc()
    nc = _CACHE["nc"]
    res = run_bass_kernel_spmd(nc, in_maps, core_ids=list(range(n_cores)))
    out = np.empty((n_cores * NB, SEQ, D), np.float32)
    for core in range(n_cores):
        o = res.results[core]["out"]
        for i in range(NB):
            out[core * NB + i] = o[i].T
    return out
```

```python
import contextlib
import os
_EXP = os.environ.get("KEXP", "")
import numpy as np
import ml_dtypes
import concourse.bass as bass
import concourse.mybir as mybir
from concourse.bass_utils import run_bass_kernel_spmd

F32 = mybir.dt.float32
BF16 = mybir.dt.bfloat16
ALU = mybir.AluOpType
AF = mybir.ActivationFunctionType

L = 4
D = 1024
NB = 2
CTX = 256
SEQ = 2048
T = CTX + SEQ
DFF = 2816
NFC = DFF // 128
CH = 256
NCH = T // CH
EPS = 1e-6
NEGV = -30000.0

WC_F = 0
WC_CQ = 256
WC_CKV = 512
WC_KRC = 640
WC_KRS = 736
WC_SQA = 832
WC_SQB = 960
WC_SQAS = 1088
WC_SQBS = 1216
WC_SK = 1344
WC_SKS = 1472
WC_SV = 1600
NWC = 1728

V_GF1 = 0
V_GMIX = 8
V_GF2 = 16
V_GCQ = 24
V_GCKV = 26
V_GMK = 27
V_GMQ = 29
V_GSQ = 31
V_GSK = 33
V_SINK = 35
V_BADA = 39
VL = 111


class Tok:
    __slots__ = ("w", "r", "excl")

    def __init__(self, excl=False):
        self.w = None
        self.r = {}
        self.excl = excl


class Prog:
    def __init__(self, nc, stack):
        self.nc = nc
        self.eng = {"pe": nc.tensor, "act": nc.scalar, "dve": nc.vector, "pool": nc.gpsimd, "sp": nc.sync}
        self.sem = {}
        self.total = {}
        for e in self.eng:
            self.sem[e] = stack.enter_context(nc.semaphore("s_" + e))
            self.total[e] = 0
        self.dq = {"sp": [], "pool": []}
        for q in self.dq:
            for i in range(14):
                n = "d_%s%d" % (q, i)
                self.sem[n] = stack.enter_context(nc.semaphore(n))
                self.total[n] = 0
                self.dq[q].append(n)
        self.dqi = {"sp": 0, "pool": 0}
        self.known = {e: {} for e in self.eng}
        self.ninstr = 0

    def wait(self, e, ev):
        s, v = ev
        if self.known[e].get(s, 0) >= v:
            return
        self.eng[e].wait_ge(self.sem[s], v)
        self.known[e][s] = v
        self.ninstr += 1

    def _deps(self, e, reads, writes):
        evs = {}
        for t in reads:
            if t.w is not None:
                s, v = t.w
                if v > evs.get(s, 0):
                    evs[s] = v
            if t.excl:
                for s, v in t.r.items():
                    if s != e and v > evs.get(s, 0):
                        evs[s] = v
        for t in writes:
            if t.w is not None:
                s, v = t.w
                if s != e and v > evs.get(s, 0):
                    evs[s] = v
            for s, v in t.r.items():
                if s != e and v > evs.get(s, 0):
                    evs[s] = v
        if e == "pe":
            evs.pop("pe", None)
        for s, v in evs.items():
            self.wait(e, (s, v))

    def _mark(self, ev, reads, writes):
        s, v = ev
        for t in reads:
            if v > t.r.get(s, 0):
                t.r[s] = v
        for t in writes:
            t.w = ev
            t.r = {}

    def op(self, e, ins_fn, reads=(), writes=()):
        self._deps(e, reads, writes)
        ins = ins_fn(self.eng[e])
        ins.then_inc(self.sem[e], 1)
        self.total[e] += 1
        self.ninstr += 1
        self._mark((e, self.total[e]), reads, writes)

    def dma(self, q, out, in_, reads=(), writes=()):
        sems = self.dq[q]
        s = sems[self.dqi[q] % len(sems)]
        self.dqi[q] += 1
        if self.total[s] > 0:
            self.wait(q, (s, self.total[s]))
        self._deps(q, reads, writes)
        self.eng[q].dma_start(out=out, in_=in_).then_inc(self.sem[s], 16)
        self.total[s] += 16
        self.ninstr += 1
        self._mark((s, self.total[s]), reads, writes)

    def barrier(self):
        for e in self.eng:
            for s, v in self.total.items():
                if v > 0 and s != e:
                    self.wait(e, (s, v))
            if self.total[e] > 0:
                self.wait(e, (e, self.total[e]))


def _rope_tables():
    pos = np.arange(SEQ)
    row = (pos // 64).astype(np.float32)
    col = (pos % 64).astype(np.float32)

    def tab(dim):
        axis_dim = dim // 2
        inv = (10000.0 ** (-np.arange(0, axis_dim, 2, dtype=np.float32) / axis_dim)).astype(np.float32)
        ang = np.concatenate([row[:, None] * inv, col[:, None] * inv], axis=-1).astype(np.float32)
        c = np.cos(ang).astype(np.float32)
        s = np.sin(ang).astype(np.float32)
        C = np.repeat(c, 2, axis=1).T
        S = np.repeat(s, 2, axis=1).T.copy()
        S[0::2] *= -1.0
        return C, S

    Cm, Sm = tab(32)
    Cs, Ss = tab(64)
    ropem = np.zeros((128, 2, SEQ), np.float32)
    ropem[64:96, 0] = Cm
    ropem[64:96, 1] = Sm
    ropes = np.zeros((128, 2, SEQ), np.float32)
    ropes[0:64, 0] = Cs
    ropes[64:128, 0] = Cs
    ropes[0:64, 1] = Ss
    ropes[64:128, 1] = Ss
    return ropem, ropes


def _dft_tables():
    bf = ml_dtypes.bfloat16
    c = np.arange(64)
    ang = 2.0 * np.pi * np.outer(c, c) / 64.0
    Cc = np.cos(ang) / 8.0
    Sc = np.sin(ang) / 8.0
    dftc = np.zeros((128, 256), np.float64)
    for g in range(2):
        dftc[g * 64:(g + 1) * 64, g * 64:(g + 1) * 64] = Cc
        dftc[g * 64:(g + 1) * 64, 128 + g * 64:128 + (g + 1) * 64] = -Sc
    p = np.arange(SEQ)
    kp = np.outer(p, p) % SEQ
    ang = 2.0 * np.pi * kp / SEQ
    sc = 1.0 / np.sqrt(SEQ)
    Cn = (np.cos(ang) * sc).astype(np.float32)
    Sn = (np.sin(ang) * sc).astype(np.float32)
    dftn = np.empty((4, 128, 2, 16, 512), np.float32)
    for i, X in enumerate((Cn, Sn)):
        dftn[:, :, i] = X.reshape(16, 128, 4, 512).transpose(2, 1, 0, 3)
    p = np.arange(CTX)
    ang = 2.0 * np.pi * (np.outer(p, p) % CTX) / CTX
    dftx = np.empty((128, 2, 2, CTX), np.float32)
    for i, X in enumerate((np.cos(ang) / 16.0, np.sin(ang) / 16.0)):
        dftx[:, i] = X.reshape(2, 128, CTX).transpose(1, 0, 2)
    maskb = np.full((128, 6, 4, 128), NEGV, np.float32)
    sl = np.arange(128)[:, None]
    tl = np.arange(128)[None, :]
    for ri, r in enumerate(range(-1, 5)):
        for qq in range(4):
            d = qq - r
            if d == 0:
                maskb[:, ri, qq] = 0.0
            elif d == -1:
                maskb[:, ri, qq] = np.where(tl >= sl, 0.0, NEGV)
            elif d == 1:
                maskb[:, ri, qq] = np.where(tl <= sl, 0.0, NEGV)
    ident = np.eye(128, dtype=np.float32)
    return (dftc.astype(np.float32).astype(bf), dftn.astype(bf), dftx.astype(bf),
            maskb.reshape(128, 6, 512).astype(bf), ident.astype(bf))


def _fm(v, nk):
    return np.ascontiguousarray(np.asarray(v, np.float32).reshape(nk, 128).T)


def _swap_pairs(a, axis=-1):
    a = np.moveaxis(a, axis, -1)
    sh = a.shape
    b = a.reshape(sh[:-1] + (sh[-1] // 2, 2))[..., ::-1].reshape(sh)
    return np.moveaxis(b, -1, axis)


def _prep_shared(inp):
    f32 = np.float32
    w_ada = np.asarray(inp["w_ada"], f32)
    sh = {}
    sh["wada"] = np.ascontiguousarray(w_ada.reshape(L, 8, 128, 9 * D).transpose(0, 2, 1, 3))
    w1 = np.stack([np.asarray(inp["w1_ffn1"], f32), np.asarray(inp["w1_ffn2"], f32)], 1)
    w3 = np.stack([np.asarray(inp["w3_ffn1"], f32), np.asarray(inp["w3_ffn2"], f32)], 1)
    w2 = np.stack([np.asarray(inp["w2_ffn1"], f32), np.asarray(inp["w2_ffn2"], f32)], 1)
    sh["w1"] = np.ascontiguousarray(w1.reshape(L, 2, 8, 128, DFF).transpose(0, 1, 3, 2, 4))
    sh["w3"] = np.ascontiguousarray(w3.reshape(L, 2, 8, 128, DFF).transpose(0, 1, 3, 2, 4))
    sh["w2"] = np.ascontiguousarray(w2.reshape(L, 2, NFC, 128, D).transpose(0, 1, 3, 2, 4))
    w_in = np.asarray(inp["w_in"], f32)
    fcols = w_in[:, :, 0:256]
    cq = w_in[:, :, 256:512]
    ckv = w_in[:, :, 512:640]
    kr = w_in[:, :, 640:672]
    sq = w_in[:, :, 672:928].reshape(L, D, 4, 64)
    sk = w_in[:, :, 928:1056]
    sv = w_in[:, :, 1056:1184]
    sqA = np.concatenate([sq[:, :, 0], sq[:, :, 2]], -1)
    sqB = np.concatenate([sq[:, :, 1], sq[:, :, 3]], -1)
    ext = np.concatenate([
        fcols, cq, ckv,
        ckv[:, :, 64:128], kr,
        ckv[:, :, 64:128], _swap_pairs(kr),
        sqA, sqB, _swap_pairs(sqA), _swap_pairs(sqB),
        sk, _swap_pairs(sk), sv], -1)
    assert ext.shape[-1] == NWC
    sh["win"] = np.ascontiguousarray(ext.reshape(L, 8, 128, NWC).transpose(0, 2, 1, 3))
    w_uq = np.asarray(inp["w_uq"], f32).reshape(L, 2, 128, 8, 96)
    w_uq_s = w_uq.copy()
    w_uq_s[..., 64:96] = _swap_pairs(w_uq[..., 64:96])
    sh["wuq"] = np.ascontiguousarray(np.stack([w_uq, w_uq_s], 4).transpose(0, 2, 1, 3, 4, 5))
    sh["wukv"] = np.ascontiguousarray(np.asarray(inp["w_ukv"], f32).reshape(L, 128, 8, 128))
    w_out = np.asarray(inp["w_out"], f32)
    rows = np.concatenate([np.arange(0, 768), np.arange(768, 832), np.arange(896, 960),
                           np.arange(832, 896), np.arange(960, 1024)])
    sh["wout"] = np.ascontiguousarray(w_out[:, rows].reshape(L, 8, 128, D).transpose(0, 2, 1, 3))
    vecs = np.zeros((128, L, VL), f32)
    for l in range(L):
        vecs[:, l, V_GF1:V_GF1 + 8] = _fm(inp["g_ffn1"][l], 8)
        vecs[:, l, V_GMIX:V_GMIX + 8] = _fm(inp["g_mix"][l], 8)
        vecs[:, l, V_GF2:V_GF2 + 8] = _fm(inp["g_ffn2"][l], 8)
        vecs[:, l, V_GCQ:V_GCQ + 2] = _fm(inp["g_cq"][l], 2)
        vecs[:, l, V_GCKV] = np.asarray(inp["g_ckv"][l], f32)
        for (col, name) in ((V_GMK, "g_mla_k"), (V_GMQ, "g_mla_q")):
            g = np.asarray(inp[name][l], f32)
            vecs[0:96, l, col] = g
            vecs[64:96, l, col + 1] = _swap_pairs(g[64:96])
        for (col, name) in ((V_GSQ, "g_swa_q"), (V_GSK, "g_swa_k")):
            g = np.asarray(inp[name][l], f32)
            vecs[:, l, col] = np.concatenate([g, g])
            vecs[:, l, col + 1] = np.concatenate([_swap_pairs(g), _swap_pairs(g)])
        vecs[:, l, V_SINK:V_SINK + 4] = np.asarray(inp["sink"][l], f32)[None, :]
        vecs[:, l, V_BADA:V_BADA + 72] = _fm(inp["b_ada"][l], 72)
    sh["vecs"] = vecs.reshape(128, L * VL)
    sh["ropem"], sh["ropes"] = _rope_tables()
    sh["dftc"], sh["dftn"], sh["dftx"], sh["maskb"], sh["ident"] = _dft_tables()
    return sh


def _prep_core(inp, core):
    f32 = np.float32
    b0 = core * NB
    x = np.asarray(inp["x"], f32)
    ctx = np.asarray(inp["ctx"], f32)
    hin = np.empty((NB, D, T), f32)
    for i in range(NB):
        hin[i, :, :CTX] = ctx[b0 + i].T
        hin[i, :, CTX:] = x[b0 + i].T
    c = np.asarray(inp["c"], f32)
    cv = np.stack([c[b0], c[b0 + 1], np.asarray(inp["c_ctx"], f32)], 0)
    cT = np.ascontiguousarray(cv.reshape(3, 8, 128).transpose(2, 1, 0))
    return {"hin": hin, "cT": cT}


def build_nc(n_layers=L, stop_after=None, debug=False):
    nc = bass.Bass("TRN2", target_bir_lowering=False)
    okind = "ExternalOutput" if debug else "Internal"

    def din(name, shape, dt=F32):
        return nc.dram_tensor(name, list(shape), dt, kind="ExternalInput").ap()

    def dscr(name, shape, dt):
        return nc.dram_tensor(name, list(shape), dt, kind=okind).ap()

    hin = din("hin", [NB, D, T])
    cT = din("cT", [128, 8, 3])
    wada = din("wada", [n_layers, 128, 8, 9 * D])
    w1d = din("w1", [n_layers, 2, 128, 8, DFF])
    w3d = din("w3", [n_layers, 2, 128, 8, DFF])
    w2d = din("w2", [n_layers, 2, 128, NFC, D])
    wind = din("win", [n_layers, 128, 8, NWC])
    wuqd = din("wuq", [n_layers, 128, 2, 8, 2, 96])
    wukvd = din("wukv", [n_layers, 128, 8, 128])
    woutd = din("wout", [n_layers, 128, 8, D])
    vecsd = din("vecs", [128, L * VL])
    ropemd = din("ropem", [128, 2, SEQ])
    ropesd = din("ropes", [128, 2, SEQ])
    dftcd = din("dftc", [128, 256], BF16)
    dftnd = din("dftn", [4, 128, 2, 16, 512], BF16)
    dftxd = din("dftx", [128, 2, 2, CTX], BF16)
    maskbd = din("maskb", [128, 6, 512], BF16)
    identd = din("ident", [128, 128], BF16)
    outd = nc.dram_tensor("out", [NB, D, SEQ], F32, kind="ExternalOutput").ap()

    hd = dscr("h", [NB, D, T], F32)
    find = dscr("fin", [NB, 256, T], BF16)
    cqnd = dscr("cqn", [NB, 256, T], BF16)
    ckvnd = dscr("ckvn", [NB, 128, T], BF16)
    krrd = dscr("krr", [NB, 32, T], F32)
    krsqd = dscr("krsq", [NB, 32, T], BF16)
    sqd = dscr("sq", [NB, 2, 128, T], BF16)
    skd = dscr("sk", [NB, 128, T], BF16)
    svd = dscr("sv", [NB, T, 128], BF16)
    mixd = dscr("mix", [NB, D, T], BF16)
    nscd = dscr("nsc", [NB, D, T], BF16)

    with contextlib.ExitStack() as top:
        P = Prog(nc, top)

        uid = [0]

        def sb(stack, name, shape, dt):
            uid[0] += 1
            return stack.enter_context(nc.sbuf_tensor("s%d_%s" % (uid[0], name), list(shape), dt))

        def ps(stack, name, shape):
            uid[0] += 1
            n = 1
            for d in shape[1:]:
                n *= d
            assert n <= 512
            if n < 512 and len(shape) == 2:
                shape = [shape[0], 512]
            assert n == 512 or len(shape) == 2, shape
            return stack.enter_context(nc.psum_tensor("p%d_%s" % (uid[0], name), list(shape), F32))

        def PTok():
            return Tok(excl=True)

        vecs = sb(top, "vecs", [128, L * VL], F32)
        mod = sb(top, "mod", [128, L, 3, 72], F32)
        gsv = sb(top, "gsv", [128, L, 3, 24], F32)
        hgv = sb(top, "hgv", [128, L, 3, 16], F32)
        esink = sb(top, "esink", [128, L * 4], F32)
        ones = sb(top, "ones", [128, 128], BF16)
        onesbd = sb(top, "onesbd", [128, 128], BF16)
        ident = sb(top, "ident", [128, 128], BF16)
        t_vecs, t_mod, t_gsv, t_hgv, t_esink, t_const = Tok(), Tok(), Tok(), Tok(), Tok(), Tok()

        P.dma("sp", vecs[:], vecsd, writes=[t_vecs])
        P.dma("sp", ident[:], identd, writes=[t_const])
        P.op("dve", lambda e: e.memset(ones[:], 1.0), writes=[t_const])
        P.op("dve", lambda e: e.memset(onesbd[:], 0.0), writes=[t_const])
        P.op("dve", lambda e: e.memset(onesbd[0:64, 0:64], 1.0), writes=[t_const])
        P.op("dve", lambda e: e.memset(onesbd[64:128, 64:128], 1.0), writes=[t_const])

        def vcol(l, c, n=1):
            return vecs[:, l * VL + c: l * VL + c + n]

        with contextlib.ExitStack() as ph:
            c32 = sb(ph, "c32", [128, 8, 3], F32)
            scb = sb(ph, "scb", [128, 8, 3], BF16)
            wa = [sb(ph, "wa%d" % i, [128, 8, D], BF16) for i in range(3)]
            pm = ps(ph, "pm", [128, 128, 4])
            t_c, t_sc, t_pm = Tok(), Tok(), PTok()
            t_wa = [Tok() for _ in range(3)]
            P.dma("sp", c32[:], cT, writes=[t_c])
            P.op("act", lambda e: e.activation(out=scb[:], in_=c32[:], func=AF.Silu), reads=[t_c], writes=[t_sc])
            gi = 0
            for l in range(n_layers):
                for j in range(9):
                    w = wa[gi % 3]
                    tw = t_wa[gi % 3]
                    gi += 1
                    P.dma("pool", w[:], wada[l, :, :, j * D:(j + 1) * D], writes=[tw])
                    for dk in range(8):
                        for k in range(8):
                            P.op("pe", lambda e, w=w, k=k, dk=dk, j=j: e.matmul(
                                pm[:, j * 8 + dk, 0:3], w[:, k, dk * 128:(dk + 1) * 128], scb[:, k, :],
                                start=(k == 0), stop=(k == 7)), reads=[tw, t_sc], writes=[t_pm])
                for v in range(3):
                    P.op("dve", lambda e, l=l, v=v: e.tensor_tensor(
                        out=mod[:, l, v, :], in0=pm[:, 0:72, v], in1=vcol(l, V_BADA, 72), op=ALU.add),
                        reads=[t_pm, t_vecs], writes=[t_mod])
                for v in range(3):
                    for si, (jj, gc) in enumerate(((1, V_GF1), (4, V_GMIX), (7, V_GF2))):
                        P.op("dve", lambda e, l=l, v=v, si=si, jj=jj, gc=gc: e.scalar_tensor_tensor(
                            out=gsv[:, l, v, si * 8:(si + 1) * 8], in0=mod[:, l, v, jj * 8:(jj + 1) * 8], scalar=1.0,
                            in1=vcol(l, gc, 8), op0=ALU.add, op1=ALU.mult), reads=[t_mod, t_vecs], writes=[t_gsv])
                    for si, jj in enumerate((2, 8)):
                        P.op("dve", lambda e, l=l, v=v, si=si, jj=jj: e.tensor_scalar(
                            out=hgv[:, l, v, si * 8:(si + 1) * 8], in0=mod[:, l, v, jj * 8:(jj + 1) * 8],
                            scalar1=0.5, scalar2=None, op0=ALU.mult), reads=[t_mod], writes=[t_hgv])
            P.op("act", lambda e: e.activation(out=esink[:].rearrange("p (l f) -> p l f", l=L),
                                               in_=vecs[:].rearrange("p (l f) -> p l f", l=L)[:, :, V_SINK:V_SINK + 4],
                                               func=AF.Exp), reads=[t_vecs], writes=[t_esink])
            P.barrier()

        def vec_of(b, c):
            return 2 if c == 0 else b

        class NormBufs:
            def __init__(self, stack, pssq, nhc=2):
                self.nhc = nhc
                self.hc = [sb(stack, "hc%d" % i, [128, 8, CH], F32) for i in range(nhc)]
                self.t_hc = [Tok() for _ in range(nhc)]
                self.sqb = sb(stack, "sqb", [128, 8, CH], BF16)
                self.t_sqb = Tok()
                self.tmp = sb(stack, "ntmp", [128, 8, CH], F32)
                self.t_tmp = Tok()
                self.nb = [sb(stack, "nb%d" % i, [128, 8, CH], BF16) for i in range(2)]
                self.t_nb = [Tok(), Tok()]
                self.rt = sb(stack, "nrt", [128, CH], F32)
                self.rstd = sb(stack, "nrstd", [128, CH], F32)
                self.t_rt, self.t_rstd = Tok(), Tok()
                self.pssq = pssq
                self.t_pssq = PTok()

        def norm_stage(NBf, i, src, l, v, gs_off, sh_j, preloaded=False):
            hc, t_hc = NBf.hc[i % NBf.nhc], NBf.t_hc[i % NBf.nhc]
            nb, t_nb = NBf.nb[i % 2], NBf.t_nb[i % 2]
            if not preloaded:
                P.dma("sp", hc[:], src.rearrange("(k p) t -> p k t", p=128), writes=[t_hc])
            P.op("act", lambda e: e.activation(out=NBf.sqb[:], in_=hc[:], func=AF.Square),
                 reads=[t_hc], writes=[NBf.t_sqb])
            for k in range(8):
                P.op("pe", lambda e, k=k: e.matmul(NBf.pssq[:, 0:CH], ones[:], NBf.sqb[:, k, :], start=(k == 0), stop=(k == 7)),
                     reads=[NBf.t_sqb, t_const], writes=[NBf.t_pssq])
            P.op("act", lambda e: e.activation(out=NBf.rt[:], in_=NBf.pssq[:, 0:CH], func=AF.Ln, bias=EPS, scale=1.0 / D),
                 reads=[NBf.t_pssq], writes=[NBf.t_rt])
            P.op("act", lambda e: e.activation(out=NBf.rstd[:], in_=NBf.rt[:], func=AF.Exp, scale=-0.5),
                 reads=[NBf.t_rt], writes=[NBf.t_rstd])
            gs_b = gsv[:, l, v, gs_off:gs_off + 8].unsqueeze(2).broadcast_to([128, 8, CH])
            sh_b = mod[:, l, v, sh_j * 8:(sh_j + 1) * 8].unsqueeze(2).broadcast_to([128, 8, CH])
            rs_b = NBf.rstd[:].unsqueeze(1).broadcast_to([128, 8, CH])
            P.op("dve", lambda e: e.tensor_tensor(out=NBf.tmp[:], in0=hc[:], in1=gs_b, op=ALU.mult),
                 reads=[t_hc, t_gsv], writes=[NBf.t_tmp])
            P.op("dve", lambda e: e.tensor_tensor(out=NBf.tmp[:], in0=NBf.tmp[:], in1=rs_b, op=ALU.mult),
                 reads=[NBf.t_tmp, NBf.t_rstd], writes=[NBf.t_tmp])
            P.op("dve" if "nbdve" in _EXP else "pool", lambda e: e.tensor_tensor(out=nb[:], in0=NBf.tmp[:], in1=sh_b, op=ALU.add),
                 reads=[NBf.t_tmp, t_mod], writes=[t_nb])
            return hc, t_hc, nb, t_nb

        HF = NFC // 2

        class WSet:
            def __init__(self, stack, tag):
                self.W1 = sb(stack, "W1" + tag, [128, 8, HF * 128], BF16)
                self.W3 = sb(stack, "W3" + tag, [128, 8, HF * 128], BF16)
                self.W2 = sb(stack, "W2" + tag, [128, HF, D], BF16)
                self.t13 = Tok()
                self.t2 = Tok()

        def load_w(ws, l, which, half):
            c0, c1 = half * HF * 128, (half + 1) * HF * 128
            P.dma("pool", ws.W1[:], w1d[l, which, :, :, c0:c1], writes=[ws.t13])
            P.dma("pool", ws.W3[:], w3d[l, which, :, :, c0:c1], writes=[ws.t13])
            P.dma("pool", ws.W2[:], w2d[l, which, :, half * HF:(half + 1) * HF, :], writes=[ws.t2])

        def ffn_pass(l, which, half, ws, first, last, prefetch=None):
            with contextlib.ExitStack() as ph:
                pssq = ps(ph, "pssq", [128, CH])
                ph13 = [ps(ph, "ph13_%d" % i, [128, 2, CH]) for i in range(3)]
                t_ph13 = [PTok() for _ in range(3)]
                py = [ps(ph, "py%d" % i, [128, 2, CH]) for i in range(4)]
                t_py = [PTok() for _ in range(4)]
                NBf = NormBufs(ph, pssq, nhc=3)
                g = sb(ph, "g", [128, HF, CH], BF16)
                t_g = Tok()
                sl = [sb(ph, "sl%d" % i, [128, CH], F32) for i in range(3)]
                t_sl = [Tok() for _ in range(3)]
                if prefetch is not None:
                    prefetch()

                chunks = [(b, c) for b in range(NB) for c in range(NCH) if not (last and which == 1 and c == 0)]
                gs_off = 0 if which == 0 else 16
                sh_j = 0 if which == 0 else 6
                src_t = hin if (first and half == 0) else hd
                state = {}

                def load_chunk(i):
                    b, c = chunks[i]
                    tsl = slice(c * CH, (c + 1) * CH)
                    hc, t_hc = NBf.hc[i % 3], NBf.t_hc[i % 3]
                    P.dma("sp", hc[:], src_t[b, :, tsl].rearrange("(k p) t -> p k t", p=128), writes=[t_hc])
                    if half == 1:
                        nb, t_nb = NBf.nb[i % 2], NBf.t_nb[i % 2]
                        P.dma("sp", nb[:], nscd[b, :, tsl].rearrange("(k p) t -> p k t", p=128), writes=[t_nb])

                def stage_a(i):
                    b, c = chunks[i]
                    tsl = slice(c * CH, (c + 1) * CH)
                    if half == 0:
                        st = norm_stage(NBf, i, None, l, vec_of(b, c), gs_off, sh_j, preloaded=True)
                        hc, t_hc, nb, t_nb = st
                        if "nost" not in _EXP:
                            P.dma("pool", nscd[b, :, tsl].rearrange("(k p) t -> p k t", p=128), nb[:], reads=[t_nb])
                    else:
                        st = (NBf.hc[i % 3], NBf.t_hc[i % 3], NBf.nb[i % 2], NBf.t_nb[i % 2])
                    state[i] = st

                def stage_b(i):
                    hc, t_hc, nb, t_nb = state[i]
                    for fc in range(HF):
                        pp, tp = ph13[fc % 3], t_ph13[fc % 3]
                        for wi, W in enumerate((ws.W1, ws.W3)):
                            for k in range(8):
                                P.op("pe", lambda e, W=W, wi=wi, k=k, fc=fc, pp=pp: e.matmul(
                                    pp[:, wi, :], W[:, k, fc * 128:(fc + 1) * 128], nb[:, k, :],
                                    start=(k == 0), stop=(k == 7)), reads=[ws.t13, t_nb], writes=[tp])
                        s_, ts_ = sl[fc % 3], t_sl[fc % 3]
                        P.op("act", lambda e, pp=pp, s_=s_: e.activation(out=s_[:], in_=pp[:, 0, :], func=AF.Silu),
                             reads=[tp], writes=[ts_])
                        P.op("dve", lambda e, pp=pp, s_=s_, fc=fc: e.tensor_tensor(
                            out=g[:, fc, :], in0=s_[:], in1=pp[:, 1, :], op=ALU.mult),
                            reads=[ts_, tp], writes=[t_g])

                def stage_c(i):
                    b, c = chunks[i]
                    hc, t_hc, nb, t_nb = state.pop(i)
                    v = vec_of(b, c)
                    for dk in range(8):
                        yy = py[dk // 2][:, dk % 2, :]
                        for fc in range(HF):
                            P.op("pe", lambda e, yy=yy, fc=fc, dk=dk: e.matmul(
                                yy, ws.W2[:, fc, dk * 128:(dk + 1) * 128], g[:, fc, :],
                                start=(fc == 0), stop=(fc == HF - 1)), reads=[ws.t2, t_g], writes=[t_py[dk // 2]])
                        P.op("dve", lambda e, yy=yy, dk=dk: e.scalar_tensor_tensor(
                            out=hc[:, dk, :], in0=yy, scalar=hgv[:, l, v, which * 8 + dk: which * 8 + dk + 1],
                            in1=hc[:, dk, :], op0=ALU.mult, op1=ALU.add), reads=[t_py[dk // 2], t_hc, t_hgv], writes=[t_hc])
                    if last and which == 1 and half == 1:
                        dst = outd[b, :, c * CH - CTX:(c + 1) * CH - CTX]
                    else:
                        dst = hd[b, :, c * CH:(c + 1) * CH]
                    if "nost" not in _EXP:
                        P.dma("pool", dst.rearrange("(k p) t -> p k t", p=128), hc[:], reads=[t_hc])

                n = len(chunks)
                load_chunk(0)
                if half == 0 and n > 1:
                    load_chunk(1)
                stage_a(0)
                for i in range(n):
                    if half == 0:
                        if i + 2 < n:
                            load_chunk(i + 2)
                    else:
                        if i + 1 < n:
                            load_chunk(i + 1)
                    if i + 1 < n:
                        stage_a(i + 1)
                    stage_b(i)
                    stage_c(i)
                P.barrier()

        def mixin_phase(l):
            with contextlib.ExitStack() as ph:
                Win = sb(ph, "Win", [128, 8, NWC], BF16)
                t_win = Tok()
                P.dma("pool", Win[:], wind[l], writes=[t_win])
                ropem = sb(ph, "ropem", [128, 2, SEQ], F32)
                ropes = sb(ph, "ropes", [128, 2, SEQ], F32)
                t_rope = Tok()
                P.dma("sp", ropem[:], ropemd, writes=[t_rope])
                P.dma("sp", ropes[:], ropesd, writes=[t_rope])
                pssq = ps(ph, "pssq", [128, CH])
                pj = [ps(ph, "pj%d" % i, [128, 2, CH]) for i in range(5)]
                t_pj = [PTok() for _ in range(5)]
                pq = [ps(ph, "pq%d" % i, [128, 512]) for i in range(2)]
                t_pq = [PTok() for _ in range(2)]
                NBf = NormBufs(ph, pssq)
                cnt = {"pj": 0, "pq": 0, "st": 0}
                stg = [sb(ph, "stg%d" % i, [128, 2, CH], BF16) for i in range(6)]
                t_stg = [Tok() for _ in range(6)]
                stf = [sb(ph, "stf%d" % i, [128, CH], F32) for i in range(2)]
                t_stf = [Tok() for _ in range(2)]
                sqs = [sb(ph, "sqs%d" % i, [128, 2, CH], BF16) for i in range(2)]
                t_sqs = [Tok() for _ in range(2)]
                rts = [sb(ph, "rts%d" % i, [128, CH], F32) for i in range(2)]
                t_rts = [Tok() for _ in range(2)]
                rsd = [sb(ph, "rsd%d" % i, [128, CH], F32) for i in range(2)]
                t_rsd = [Tok() for _ in range(2)]
                t1s = [sb(ph, "t1s%d" % i, [128, CH], F32) for i in range(2)]
                t2s = [sb(ph, "t2s%d" % i, [128, CH], F32) for i in range(2)]
                t_t1 = [Tok() for _ in range(2)]
                t_t2 = [Tok() for _ in range(2)]
                c2 = {"n": 0}

                def proj(nb, t_nb, col0, ncols, slot):
                    tile_, idx, tk = slot
                    for k in range(8):
                        P.op("pe", lambda e, k=k: e.matmul(tile_[0:ncols, idx, :], Win[:, k, col0:col0 + ncols], nb[:, k, :],
                                                           start=(k == 0), stop=(k == 7)), reads=[t_win, t_nb], writes=[tk])

                def next_pj():
                    i = cnt["pj"] % 5
                    cnt["pj"] += 1
                    return pj[i], t_pj[i]

                def next_pq():
                    i = cnt["pq"] % 2
                    cnt["pq"] += 1
                    return pq[i][:, 0:CH], t_pq[i]

                def next_stg():
                    i = cnt["st"] % 6
                    cnt["st"] += 1
                    return stg[i], t_stg[i]

                def small_rstd(src_ap, t_src, nparts, lhs_ones, nfeat, nslots):
                    j = c2["n"] % 2
                    c2["n"] += 1
                    sq_ = sqs[j]
                    P.op("act", lambda e: e.activation(out=sq_[0:nparts, 0:nslots, :], in_=src_ap, func=AF.Square),
                         reads=[t_src], writes=[t_sqs[j]])
                    pqa, tq = next_pq()
                    for s in range(nslots):
                        P.op("pe", lambda e, s=s: e.matmul(pqa[0:nparts, :], lhs_ones[0:nparts, 0:nparts], sq_[0:nparts, s, :],
                                                           start=(s == 0), stop=(s == nslots - 1)),
                             reads=[t_sqs[j], t_const], writes=[tq])
                    P.op("act", lambda e: e.activation(out=rts[j][0:nparts, :], in_=pqa[0:nparts, :], func=AF.Ln,
                                                       bias=EPS, scale=1.0 / nfeat), reads=[tq], writes=[t_rts[j]])
                    P.op("act", lambda e: e.activation(out=rsd[j][0:nparts, :], in_=rts[j][0:nparts, :], func=AF.Exp, scale=-0.5),
                         reads=[t_rts[j]], writes=[t_rsd[j]])
                    return rsd[j], t_rsd[j]

                tmpq = sb(ph, "tmpq", [128, 2, CH], F32)
                t_tmpq = Tok()
                bcs = [(b, c) for b in range(NB) for c in range(NCH)]
                nxt = norm_stage(NBf, 0, hd[bcs[0][0], :, bcs[0][1] * CH:(bcs[0][1] + 1) * CH], l, vec_of(*bcs[0]), 8, 3)
                for i, (b, c) in enumerate(bcs):
                    if True:
                        hc, t_hc, nb, t_nb = nxt
                        tsl = slice(c * CH, (c + 1) * CH)
                        lat = c > 0
                        psl = slice((c - 1) * CH, c * CH)
                        pt, tk = next_pj()
                        proj(nb, t_nb, WC_F, 128, (pt, 0, tk))
                        proj(nb, t_nb, WC_F + 128, 128, (pt, 1, tk))
                        st, ts = next_stg()
                        P.op("act", lambda e, st=st, pt=pt: e.activation(out=st[:], in_=pt[:], func=AF.Copy),
                             reads=[tk], writes=[ts])
                        P.dma("pool", find[b, :, tsl].rearrange("(j p) t -> p j t", p=128), st[:], reads=[ts])
                        pt, tk = next_pj()
                        proj(nb, t_nb, WC_CQ, 128, (pt, 0, tk))
                        proj(nb, t_nb, WC_CQ + 128, 128, (pt, 1, tk))
                        rs, trs = small_rstd(pt[:], tk, 128, ones, 256.0, 2)
                        st, ts = next_stg()
                        gq_b = vcol(l, V_GCQ, 2).unsqueeze(2).broadcast_to([128, 2, CH])
                        rs_b = rs[:].unsqueeze(1).broadcast_to([128, 2, CH])
                        P.op("dve", lambda e, pt=pt, gq_b=gq_b: e.tensor_tensor(out=tmpq[:], in0=pt[:], in1=gq_b, op=ALU.mult),
                             reads=[tk, t_vecs], writes=[t_tmpq])
                        P.op("dve", lambda e, st=st, rs_b=rs_b: e.tensor_tensor(out=st[:], in0=tmpq[:], in1=rs_b, op=ALU.mult),
                             reads=[t_tmpq, trs], writes=[ts])
                        P.dma("pool", cqnd[b, :, tsl].rearrange("(j p) t -> p j t", p=128), st[:], reads=[ts])
                        if i + 1 < len(bcs):
                            b2, c2_ = bcs[i + 1]
                            nxt = norm_stage(NBf, i + 1, hd[b2, :, c2_ * CH:(c2_ + 1) * CH], l, vec_of(b2, c2_), 8, 3)
                        pt, tk = next_pj()
                        proj(nb, t_nb, WC_CKV, 128, (pt, 0, tk))
                        rs, trs = small_rstd(pt[:, 0:1, :], tk, 128, ones, 128.0, 1)
                        st, ts = next_stg()
                        P.op("dve", lambda e, pt=pt, st=st, rs=rs: e.scalar_tensor_tensor(
                            out=st[:, 0, :], in0=pt[:, 0, :], scalar=vcol(l, V_GCKV), in1=rs[:], op0=ALU.mult, op1=ALU.mult),
                            reads=[tk, trs, t_vecs], writes=[ts])
                        P.dma("pool", ckvnd[b, :, tsl], st[:, 0, :], reads=[ts])
                        pt, tk = next_pj()
                        proj(nb, t_nb, WC_KRC, 96, (pt, 0, tk))
                        if lat:
                            proj(nb, t_nb, WC_KRS, 96, (pt, 1, tk))
                        st, ts = next_stg()
                        P.op("act", lambda e, pt=pt, st=st: e.activation(out=st[64:96, 0, :], in_=pt[64:96, 0, :], func=AF.Square),
                             reads=[tk], writes=[ts])
                        P.dma("pool", krsqd[b, :, tsl], st[64:96, 0, :], reads=[ts])
                        jf = cnt["st"] % 2
                        sf, tsf = stf[jf], t_stf[jf]
                        if lat:
                            j = c2["n"] % 2
                            c2["n"] += 1
                            P.op("dve", lambda e, pt=pt, j=j: e.scalar_tensor_tensor(
                                out=t1s[j][64:96, :], in0=pt[64:96, 0, :], scalar=vcol(l, V_GMK)[64:96, :],
                                in1=ropem[64:96, 0, psl], op0=ALU.mult, op1=ALU.mult),
                                reads=[tk, t_vecs, t_rope], writes=[t_t1[j]])
                            P.op("dve", lambda e, pt=pt, j=j: e.scalar_tensor_tensor(
                                out=t2s[j][64:96, :], in0=pt[64:96, 1, :], scalar=vcol(l, V_GMK + 1)[64:96, :],
                                in1=ropem[64:96, 1, psl], op0=ALU.mult, op1=ALU.mult),
                                reads=[tk, t_vecs, t_rope], writes=[t_t2[j]])
                            P.op("pool", lambda e, sf=sf, j=j: e.tensor_tensor(
                                out=sf[64:96, :], in0=t1s[j][64:96, :], in1=t2s[j][64:96, :], op=ALU.add),
                                reads=[t_t1[j], t_t2[j]], writes=[tsf])
                        else:
                            P.op("dve", lambda e, pt=pt, sf=sf: e.tensor_scalar(
                                out=sf[64:96, :], in0=pt[64:96, 0, :], scalar1=vcol(l, V_GMK)[64:96, :], scalar2=None,
                                op0=ALU.mult), reads=[tk, t_vecs], writes=[tsf])
                        P.dma("pool", krrd[b, :, tsl], sf[64:96, :], reads=[tsf])
                        for (col, cols, gcol, dst) in ((WC_SQA, WC_SQAS, V_GSQ, sqd[b, 0, :, tsl]),
                                                       (WC_SQB, WC_SQBS, V_GSQ, sqd[b, 1, :, tsl]),
                                                       (WC_SK, WC_SKS, V_GSK, skd[b, :, tsl])):
                            pt, tk = next_pj()
                            proj(nb, t_nb, col, 128, (pt, 0, tk))
                            if lat:
                                proj(nb, t_nb, cols, 128, (pt, 1, tk))
                            rs, trs = small_rstd(pt[:, 0:1, :], tk, 128, onesbd, 64.0, 1)
                            st, ts = next_stg()
                            if lat:
                                j = c2["n"] % 2
                                c2["n"] += 1
                                P.op("dve", lambda e, pt=pt, j=j, gcol=gcol: e.scalar_tensor_tensor(
                                    out=t1s[j][:], in0=pt[:, 0, :], scalar=vcol(l, gcol),
                                    in1=ropes[:, 0, psl], op0=ALU.mult, op1=ALU.mult),
                                    reads=[tk, t_vecs, t_rope], writes=[t_t1[j]])
                                P.op("dve", lambda e, pt=pt, j=j, gcol=gcol: e.scalar_tensor_tensor(
                                    out=t2s[j][:], in0=pt[:, 1, :], scalar=vcol(l, gcol + 1),
                                    in1=ropes[:, 1, psl], op0=ALU.mult, op1=ALU.mult),
                                    reads=[tk, t_vecs, t_rope], writes=[t_t2[j]])
                                P.op("pool", lambda e, j=j: e.tensor_tensor(
                                    out=t1s[j][:], in0=t1s[j][:], in1=t2s[j][:], op=ALU.add),
                                    reads=[t_t1[j], t_t2[j]], writes=[t_t1[j]])
                                P.op("dve", lambda e, st=st, j=j, rs=rs: e.tensor_tensor(
                                    out=st[:, 0, :], in0=t1s[j][:], in1=rs[:], op=ALU.mult),
                                    reads=[t_t1[j], trs], writes=[ts])
                            else:
                                P.op("dve", lambda e, pt=pt, st=st, rs=rs, gcol=gcol: e.scalar_tensor_tensor(
                                    out=st[:, 0, :], in0=pt[:, 0, :], scalar=vcol(l, gcol), in1=rs[:],
                                    op0=ALU.mult, op1=ALU.mult), reads=[tk, trs, t_vecs], writes=[ts])
                            P.dma("pool", dst, st[:, 0, :], reads=[ts])
                        pt, tk = next_pj()
                        for tb in range(2):
                            for k in range(8):
                                P.op("pe", lambda e, pt=pt, tb=tb, k=k: e.matmul(
                                    pt[:, 0, tb * 128:(tb + 1) * 128], nb[:, k, tb * 128:(tb + 1) * 128],
                                    Win[:, k, WC_SV:WC_SV + 128], start=(k == 0), stop=(k == 7)),
                                    reads=[t_win, t_nb], writes=[tk])
                        st, ts = next_stg()
                        P.op("act", lambda e, pt=pt, st=st: e.activation(out=st[:, 0, :], in_=pt[:, 0, :], func=AF.Copy),
                             reads=[tk], writes=[ts])
                        P.dma("pool", svd[b, tsl, :].rearrange("(tb p) f -> p tb f", p=128),
                              st[:, 0, :].rearrange("p (tb f) -> p tb f", tb=2), reads=[ts])
                P.barrier()

        def fourier_phase(l, last):
            with contextlib.ExitStack() as ph:
                dftc = sb(ph, "dftc", [128, 256], BF16)
                dftx = sb(ph, "dftx", [128, 2, 2, CTX], BF16)
                t_dc = Tok()
                P.dma("sp", dftc[:], dftcd, writes=[t_dc])
                P.dma("sp", dftx[:], dftxd, writes=[t_dc])
                fin = [sb(ph, "fin%d" % b, [128, 2, T], BF16) for b in range(NB)]
                t_fin = [Tok() for _ in range(NB)]
                A = [sb(ph, "A%d" % b, [128, 18, 2, 256], BF16) for b in range(NB)]
                t_A = [Tok() for _ in range(NB)]
                tab = [sb(ph, "tab%d" % i, [128, 2, 16, 512], BF16) for i in range(2)]
                t_tab = [Tok(), Tok()]
                pa = [ps(ph, "pa%d" % i, [128, 2, 256]) for i in range(3)]
                t_pa = [PTok() for _ in range(3)]
                po = [ps(ph, "po%d" % i, [128, 512]) for i in range(3)]
                t_po = [PTok() for _ in range(3)]
                so = [sb(ph, "so%d" % i, [128, 512], BF16) for i in range(3)]
                t_so = [Tok() for _ in range(3)]
                for b in range(NB):
                    P.dma("sp", fin[b][:], find[b].rearrange("(j p) t -> p j t", p=128), writes=[t_fin[b]])
                n = 0
                for b in range(NB):
                    for pb in range(18):
                        if last and pb < 2:
                            continue
                        pt, tk = pa[n % 3], t_pa[n % 3]
                        for j in range(2):
                            P.op("pe", lambda e, pt=pt, j=j, b=b, pb=pb: e.matmul(
                                pt[:, j, :], fin[b][:, j, pb * 128:(pb + 1) * 128], dftc[:], start=True, stop=True),
                                reads=[t_fin[b], t_dc], writes=[tk])
                        if n % 2 == 0:
                            P.op("act", lambda e, pt=pt, b=b, pb=pb: e.activation(out=A[b][:, pb, :, :], in_=pt[:], func=AF.Copy),
                                 reads=[tk], writes=[t_A[b]])
                        else:
                            P.op("dve", lambda e, pt=pt, b=b, pb=pb: e.tensor_copy(out=A[b][:, pb, :, :], in_=pt[:]),
                                 reads=[tk], writes=[t_A[b]])
                        n += 1
                n = 0
                for kc in range(4):
                    tb_, tt = tab[kc % 2], t_tab[kc % 2]
                    P.dma("sp", tb_[:], dftnd[kc], writes=[tt])
                    for b in range(NB):
                        for j in range(2):
                            pt, tk = po[n % 3], t_po[n % 3]
                            st, ts = so[n % 3], t_so[n % 3]
                            for pb in range(16):
                                for cs in range(2):
                                    P.op("pe", lambda e, pt=pt, b=b, j=j, pb=pb, cs=cs, tb_=tb_: e.matmul(
                                        pt[:], A[b][:, 2 + pb, j, cs * 128:(cs + 1) * 128], tb_[:, cs, pb, :],
                                        start=(pb == 0 and cs == 0), stop=(pb == 15 and cs == 1)),
                                        reads=[t_A[b], tt], writes=[tk])
                            if n % 2 == 0:
                                P.op("act", lambda e, pt=pt, st=st: e.activation(out=st[:], in_=pt[:], func=AF.Copy),
                                     reads=[tk], writes=[ts])
                            else:
                                P.op("dve", lambda e, pt=pt, st=st: e.tensor_copy(out=st[:], in_=pt[:]), reads=[tk], writes=[ts])
                            P.dma("pool", mixd[b, j * 128:(j + 1) * 128, CTX + kc * 512:CTX + (kc + 1) * 512], st[:], reads=[ts])
                            n += 1
                if not last:
                    for b in range(NB):
                        for j in range(2):
                            pt, tk = po[n % 3], t_po[n % 3]
                            st, ts = so[n % 3], t_so[n % 3]
                            for pb in range(2):
                                for cs in range(2):
                                    P.op("pe", lambda e, pt=pt, b=b, j=j, pb=pb, cs=cs: e.matmul(
                                        pt[:, 0:CTX], A[b][:, pb, j, cs * 128:(cs + 1) * 128], dftx[:, cs, pb, :],
                                        start=(pb == 0 and cs == 0), stop=(pb == 1 and cs == 1)),
                                        reads=[t_A[b], t_dc], writes=[tk])
                            P.op("dve", lambda e, pt=pt, st=st: e.tensor_copy(out=st[:, 0:CTX], in_=pt[:, 0:CTX]),
                                 reads=[tk], writes=[ts])
                            P.dma("pool", mixd[b, j * 128:(j + 1) * 128, 0:CTX], st[:, 0:CTX], reads=[ts])
                            n += 1
                P.barrier()

        def normalize_out(pO, t_pO, orow, srow, ncols, extra, dst, den, t_den, so_, t_so_, oc, t_oc):
            P.op("dve", lambda e: e.tensor_copy(out=den[orow, 0:ncols], in_=pO[srow, 0:ncols]), reads=[t_pO], writes=[t_den])
            P.op("dve", lambda e: e.tensor_copy(out=oc[orow, 0:ncols], in_=pO[orow, 0:ncols]), reads=[t_pO], writes=[t_oc])

            def st1():
                if extra is not None:
                    P.op("dve", lambda e: e.tensor_scalar(out=den[orow, 0:ncols], in0=den[orow, 0:ncols], scalar1=extra[orow, :],
                                                          scalar2=None, op0=ALU.add), reads=[t_den, t_esink], writes=[t_den])

            def st2():
                P.op("act", lambda e: e.activation(out=den[orow, 0:ncols], in_=den[orow, 0:ncols], func=AF.Ln),
                     reads=[t_den], writes=[t_den])

            def st3():
                P.op("act", lambda e: e.activation(out=den[orow, 0:ncols], in_=den[orow, 0:ncols], func=AF.Exp, scale=-1.0),
                     reads=[t_den], writes=[t_den])

            def st4():
                P.op("dve", lambda e: e.tensor_tensor(out=so_[orow, 0:ncols], in0=oc[orow, 0:ncols], in1=den[orow, 0:ncols], op=ALU.mult),
                     reads=[t_oc, t_den], writes=[t_so_])

            def st5():
                P.dma("pool", dst, so_[orow, 0:ncols], reads=[t_so_])
            return [st1, st2, st3, st4, st5] if extra is not None else [st2, st3, st4, st5]

        LO = slice(0, 64)
        HI = slice(64, 128)

        def run_pipeline(items, LA=2):
            n = len(items)
            deferred = []
            for j in range(n + LA):
                if j < n:
                    items[j][0]()
                if j >= LA:
                    qk, ex, pv, post, pre = items[j - LA]
                    if pre is not None:
                        pre()
                    ex()
                    pv()
                    for dq in deferred:
                        if dq:
                            dq.pop(0)()
                    deferred[:] = [dq for dq in deferred if dq]
                    if post is not None:
                        r = post()
                        if r:
                            deferred.append(list(r))
            for dq in deferred:
                for f in dq:
                    f()

        def swa_phase(l, last):
            with contextlib.ExitStack() as ph:
                maskb = sb(ph, "maskb", [128, 6, 512], BF16)
                t_mk = Tok()
                P.dma("sp", maskb[:], maskbd, writes=[t_mk])
                pS = [ps(ph, "pS%d" % i, [128, 512]) for i in range(5)]
                t_pS = [PTok() for _ in range(5)]
                pO = [ps(ph, "pO%d" % i, [128, 512]) for i in range(2)]
                t_pO = [PTok() for _ in range(2)]
                PT = [sb(ph, "PT%d" % i, [128, 512], BF16) for i in range(5)]
                t_PT = [Tok() for _ in range(5)]
                den = [sb(ph, "den%d" % i, [128, 512], F32) for i in range(3)]
                t_den = [Tok() for _ in range(3)]
                so = [sb(ph, "so%d" % i, [128, 512], BF16) for i in range(3)]
                t_so = [Tok() for _ in range(3)]
                oc = [sb(ph, "oc%d" % i, [128, 512], F32) for i in range(3)]
                t_oc = [Tok() for _ in range(3)]
                sq = sb(ph, "sq", [128, 2, T], BF16)
                sk = sb(ph, "sk", [128, T], BF16)
                Vx = sb(ph, "Vx", [128, 18, 2, 128], BF16)
                t_q, t_k, t_v = Tok(), Tok(), Tok()
                P.op("dve", lambda e: e.memset(Vx[:, :, 0, 64:128], 1.0), writes=[t_v])
                P.op("dve", lambda e: e.memset(Vx[:, :, 1, 0:64], 1.0), writes=[t_v])
                ns = 0
                no = 0
                for b in range(NB):
                    for i in range(2):
                        P.dma("sp", sq[:, i, :], sqd[b, i], writes=[t_q])
                    P.dma("sp", sk[:], skd[b], writes=[t_k])
                    svv = svd[b].rearrange("(blk p) f -> p blk f", p=128)
                    P.dma("sp", Vx[:, :, 0, 0:64], svv[:, :, 0:64], writes=[t_v])
                    P.dma("sp", Vx[:, :, 1, 64:128], svv[:, :, 64:128], writes=[t_v])
                    items = []
                    for g in range(2):
                        half = LO if g == 0 else HI
                        oth = HI if g == 0 else LO
                        for i in range(2):
                            head = 2 * g + i
                            esk = esink[:, l * 4 + head: l * 4 + head + 1]
                            groups = [] if last else [("ctx", 0)]
                            groups += [("lat", qg) for qg in range(4)]
                            for kind, qg in groups:
                                if kind == "ctx":
                                    ncols = CTX
                                    qsl = slice(0, CTX)
                                    keys = [(0, None), (1, None)]
                                else:
                                    ncols = 512
                                    qsl = slice(CTX + qg * 512, CTX + (qg + 1) * 512)
                                    keys = [(0, None), (1, None)]
                                    for r in range(-1, 5):
                                        kb = qg * 4 + r
                                        if 0 <= kb < 16:
                                            keys.append((2 + kb, r + 1))
                                k2 = no % 2
                                k3 = no % 3
                                no += 1
                                o_, to_ = pO[k2], t_pO[k2]
                                dst = mixd[b, 768 + i * 128 + g * 64: 768 + i * 128 + (g + 1) * 64, qsl]
                                nk = len(keys)
                                for ki, (sblk, mi) in enumerate(keys):
                                    s_, ts_ = pS[ns % 5], t_pS[ns % 5]
                                    p_, tp_ = PT[ns % 5], t_PT[ns % 5]
                                    ns += 1

                                    def qk(s_=s_, ts_=ts_, sblk=sblk, qsl=qsl, i=i, half=half, mi=mi, ncols=ncols):
                                        P.op("pe", lambda e: e.matmul(
                                            s_[:, 0:ncols], sk[half, sblk * 128:(sblk + 1) * 128], sq[half, i, qsl],
                                            start=True, stop=(mi is None)), reads=[t_k, t_q], writes=[ts_])
                                        if mi is not None:
                                            P.op("pe", lambda e: e.matmul(
                                                s_[:], ident[:], maskb[:, mi, :], start=False, stop=True),
                                                reads=[t_const, t_mk], writes=[ts_])

                                    def ex(s_=s_, ts_=ts_, p_=p_, tp_=tp_, ncols=ncols):
                                        P.op("act", lambda e: e.activation(
                                            out=p_[:, 0:ncols], in_=s_[:, 0:ncols], func=AF.Exp, scale=0.125),
                                            reads=[ts_], writes=[tp_])

                                    def pv(o_=o_, to_=to_, p_=p_, tp_=tp_, sblk=sblk, g=g, ki=ki, nk=nk, ncols=ncols):
                                        P.op("pe", lambda e: e.matmul(
                                            o_[:, 0:ncols], Vx[:, sblk, g, :], p_[:, 0:ncols],
                                            start=(ki == 0), stop=(ki == nk - 1)), reads=[t_v, tp_], writes=[to_])

                                    post = None
                                    if ki == nk - 1:
                                        def post(o_=o_, to_=to_, half=half, oth=oth, ncols=ncols, esk=esk, dst=dst, k3=k3):
                                            return normalize_out(o_, to_, half, oth, ncols, esk, dst, den[k3], t_den[k3],
                                                                 so[k3], t_so[k3], oc[k3], t_oc[k3])
                                    items.append([qk, ex, pv, post, None])
                    run_pipeline(items, LA=4)
                P.barrier()

        def mla_phase(l, last):
            with contextlib.ExitStack() as ph:
                Wuq = sb(ph, "Wuq", [128, 2, 8, 2, 96], BF16)
                Wukv = sb(ph, "Wukv", [128, 8, 128], BF16)
                t_w = Tok()
                P.dma("pool", Wuq[:], wuqd[l], writes=[t_w])
                P.dma("pool", Wukv[:], wukvd[l], writes=[t_w])
                ropem = sb(ph, "ropem", [128, 2, SEQ], F32)
                t_rope = Tok()
                P.dma("sp", ropem[:], ropemd, writes=[t_rope])
                cqn = sb(ph, "cqn", [128, 2, T], BF16)
                ckvn = sb(ph, "ckvn", [128, T], BF16)
                Kpre = sb(ph, "Kpre", [96, T], F32)
                SQK = sb(ph, "SQK", [96, T], BF16)
                t_cqn, t_ckvn, t_kpre_hi, t_sqk_hi = Tok(), Tok(), Tok(), Tok()
                t_kpre_lo, t_sqk_lo = Tok(), Tok()
                Kh = [sb(ph, "Kh%d" % i, [96, T], BF16) for i in range(2)]
                Qh = [sb(ph, "Qh%d" % i, [96, T], BF16) for i in range(2)]
                Vx = [sb(ph, "Vx%d" % i, [128, 18, 128], BF16) for i in range(2)]
                t_Kh = [Tok(), Tok()]
                t_Qh = [Tok(), Tok()]
                t_Vx = [Tok(), Tok()]
                P.op("dve", lambda e: e.memset(Vx[0][:, :, 64:128], 1.0), writes=[t_Vx[0]])
                P.op("dve", lambda e: e.memset(Vx[1][:, :, 0:64], 1.0), writes=[t_Vx[1]])
                pS = [ps(ph, "pS%d" % i, [128, 512]) for i in range(4)]
                t_pS = [PTok() for _ in range(4)]
                pO = [ps(ph, "pO%d" % i, [128, 512]) for i in range(1)]
                t_pO = [PTok() for _ in range(1)]
                pM = [ps(ph, "pM%d" % i, [128, 512]) for i in range(3)]
                t_pM = [PTok() for _ in range(3)]
                PT = [sb(ph, "PT%d" % i, [128, 512], BF16) for i in range(5)]
                t_PT = [Tok() for _ in range(5)]
                den = [sb(ph, "den%d" % i, [128, 512], F32) for i in range(3)]
                t_den = [Tok() for _ in range(3)]
                so = [sb(ph, "so%d" % i, [128, 512], BF16) for i in range(3)]
                t_so = [Tok() for _ in range(3)]
                oc = [sb(ph, "oc%d" % i, [128, 512], F32) for i in range(3)]
                t_oc = [Tok() for _ in range(3)]
                sqt = [sb(ph, "sqt%d" % i, [96, 512], BF16) for i in range(2)]
                t_sqt = [Tok() for _ in range(2)]
                rtt = [sb(ph, "rtt%d" % i, [96, 512], F32) for i in range(2)]
                t_rtt = [Tok() for _ in range(2)]
                rsd = [sb(ph, "rsd%d" % i, [96, 512], F32) for i in range(2)]
                t_rsd = [Tok() for _ in range(2)]
                t1s = [sb(ph, "t1s%d" % i, [96, 512], F32) for i in range(2)]
                t2s = [sb(ph, "t2s%d" % i, [96, 512], F32) for i in range(2)]
                t_t1 = [Tok() for _ in range(2)]
                t_t2 = [Tok() for _ in range(2)]
                cn = {"m": 0, "r": 0, "s": 0, "o": 0, "k": 0}
                tchunks = [(0, CTX)] + [(CTX + i * 512, 512) for i in range(4)]
                scale = 96.0 ** -0.5

                def next_pm():
                    i = cn["m"] % 3
                    cn["m"] += 1
                    return pM[i], t_pM[i]

                kraw = [sb(ph, "kraw%d" % i, [96, 512], F32) for i in range(2)]
                t_kraw = [Tok() for _ in range(2)]
                lnv = [sb(ph, "lnv%d" % i, [96, 512], F32) for i in range(2)]
                t_lnv = [Tok() for _ in range(2)]

                def prep_steps(b, h):
                    hb = h % 2
                    K_, tK = Kh[hb], t_Kh[hb]
                    Q_, tQ = Qh[hb], t_Qh[hb]
                    V_, tV = Vx[hb], t_Vx[hb]
                    gk = vcol(l, V_GMK)
                    gq = vcol(l, V_GMQ)
                    gqs = vcol(l, V_GMQ + 1)
                    steps = []
                    kchains, qchains, vsteps = [], [], []

                    def rstd_steps(rhs_fn, t_rhs, n, box):
                        def s3():
                            pm_, tm_ = next_pm()
                            box["pm"] = (pm_, tm_)
                            P.op("pe", lambda e: e.matmul(pm_[0:96, 0:n], ones[0:96, 0:96], rhs_fn(), start=True, stop=True),
                                 reads=list(t_rhs) + [t_const], writes=[tm_])

                        def s4():
                            pm_, tm_ = box["pm"]
                            j = cn["r"] % 2
                            cn["r"] += 1
                            box["j"] = j
                            P.op("act", lambda e: e.activation(out=lnv[j][:, 0:n], in_=pm_[0:96, 0:n], func=AF.Ln,
                                                               bias=EPS, scale=1.0 / 96.0), reads=[tm_], writes=[t_lnv[j]])

                        def s5():
                            j = box["j"]
                            P.op("act", lambda e: e.activation(out=rsd[j][:, 0:n], in_=lnv[j][:, 0:n], func=AF.Exp, scale=-0.5),
                                 reads=[t_lnv[j]], writes=[t_rsd[j]])
                            box["rs"] = (rsd[j], t_rsd[j])
                        return [s3, s4, s5]

                    for (t0, n) in tchunks:
                        cs = slice(t0, t0 + n)
                        box = {}

                        def k1(box=box, cs=cs, n=n):
                            pm_, tm_ = next_pm()
                            box["a"] = (pm_, tm_)
                            P.op("pe", lambda e: e.matmul(pm_[0:64, 0:n], Wukv[:, h, 0:64], ckvn[:, cs], start=True, stop=True),
                                 reads=[t_w, t_ckvn], writes=[tm_])

                        def k2(box=box, cs=cs, n=n):
                            pm_, tm_ = box["a"]
                            jr = cn["k"] % 2
                            cn["k"] += 1
                            kr_, tkr_ = kraw[jr], t_kraw[jr]
                            P.op("dve", lambda e: e.tensor_copy(out=kr_[0:64, 0:n], in_=pm_[0:64, 0:n]), reads=[tm_], writes=[tkr_])
                            P.op("pool", lambda e: e.tensor_tensor(out=SQK[0:64, cs], in0=kr_[0:64, 0:n], in1=kr_[0:64, 0:n], op=ALU.mult),
                                 reads=[tkr_], writes=[t_sqk_lo])
                            P.op("dve", lambda e: e.tensor_scalar(out=Kpre[0:64, cs], in0=kr_[0:64, 0:n], scalar1=gk[0:64, :],
                                                                  scalar2=None, op0=ALU.mult), reads=[tkr_, t_vecs], writes=[t_kpre_lo])

                        def k6(box=box, cs=cs, n=n):
                            rs, trs = box["rs"]
                            P.op("dve", lambda e: e.tensor_tensor(out=K_[:, cs], in0=Kpre[:, cs], in1=rs[:, 0:n], op=ALU.mult),
                                 reads=[t_kpre_lo, t_kpre_hi, trs], writes=[tK])
                        kchains.append([k1, k2] + rstd_steps(lambda cs=cs: SQK[0:96, cs], [t_sqk_lo, t_sqk_hi], n, box) + [k6])
                    vcols = slice(0, 64) if hb == 0 else slice(64, 128)
                    for (s0, s1) in ((0, 8), (8, 16), (16, 18)):
                        box = {}

                        def v1(box=box, s0=s0, s1=s1):
                            pm_, tm_ = next_pm()
                            box["a"] = (pm_, tm_)
                            for sblk in range(s0, s1):
                                P.op("pe", lambda e, sblk=sblk: e.matmul(
                                    pm_[:, (sblk - s0) * 64:(sblk - s0 + 1) * 64], ckvn[:, sblk * 128:(sblk + 1) * 128],
                                    Wukv[:, h, 64:128], start=True, stop=True), reads=[t_w, t_ckvn], writes=[tm_])

                        def v2(box=box, s0=s0, s1=s1):
                            pm_, tm_ = box["a"]
                            ns_ = s1 - s0
                            P.op("dve", lambda e: e.tensor_copy(
                                out=V_[:, s0:s1, vcols], in_=pm_[:, 0:ns_ * 64].rearrange("p (s f) -> p s f", f=64)),
                                reads=[tm_], writes=[tV])
                        vsteps += [v1, v2]
                    for (t0, n) in tchunks:
                        if last and t0 == 0:
                            continue
                        cs = slice(t0, t0 + n)
                        lat = t0 >= CTX
                        psl = slice(t0 - CTX, t0 - CTX + n)
                        box = {}

                        def q1(box=box, cs=cs, n=n, lat=lat):
                            pq_, tq_ = next_pm()
                            box["a"] = (pq_, tq_)
                            for j in range(2):
                                P.op("pe", lambda e, j=j: e.matmul(
                                    pq_[0:96, 0:n], Wuq[:, j, h, 0, :], cqn[:, j, cs], start=(j == 0), stop=(j == 1)),
                                    reads=[t_w, t_cqn], writes=[tq_])
                            if lat:
                                px_, tx_ = next_pm()
                                box["c"] = (px_, tx_)
                                for j in range(2):
                                    P.op("pe", lambda e, j=j: e.matmul(
                                        px_[0:96, 0:n], Wuq[:, j, h, 1, :], cqn[:, j, cs], start=(j == 0), stop=(j == 1)),
                                        reads=[t_w, t_cqn], writes=[tx_])

                        def q2(box=box, n=n, lat=lat, psl=psl):
                            pq_, tq_ = box["a"]
                            jr = cn["k"] % 2
                            cn["k"] += 1
                            box["jr"] = jr
                            qr_, tqr_ = kraw[jr], t_kraw[jr]
                            P.op("dve", lambda e: e.tensor_copy(out=qr_[0:96, 0:n], in_=pq_[0:96, 0:n]), reads=[tq_], writes=[tqr_])
                            P.op("pool", lambda e: e.tensor_tensor(out=sqt[jr][:, 0:n], in0=qr_[0:96, 0:n], in1=qr_[0:96, 0:n], op=ALU.mult),
                                 reads=[tqr_], writes=[t_sqt[jr]])
                            if lat:
                                px_, tx_ = box["c"]
                                P.op("dve", lambda e: e.scalar_tensor_tensor(
                                    out=t2s[jr][64:96, 0:n], in0=px_[64:96, 0:n], scalar=gqs[64:96, :], in1=ropem[64:96, 1, psl],
                                    op0=ALU.mult, op1=ALU.mult), reads=[tx_, t_vecs, t_rope], writes=[t_t2[jr]])

                        def q6(box=box, cs=cs, n=n, lat=lat, psl=psl):
                            rs, trs = box["rs"]
                            jr = box["jr"]
                            qr_, tqr_ = kraw[jr], t_kraw[jr]
                            if lat:
                                P.op("dve", lambda e: e.scalar_tensor_tensor(
                                    out=Q_[0:64, cs], in0=qr_[0:64, 0:n], scalar=gq[0:64, :], in1=rs[0:64, 0:n],
                                    op0=ALU.mult, op1=ALU.mult), reads=[tqr_, trs, t_vecs], writes=[tQ])
                                P.op("dve", lambda e: e.scalar_tensor_tensor(
                                    out=t1s[jr][64:96, 0:n], in0=qr_[64:96, 0:n], scalar=gq[64:96, :], in1=ropem[64:96, 0, psl],
                                    op0=ALU.mult, op1=ALU.mult), reads=[tqr_, t_vecs, t_rope], writes=[t_t1[jr]])
                                P.op("pool", lambda e: e.tensor_tensor(
                                    out=t1s[jr][64:96, 0:n], in0=t1s[jr][64:96, 0:n], in1=t2s[jr][64:96, 0:n], op=ALU.add),
                                    reads=[t_t1[jr], t_t2[jr]], writes=[t_t1[jr]])
                                P.op("dve", lambda e: e.tensor_tensor(
                                    out=Q_[64:96, cs], in0=t1s[jr][64:96, 0:n], in1=rs[64:96, 0:n], op=ALU.mult),
                                    reads=[t_t1[jr], trs], writes=[tQ])
                            else:
                                P.op("dve", lambda e: e.scalar_tensor_tensor(
                                    out=Q_[:, cs], in0=qr_[0:96, 0:n], scalar=gq[0:96, :], in1=rs[:, 0:n],
                                    op0=ALU.mult, op1=ALU.mult), reads=[tqr_, trs, t_vecs], writes=[tQ])
                        qchains.append([q1, q2] + rstd_steps(lambda box=box, n=n: sqt[box["jr"]][:, 0:n], [t_sqt[0], t_sqt[1]], n, box) + [q6])
                    while kchains or qchains:
                        kc = kchains.pop(0) if kchains else []
                        qc = qchains.pop(0) if qchains else []
                        for i_ in range(max(len(kc), len(qc))):
                            if i_ < len(kc):
                                steps.append(kc[i_])
                            if i_ < len(qc):
                                steps.append(qc[i_])
                    steps += vsteps
                    return steps

                def head_items(b, h):
                    hb = h % 2
                    K_, tK = Kh[hb], t_Kh[hb]
                    Q_, tQ = Qh[hb], t_Qh[hb]
                    V_, tV = Vx[hb], t_Vx[hb]
                    orow = LO if hb == 0 else HI
                    srow = HI if hb == 0 else LO
                    groups = [] if last else [(0, CTX, 2)]
                    groups += [(CTX + i * 512, 512, 18) for i in range(4)]
                    items = []
                    for (t0, n, nk) in groups:
                        k2 = cn["o"] % 3
                        cn["o"] += 1
                        o_, to_ = pO[0], t_pO[0]
                        dst = mixd[b, 256 + h * 64: 256 + (h + 1) * 64, t0:t0 + n]
                        for sblk in range(nk):
                            i3 = cn["s3"] % 4
                            cn["s3"] += 1
                            s_, ts_ = pS[i3], t_pS[i3]
                            p_, tp_ = PT[i3], t_PT[i3]

                            def qk(s_=s_, ts_=ts_, sblk=sblk, t0=t0, n=n):
                                P.op("pe", lambda e: e.matmul(
                                    s_[:, 0:n], K_[:, sblk * 128:(sblk + 1) * 128], Q_[:, t0:t0 + n], start=True, stop=True),
                                    reads=[tK, tQ], writes=[ts_])

                            def ex(s_=s_, ts_=ts_, p_=p_, tp_=tp_, n=n):
                                P.op("act", lambda e: e.activation(out=p_[:, 0:n], in_=s_[:, 0:n], func=AF.Exp, scale=scale),
                                     reads=[ts_], writes=[tp_])

                            def pv(o_=o_, to_=to_, p_=p_, tp_=tp_, sblk=sblk, n=n, nk=nk):
                                P.op("pe", lambda e: e.matmul(
                                    o_[:, 0:n], V_[:, sblk, :], p_[:, 0:n], start=(sblk == 0), stop=(sblk == nk - 1)),
                                    reads=[tV, tp_], writes=[to_])

                            post = None
                            if sblk == nk - 1:
                                def post(o_=o_, to_=to_, n=n, dst=dst, k2=k2):
                                    return normalize_out(o_, to_, orow, srow, n, None, dst, den[k2], t_den[k2], so[k2], t_so[k2], oc[k2], t_oc[k2])
                            items.append([qk, ex, pv, post, None])
                    return items

                cn["s3"] = 0
                for b in range(NB):
                    P.dma("sp", cqn[:], cqnd[b].rearrange("(j p) t -> p j t", p=128), writes=[t_cqn])
                    P.dma("sp", ckvn[:], ckvnd[b], writes=[t_ckvn])
                    P.dma("sp", Kpre[64:96, :], krrd[b], writes=[t_kpre_hi])
                    P.dma("sp", SQK[64:96, :], krsqd[b], writes=[t_sqk_hi])
                    for st_ in prep_steps(b, 0):
                        st_()
                    items = []
                    for h in range(8):
                        its = head_items(b, h)
                        if h + 1 < 8:
                            steps = prep_steps(b, h + 1)
                            per = [[] for _ in its]
                            for si, st_ in enumerate(steps):
                                per[min(si, len(its) - 1)].append(st_)
                            for it, sts in zip(its, per):
                                if sts:
                                    it[4] = (lambda sts=sts: [f() for f in sts])
                        items += its
                    run_pipeline(items, LA=3)
                P.barrier()

        def outproj_phase(l, last):
            with contextlib.ExitStack() as ph:
                Wout = sb(ph, "Wout", [128, 8, D], BF16)
                t_w = Tok()
                P.dma("pool", Wout[:], woutd[l], writes=[t_w])
                hc = [sb(ph, "hc%d" % i, [128, 8, CH], F32) for i in range(3)]
                t_hc = [Tok() for _ in range(3)]
                mx = [sb(ph, "mx%d" % i, [128, 8, CH], BF16) for i in range(3)]
                t_mx = [Tok() for _ in range(3)]
                py = [ps(ph, "py%d" % i, [128, 2, CH]) for i in range(8)]
                t_py = [PTok() for _ in range(8)]
                i = 0
                n = 0
                for b in range(NB):
                    for c in range(NCH):
                        if last and c == 0:
                            continue
                        v = vec_of(b, c)
                        tsl = slice(c * CH, (c + 1) * CH)
                        h_, th_ = hc[i % 3], t_hc[i % 3]
                        m_, tm_ = mx[i % 3], t_mx[i % 3]
                        i += 1
                        P.dma("sp", h_[:], hd[b, :, tsl].rearrange("(k p) t -> p k t", p=128), writes=[th_])
                        P.dma("sp", m_[:], mixd[b, :, tsl].rearrange("(k p) t -> p k t", p=128), writes=[tm_])
                        for dk in range(8):
                            yy = py[n % 8][:, 0, :]
                            ty = t_py[n % 8]
                            n += 1
                            for k in range(8):
                                P.op("pe", lambda e, yy=yy, k=k, dk=dk, m_=m_: e.matmul(
                                    yy, Wout[:, k, dk * 128:(dk + 1) * 128], m_[:, k, :], start=(k == 0), stop=(k == 7)),
                                    reads=[t_w, tm_], writes=[ty])
                            P.op("dve", lambda e, yy=yy, dk=dk, h_=h_, v=v: e.scalar_tensor_tensor(
                                out=h_[:, dk, :], in0=yy, scalar=mod[:, l, v, 40 + dk: 41 + dk], in1=h_[:, dk, :],
                                op0=ALU.mult, op1=ALU.add), reads=[ty, th_, t_mod], writes=[th_])
                        P.dma("pool", hd[b, :, tsl].rearrange("(k p) t -> p k t", p=128), h_[:], reads=[th_])
                P.barrier()

        def want(l, k):
            return stop_after is None or (l * 7 + k) <= stop_after

        wstack = contextlib.ExitStack()
        X = Y = None
        if want(0, 0):
            wstack = contextlib.ExitStack()
            X, Y = WSet(wstack, "x0"), WSet(wstack, "y0")
            load_w(X, 0, 0, 0)
            ffn_pass(0, 0, 0, X, True, False, prefetch=lambda: load_w(Y, 0, 0, 1))
            ffn_pass(0, 0, 1, Y, True, False)
            wstack.close()
        for l in range(n_layers):
            last = (l == L - 1)
            if want(l, 1):
                mixin_phase(l)
            if want(l, 2):
                fourier_phase(l, last)
            if want(l, 3):
                swa_phase(l, last)
            if want(l, 4):
                mla_phase(l, last)
            if want(l, 5):
                wstack = contextlib.ExitStack()
                X, Y = WSet(wstack, "x%d" % (l + 1)), WSet(wstack, "y%d" % (l + 1))
                if want(l, 6):
                    load_w(X, l, 1, 0)
                outproj_phase(l, last)
                if want(l, 6):
                    nxt_ffn1 = (l + 1 < n_layers) and want(l + 1, 0)
                    ffn_pass(l, 1, 0, X, False, last, prefetch=lambda: load_w(Y, l, 1, 1))
                    ffn_pass(l, 1, 1, Y, False, last,
                             prefetch=(lambda: load_w(X, l + 1, 0, 0)) if nxt_ffn1 else None)
                    if nxt_ffn1:
                        ffn_pass(l + 1, 0, 0, X, False, False, prefetch=lambda: load_w(Y, l + 1, 0, 1))
                        ffn_pass(l + 1, 0, 1, Y, False, False)
                wstack.close()
        P.barrier()
        nc._n_emitted = P.ninstr
    return nc


_CACHE = {}


def kernel(**inputs):
    n_cores = 8
    shared = _prep_shared(inputs)
    in_maps = []
    for core in range(n_cores):
        m = dict(shared)
        m.update(_prep_core(inputs, core))
        in_maps.append(m)
    if "nc" not in _CACHE:
        _CACHE["nc"] = build_nc()
    nc = _CACHE["nc"]
    res = run_bass_kernel_spmd(nc, in_maps, core_ids=list(range(n_cores)))
    out = np.empty((n_cores * NB, SEQ, D), np.float32)
    for core in range(n_cores):
        o = res.results[core]["out"]
        for i in range(NB):
            out[core * NB + i] = o[i].T
    return out
```
